# Optimizing a Trainium2 kernel written in Bass

```python
import math
import jax, jax.numpy as jnp
from jax import lax
import numpy as np

D_MODEL = 1024
BATCH = 4
SEQ = 4096
DEPTH = 1
DEC_BATCH = 32
DEC_SEQ = 8
PAST_LEN = 16384
PAGE_SIZE = 128

D_PLE = 256
HEAD_DIM = 64
D_ATTN = D_MODEL // 2
N_ATTN_HEADS = D_ATTN // HEAD_DIM
DILATED_BRANCHES = ((128, 1), (512, 4), (2048, 16))
N_STEPS = 128
MAX_WINDOW = 2048
BLK = 128
N_BUCKETS = 32
MAX_DISTANCE = 2048
D_SSM = D_MODEL // 2
SSM_HEAD_DIM = 64
N_SSM_HEADS = D_SSM // SSM_HEAD_DIM
D_STATE = 128
N_SSM_GROUPS = 2
HEADS_PER_GROUP = N_SSM_HEADS // N_SSM_GROUPS
SSM_CONV = 4
CONV_DIM = D_SSM + 2 * N_SSM_GROUPS * D_STATE
SSD_CHUNK = 128
D_MIX = D_ATTN + D_SSM
IN_DIM = 3 * D_ATTN + D_SSM + CONV_DIM + N_SSM_HEADS
D_FF = ((8 * D_MODEL // 3 + 127) // 128) * 128
FFN_CONV = 3
EPS = 1e-6

kernel_name = "hymba_dilated_ssd_convffn_step"

F32 = jnp.float32


def rms_norm(x, g):
    xf = x.astype(F32)
    y = xf * lax.rsqrt(jnp.mean(xf * xf, axis=-1, keepdims=True) + EPS)
    return (y * g.astype(F32)).astype(x.dtype)


def rel_bucket(dist):
    dist = jnp.asarray(dist, jnp.int32)
    max_exact = N_BUCKETS // 2
    d = jnp.maximum(dist, 1).astype(F32)
    large = max_exact + (jnp.log(d / max_exact) / math.log(MAX_DISTANCE / max_exact)
                         * (N_BUCKETS - max_exact)).astype(jnp.int32)
    large = jnp.minimum(large, N_BUCKETS - 1)
    return jnp.where(dist < max_exact, dist, large)


def causal_dwconv(x, prefix, w, b):
    K = w.shape[0]
    L = x.shape[1]
    xp = jnp.concatenate([prefix.astype(x.dtype), x], axis=1)
    y = b + w[0] * xp[:, 0:L]
    for k in range(1, K):
        y = y + w[k] * xp[:, k:k + L]
    return y, xp[:, xp.shape[1] - (K - 1):]


def combine_branches(outs, lses):
    w = jax.nn.softmax(jnp.stack(lses, axis=-1), axis=-1)
    acc = w[..., 0, None] * outs[0].astype(F32)
    for n in range(1, len(outs)):
        acc = acc + w[..., n, None] * outs[n].astype(F32)
    return acc.astype(outs[0].dtype)


def dilated_attn_prompt(q, k, v, rel_bias):
    Bsz, S, H, E = q.shape
    scale = E ** -0.5
    i = jnp.arange(BLK)[:, None]
    c = jnp.arange(2 * BLK)[None, :]
    j = BLK + i - c
    band = (j >= 0) & (j <= N_STEPS)
    outs, lses = [], []
    for window, dil in DILATED_BRANCHES:
        L = S // dil
        nb = -(-L // BLK)
        Lp = nb * BLK

        def to_sub(t):
            t = t.reshape(Bsz, L, dil, H, E).transpose(0, 2, 1, 3, 4)
            t = jnp.pad(t, ((0, 0), (0, 0), (0, Lp - L), (0, 0), (0, 0)))
            return t.reshape(Bsz, dil, nb, BLK, H, E)

        def with_prev(t):
            prev = jnp.pad(t, ((0, 0), (0, 0), (1, 0), (0, 0), (0, 0), (0, 0)))[:, :, :-1]
            return jnp.concatenate([prev, t], axis=3)

        qs = to_sub(q)
        kk = with_prev(to_sub(k))
        vv = with_prev(to_sub(v))
        l_q = jnp.arange(nb)[:, None, None] * BLK + i[None]
        valid = band[None] & (l_q - j[None] >= 0)
        bias = rel_bias[rel_bucket(jnp.clip(j, 0, N_STEPS) * dil)]
        s = jnp.einsum('bdnqhe,bdnkhe->bdnhqk', qs, kk).astype(F32) * scale
        s = s + bias.astype(F32).transpose(2, 0, 1)[None, None, None]
        s = jnp.where(valid[None, None, :, None], s, -jnp.inf)
        m = jnp.max(s, axis=-1, keepdims=True)
        pe = jnp.exp(s - m)
        den = jnp.sum(pe, axis=-1, keepdims=True)
        o = jnp.einsum('bdnhqk,bdnkhe->bdnqhe', (pe / den).astype(v.dtype), vv)
        lse = (m + jnp.log(den))[..., 0]
        o = o.reshape(Bsz, dil, Lp, H, E)[:, :, :L].transpose(0, 2, 1, 3, 4).reshape(Bsz, S, H, E)
        lse = lse.transpose(0, 1, 2, 4, 3).reshape(Bsz, dil, Lp, H)[:, :, :L]
        lse = lse.transpose(0, 2, 1, 3).reshape(Bsz, S, H)
        outs.append(o)
        lses.append(lse)
    return combine_branches(outs, lses)


def dilated_attn_sample(q, k_new, v_new, k_buf, v_buf, rel_bias):
    T = q.shape[1]
    Wb = k_buf.shape[1]
    scale = q.shape[-1] ** -0.5
    kc = jnp.concatenate([k_buf.astype(k_new.dtype), k_new], axis=1)
    vc = jnp.concatenate([v_buf.astype(v_new.dtype), v_new], axis=1)
    steps = jnp.arange(N_STEPS + 1)
    outs, lses = [], []
    for window, dil in DILATED_BRANCHES:
        idx = Wb + jnp.arange(T)[:, None] - steps[None] * dil
        valid = idx >= 0
        idxc = jnp.maximum(idx, 0)
        kg = kc[:, idxc]
        vg = vc[:, idxc]
        bias = rel_bias[rel_bucket(steps * dil)]
        s = jnp.einsum('bthe,btjhe->bthj', q, kg).astype(F32) * scale
        s = s + bias.astype(F32).T[None, None]
        s = jnp.where(valid[None, :, None, :], s, -jnp.inf)
        m = jnp.max(s, axis=-1, keepdims=True)
        pe = jnp.exp(s - m)
        den = jnp.sum(pe, axis=-1, keepdims=True)
        o = jnp.einsum('bthj,btjhe->bthe', (pe / den).astype(vg.dtype), vg)
        outs.append(o)
        lses.append((m + jnp.log(den))[..., 0])
    return combine_branches(outs, lses)


def ssd_scan(x, dt, A, Bm, Cm, state0, chunk):
    b, L = x.shape[:2]
    nc = L // chunk
    G, HG, P, N = N_SSM_GROUPS, HEADS_PER_GROUP, SSM_HEAD_DIM, D_STATE
    xg = x.astype(F32).reshape(b, nc, chunk, G, HG, P)
    dtg = dt.reshape(b, nc, chunk, G, HG)
    Bc = Bm.astype(F32).reshape(b, nc, chunk, G, N)
    Cc = Cm.astype(F32).reshape(b, nc, chunk, G, N)
    acum = jnp.cumsum(dtg * A.reshape(G, HG), axis=2)
    tri = jnp.tril(jnp.ones((chunk, chunk), bool))[:, :, None, None]
    seg = acum[:, :, :, None] - acum[:, :, None]
    decay = jnp.where(tri, jnp.exp(jnp.where(tri, seg, 0.0)), 0.0)
    cb = jnp.einsum('bctgn,bcsgn->bctsg', Cc, Bc)
    y_diag = jnp.einsum('bctsgh,bcsghp->bctghp', cb[..., None] * decay * dtg[:, :, None], xg)
    decay_end = jnp.exp(acum[:, :, -1:] - acum) * dtg
    chunk_states = jnp.einsum('bclgn,bclgh,bclghp->bcghpn', Bc, decay_end, xg)
    chunk_decay = jnp.exp(acum[:, :, -1])
    s0 = state0.astype(F32).reshape(b, G, HG, P, N)

    def step(s, inp):
        dec, st = inp
        return s * dec[..., None, None] + st, s

    final, prev = lax.scan(step, s0, (chunk_decay.transpose(1, 0, 2, 3),
                                      chunk_states.transpose(1, 0, 2, 3, 4, 5)))
    prev = prev.transpose(1, 0, 2, 3, 4, 5)
    y_off = jnp.einsum('bctgn,bctgh,bcghpn->bctghp', Cc, jnp.exp(acum), prev)
    y = (y_diag + y_off).reshape(b, L, N_SSM_HEADS, P)
    return y, final.reshape(b, N_SSM_HEADS, P, N)


def gated_group_rmsnorm(y, z, g):
    yf = y.astype(F32) * jax.nn.silu(z.astype(F32))
    yg = yf.reshape(*yf.shape[:-1], N_SSM_GROUPS, D_SSM // N_SSM_GROUPS)
    yg = yg * lax.rsqrt(jnp.mean(yg * yg, axis=-1, keepdims=True) + EPS)
    return (yg.reshape(yf.shape) * g.astype(F32)).astype(z.dtype)


def layer(h, p_i, attn_fn, conv_prefix, ssm_state0, ffn_prefix, chunk,
          g_mix, w_in, conv_w, conv_b, dt_bias, a_log, d_skip, g_ssm, w_out,
          g_ffn, w_up, ffn_conv_w, ffn_conv_b, w_down, w_ple_proj, g_ple, w_ple_gate):
    b, L, _ = h.shape
    xn = rms_norm(h, g_mix)
    proj = xn @ w_in
    o1, o2, o3 = D_ATTN, 2 * D_ATTN, 3 * D_ATTN
    o4 = o3 + D_SSM
    o5 = o4 + CONV_DIM
    q = proj[..., :o1].reshape(b, L, N_ATTN_HEADS, HEAD_DIM)
    k = proj[..., o1:o2].reshape(b, L, N_ATTN_HEADS, HEAD_DIM)
    v = proj[..., o2:o3].reshape(b, L, N_ATTN_HEADS, HEAD_DIM)
    z = proj[..., o3:o4]
    xbc = proj[..., o4:o5]
    dt_raw = proj[..., o5:]
    attn_o = attn_fn(q, k, v).reshape(b, L, D_ATTN)
    xbc_c, conv_state = causal_dwconv(xbc, conv_prefix, conv_w, conv_b)
    xbc_c = jax.nn.silu(xbc_c)
    GN = N_SSM_GROUPS * D_STATE
    xs = xbc_c[..., :D_SSM].reshape(b, L, N_SSM_HEADS, SSM_HEAD_DIM)
    Bm = xbc_c[..., D_SSM:D_SSM + GN].reshape(b, L, N_SSM_GROUPS, D_STATE)
    Cm = xbc_c[..., D_SSM + GN:].reshape(b, L, N_SSM_GROUPS, D_STATE)
    dt = jax.nn.softplus(dt_raw.astype(F32) + dt_bias.astype(F32))
    A = -jnp.exp(a_log.astype(F32))
    y, ssm_state = ssd_scan(xs, dt, A, Bm, Cm, ssm_state0, chunk)
    y = y + d_skip.astype(F32)[:, None] * xs.astype(F32)
    y = gated_group_rmsnorm(y.reshape(b, L, D_SSM), z, g_ssm)
    h = h + jnp.concatenate([attn_o, y.astype(attn_o.dtype)], axis=-1) @ w_out
    hn = rms_norm(h, g_ffn)
    u = hn @ w_up
    u_c, ffn_state = causal_dwconv(u, ffn_prefix, ffn_conv_w, ffn_conv_b)
    h = h + (jax.nn.silu(u_c[..., :D_FF]) * u_c[..., D_FF:]) @ w_down
    e = rms_norm(p_i @ w_ple_proj, g_ple)
    h = h + jax.nn.sigmoid((h @ w_ple_gate).astype(F32)).astype(h.dtype) * e
    return h, k, v, ssm_state, conv_state, ffn_state


def setup_inputs(seed: int = 0) -> dict:
    key = jax.random.key(seed)
    ks = jax.random.split(key, 32)

    def nrm(k, shape, scale):
        return jax.random.normal(k, shape, F32) * scale

    w_buf = min(MAX_WINDOW, PAST_LEN)
    dt0 = jnp.exp(jax.random.uniform(ks[10], (DEPTH, N_SSM_HEADS), F32,
                                     math.log(1e-3), math.log(1e-1)))
    return {
        "x_prompt": nrm(ks[0], (BATCH, SEQ, D_MODEL), 1.0),
        "x_sample": nrm(ks[1], (DEC_BATCH, DEC_SEQ, D_MODEL), 1.0),
        "p_prompt": nrm(ks[2], (DEPTH, BATCH, SEQ, D_PLE), 1.0),
        "p_sample": nrm(ks[3], (DEPTH, DEC_BATCH, DEC_SEQ, D_PLE), 1.0),
        "cache_k": nrm(ks[4], (DEPTH, DEC_BATCH, w_buf, N_ATTN_HEADS, HEAD_DIM), 1.0),
        "cache_v": nrm(ks[5], (DEPTH, DEC_BATCH, w_buf, N_ATTN_HEADS, HEAD_DIM), 1.0),
        "state_ssm": nrm(ks[6], (DEPTH, DEC_BATCH, N_SSM_HEADS, SSM_HEAD_DIM, D_STATE), 0.1),
        "state_conv": nrm(ks[7], (DEPTH, DEC_BATCH, SSM_CONV - 1, CONV_DIM), 1.0),
        "state_ffn_conv": nrm(ks[8], (DEPTH, DEC_BATCH, FFN_CONV - 1, 2 * D_FF), 1.0),
        "rel_bias": nrm(ks[9], (N_BUCKETS, N_ATTN_HEADS), 0.1),
        "g_mix": 1.0 + nrm(ks[11], (DEPTH, D_MODEL), 0.02),
        "w_in": nrm(ks[12], (DEPTH, D_MODEL, IN_DIM), D_MODEL ** -0.5),
        "conv_w": nrm(ks[13], (DEPTH, SSM_CONV, CONV_DIM), SSM_CONV ** -0.5),
        "conv_b": nrm(ks[14], (DEPTH, CONV_DIM), 0.01),
        "dt_bias": dt0 + jnp.log(-jnp.expm1(-dt0)),
        "a_log": jnp.log(jax.random.uniform(ks[15], (DEPTH, N_SSM_HEADS), F32, 1.0, 16.0)),
        "d_skip": 1.0 + nrm(ks[16], (DEPTH, N_SSM_HEADS), 0.02),
        "g_ssm": 1.0 + nrm(ks[17], (DEPTH, D_SSM), 0.02),
        "w_out": nrm(ks[18], (DEPTH, D_MIX, D_MODEL), D_MIX ** -0.5),
        "g_ffn": 1.0 + nrm(ks[19], (DEPTH, D_MODEL), 0.02),
        "w_up": nrm(ks[20], (DEPTH, D_MODEL, 2 * D_FF), D_MODEL ** -0.5),
        "ffn_conv_w": nrm(ks[21], (DEPTH, FFN_CONV, 2 * D_FF), FFN_CONV ** -0.5),
        "ffn_conv_b": nrm(ks[22], (DEPTH, 2 * D_FF), 0.01),
        "w_down": nrm(ks[23], (DEPTH, D_FF, D_MODEL), D_FF ** -0.5),
        "w_ple_proj": nrm(ks[24], (DEPTH, D_PLE, D_MODEL), D_PLE ** -0.5),
        "g_ple": 1.0 + nrm(ks[25], (DEPTH, D_MODEL), 0.02),
        "w_ple_gate": nrm(ks[26], (DEPTH, D_MODEL, D_MODEL), D_MODEL ** -0.5),
        "g_final": 1.0 + nrm(ks[27], (D_MODEL,), 0.02),
    }


def reference(x_prompt, x_sample, p_prompt, p_sample, cache_k, cache_v, state_ssm,
              state_conv, state_ffn_conv, rel_bias, g_mix, w_in, conv_w, conv_b,
              dt_bias, a_log, d_skip, g_ssm, w_out, g_ffn, w_up, ffn_conv_w,
              ffn_conv_b, w_down, w_ple_proj, g_ple, w_ple_gate, g_final):
    bp, S, _ = x_prompt.shape
    T = x_sample.shape[1]
    n_keep = min(MAX_WINDOW, S)
    hp, hs = x_prompt, x_sample
    kp_l, vp_l, ks_l, vs_l = [], [], [], []
    sp_l, ss_l, cp_l, cs_l, fp_l, fs_l = [], [], [], [], [], []
    for i in range(DEPTH):
        wi = (g_mix[i], w_in[i], conv_w[i], conv_b[i], dt_bias[i], a_log[i], d_skip[i],
              g_ssm[i], w_out[i], g_ffn[i], w_up[i], ffn_conv_w[i], ffn_conv_b[i],
              w_down[i], w_ple_proj[i], g_ple[i], w_ple_gate[i])
        hp, kp, vp, sp, cp, fp = layer(
            hp, p_prompt[i],
            lambda q, k, v: dilated_attn_prompt(q, k, v, rel_bias),
            jnp.zeros((bp, SSM_CONV - 1, CONV_DIM), x_prompt.dtype),
            jnp.zeros((bp, N_SSM_HEADS, SSM_HEAD_DIM, D_STATE), F32),
            jnp.zeros((bp, FFN_CONV - 1, 2 * D_FF), x_prompt.dtype),
            SSD_CHUNK, *wi)
        hs, ksn, vsn, ssn, csn, fsn = layer(
            hs, p_sample[i],
            lambda q, k, v, kb=cache_k[i], vb=cache_v[i]: dilated_attn_sample(q, k, v, kb, vb, rel_bias),
            state_conv[i], state_ssm[i], state_ffn_conv[i],
            T, *wi)
        kp_l.append(kp[:, S - n_keep:])
        vp_l.append(vp[:, S - n_keep:])
        ks_l.append(ksn)
        vs_l.append(vsn)
        sp_l.append(sp)
        ss_l.append(ssn)
        cp_l.append(cp)
        cs_l.append(csn)
        fp_l.append(fp)
        fs_l.append(fsn)
    y_prompt = rms_norm(hp, g_final)
    y_sample = rms_norm(hs, g_final)
    return (y_prompt, y_sample,
            jnp.stack(kp_l), jnp.stack(vp_l), jnp.stack(ks_l), jnp.stack(vs_l),
            jnp.stack(sp_l), jnp.stack(ss_l), jnp.stack(cp_l), jnp.stack(cs_l),
            jnp.stack(fp_l), jnp.stack(fs_l))
```

```python
import numpy as np
from contextlib import ExitStack
import concourse.bass as bass
import concourse.mybir as mybir
from concourse.bass_utils import run_bass_kernel_spmd

F32 = mybir.dt.float32
BF16 = mybir.dt.bfloat16
AF = mybir.ActivationFunctionType
ALU = mybir.AluOpType

NEG = -30000.0
D = 1024
NPRE = 2048
NMAIN = 2048
NHALO = 128
NLOC = NPRE + NMAIN + NHALO
NQ = NMAIN + NHALO
IN_DIM = 3080
DFF = 2816
NG = 22
EPS = 1e-6
BRANCH_D = (1, 4, 16)


class Buf:
    def __init__(self, name="b", excl=False):
        self.name = name
        self.w = None
        self.r = {}
        self.excl = excl


class Sched:
    NDQ = 28
    NSP = 20

    def __init__(self, nc, stack):
        self.nc = nc
        self.h = {'pe': nc.tensor, 'act': nc.scalar, 'dve': nc.vector, 'pool': nc.gpsimd, 'sp': nc.sync}
        self.sem = {k: stack.enter_context(nc.semaphore(k + "_sem")) for k in self.h}
        self.cnt = {k: 0 for k in self.h}
        self.seen = {k: {} for k in self.h}
        for i in range(self.NDQ):
            self.sem[('dq', i)] = stack.enter_context(nc.semaphore(f"dq{i}"))
        self.dcnt = [0] * self.NDQ
        self.rr = 0
        self.rr_pool = 0
        self.nwait = 0
        self.ndma = 0

    def _deps(self, reads, writes):
        deps = {}
        for b in reads:
            if b.w is not None:
                k, v = b.w
                if deps.get(k, 0) < v:
                    deps[k] = v
            if b.excl:
                for k, v in b.r.items():
                    if deps.get(k, 0) < v:
                        deps[k] = v
        for b in writes:
            if b.w is not None:
                k, v = b.w
                if deps.get(k, 0) < v:
                    deps[k] = v
            for k, v in b.r.items():
                if deps.get(k, 0) < v:
                    deps[k] = v
        return deps

    def _wait(self, eng, deps):
        h = self.h[eng]
        seen = self.seen[eng]
        for k, v in deps.items():
            if k == eng and eng in ('pe', 'sp'):
                continue
            if seen.get(k, 0) >= v:
                continue
            h.wait_ge(self.sem[k], v)
            self.nwait += 1
            seen[k] = v

    def _mark(self, ev, reads, writes):
        k, v = ev
        for b in reads:
            if b.r.get(k, 0) < v:
                b.r[k] = v
        for b in writes:
            b.w = ev
            b.r = {}

    def op(self, eng, fn, reads=(), writes=()):
        self._wait(eng, self._deps(reads, writes))
        ins = fn(self.h[eng])
        self.cnt[eng] += 1
        ins.then_inc(self.sem[eng], 1)
        self._mark((eng, self.cnt[eng]), reads, writes)

    def dma(self, out, in_, reads=(), writes=(), eng='sp', **kw):
        if eng == 'pool':
            i = self.NSP + self.rr_pool
            self.rr_pool = (self.rr_pool + 1) % (self.NDQ - self.NSP)
        else:
            i = self.rr
            self.rr = (i + 1) % self.NSP
        deps = self._deps(reads, writes)
        if self.dcnt[i] > 0:
            k = ('dq', i)
            deps[k] = max(deps.get(k, 0), 16 * self.dcnt[i])
        self._wait(eng, deps)
        ins = self.h[eng].dma_start(out=out, in_=in_, **kw)
        self.dcnt[i] += 1
        self.ndma += 1
        ins.then_inc(self.sem[('dq', i)], 16)
        self._mark((('dq', i), 16 * self.dcnt[i]), reads, writes)

    def pe_mode(self, mode):
        if getattr(self, 'cur_mode', None) is not None and self.cur_mode != mode and self.cnt['pe'] > 0:
            self.h['pe'].wait_ge(self.sem['pe'], self.cnt['pe'])
            self.h['pe'].drain()
            self.nwait += 1
            self.ndrain = getattr(self, 'ndrain', 0) + 1
        self.cur_mode = mode

    def barrier(self):
        for eng in self.h:
            deps = {k: self.cnt[k] for k in self.h if k != eng and self.cnt[k] > 0}
            for i in range(self.NDQ):
                if self.dcnt[i] > 0:
                    deps[('dq', i)] = 16 * self.dcnt[i]
            self._wait(eng, deps)

    def finish(self):
        deps = {('dq', i): 16 * self.dcnt[i] for i in range(self.NDQ) if self.dcnt[i] > 0}
        for k in self.h:
            if k != 'sp' and self.cnt[k] > 0:
                deps[k] = self.cnt[k]
        self._wait('sp', deps)


def rel_bucket_np(dist):
    dist = np.asarray(dist, np.int64)
    d = np.maximum(dist, 1).astype(np.float32)
    large = 16 + (np.log(d / np.float32(16)) / np.float32(np.log(2048 / 16)) * np.float32(16)).astype(np.int32)
    large = np.minimum(large, 31)
    return np.where(dist < 16, dist, large).astype(np.int64)


def host_consts():
    ident = np.eye(128, dtype=np.float32)
    triu = np.triu(np.ones((128, 128), np.float32))
    negtri = np.where(triu > 0, 0.0, NEG).astype(np.float32)
    oh = np.zeros((32, 3 * 129), np.float32)
    for bi, d in enumerate(BRANCH_D):
        bk = rel_bucket_np(np.arange(129) * d)
        for j in range(129):
            oh[bk[j], bi * 129 + j] = 1.0
    negdiag = np.full((128, 2), NEG, np.float32)
    negdiag[0, 0] = 0.0
    negdiag[1, 1] = 0.0
    cm = np.full((32, 3, 32), NEG, np.float32)
    for kg in range(32):
        for qg in range(32):
            if kg // 8 != qg // 8:
                continue
            dist = qg % 8 - kg % 8
            if dist < 0:
                continue
            cm[kg, 0, qg] = 0.0
            if dist in (0, 4):
                cm[kg, 1, qg] = 0.0
            if dist == 0:
                cm[kg, 2, qg] = 0.0
    return dict(ident=ident, triu=triu, negtri=negtri, oh=oh, negdiag=negdiag, cm=cm)


def build_nc(stage=99):
    global _LAST_S
    import os
    skip_p = os.environ.get('KSKIPP', '0') == '1'
    skip_a1 = os.environ.get('KNOSAMP', '0') == '1'
    ks1 = float(os.environ.get('KS1', '99'))
    nc = bass.Bass("TRN2", target_bir_lowering=False)

    def din(name, shape):
        return nc.dram_tensor(name, list(shape), F32, kind="ExternalInput").ap()

    def dout(name, shape):
        return nc.dram_tensor(name, list(shape), F32, kind="ExternalOutput").ap()

    xl = din("xl", [NLOC, D])
    pl = din("pl", [NQ, 256])
    pv = din("pv", [128, 1])
    rel_bias = din("rel_bias", [32, 8])
    g_mix = din("g_mix", [D])
    w_in = din("w_in", [D, IN_DIM])
    conv_w = din("conv_w", [4, 1024])
    conv_b = din("conv_b", [1024])
    dt_bias = din("dt_bias", [8])
    a_log = din("a_log", [8])
    d_skip = din("d_skip", [8])
    g_ssm = din("g_ssm", [512])
    w_out = din("w_out", [D, D])
    g_ffn = din("g_ffn", [D])
    w_up = din("w_up", [D, 2 * DFF])
    ffn_conv_w = din("ffn_conv_w", [3, 2 * DFF])
    ffn_conv_b = din("ffn_conv_b", [2 * DFF])
    w_down = din("w_down", [DFF, D])
    w_ple_proj = din("w_ple_proj", [256, D])
    g_ple = din("g_ple", [D])
    w_ple_gate = din("w_ple_gate", [D, D])
    g_final = din("g_final", [D])
    c_ident = din("ident", [128, 128])
    c_triu = din("triu", [128, 128])
    c_negtri = din("negtri", [128, 128])
    c_oh = din("oh", [32, 387])
    c_negdiag = din("negdiag", [128, 2])

    xs_d = din("xs", [32, D])
    psm_d = din("psm", [32, 256])
    ck_d = din("ck", [4, 2048, 512])
    cv_d = din("cv", [4, 2048, 512])
    sssm_d = din("sssm", [4, 512, 128])
    sconv_d = din("sconv", [12, 1024])
    sffn_d = din("sffn", [8, 2 * DFF])
    c_cm = din("cm", [32, 3, 32])
    ys_d = dout("ys", [32, D])
    ks_d = dout("ks", [32, 512])
    vs_d = dout("vs", [32, 512])
    ssm_s_d = dout("ssm_s", [4, 512, 128])
    conv_s_d = dout("conv_s", [12, 1024])
    ffn_s_d = dout("ffn_s", [8, 2 * DFF])

    y_loc = dout("y_loc", [NQ, D])
    k_loc = dout("k_loc", [NMAIN, 512])
    v_loc = dout("v_loc", [NMAIN, 512])
    ssm_loc = dout("ssm_loc", [512, 128])
    conv_loc = dout("conv_loc", [3, 1024])
    ffn_loc = dout("ffn_loc", [2, 2 * DFF])

    qT_d = nc.dram_tensor("qT_d", [4, 128, NQ], BF16, kind="Internal").ap()
    kT_d = nc.dram_tensor("kT_d", [4, 128, NLOC], BF16, kind="Internal").ap()
    vT_d = nc.dram_tensor("vT_d", [4, 128, NLOC], BF16, kind="Internal").ap()
    bm_d = nc.dram_tensor("bm_d", [8, 3, 384], BF16, kind="Internal").ap()
    wup_d = nc.dram_tensor("wup_d", [NG, 128, 8 * 256], BF16, kind="Internal").ap()
    wB_d = nc.dram_tensor("wB_d", [40, 128, D], BF16, kind="Internal").ap()
    dg_d = nc.dram_tensor("dg_d", [NG, 128, 768], BF16, kind="Internal").ap()

    with ExitStack() as top:
        S = Sched(nc, top)
        _LAST_S = S

        def sbt(stack, name, shape, dt):
            return stack.enter_context(nc.sbuf_tensor("s_" + name, list(shape), dt))

        def _ru(x):
            return 32 if x <= 32 else (64 if x <= 64 else 128)

        def _mode(lhsT):
            shp = lhsT.shape
            m = 1
            for d_ in shp[1:]:
                m *= int(d_)
            return (_ru(int(shp[0])), _ru(m), lhsT.dtype == F32)

        def MM(out, lhsT, rhs, start, stop, r, w):
            md = _mode(lhsT)
            S.pe_mode(md if md[:2] != (128, 128) else (128, 128))
            S.op('pe', lambda e: e.matmul(out, lhsT=lhsT, rhs=rhs, start=start, stop=stop), r, w)

        def TR(out, in_, ident, r, w):
            md = _mode(in_)
            S.pe_mode((md + ('T',)) if md[:2] != (128, 128) else (128, 128))
            S.op('pe', lambda e: e.transpose(out=out, in_=in_, identity=ident), r, w)

        def ACT(out, in_, func, r, w, **kw):
            S.op('act', lambda e: e.activation(out=out, in_=in_, func=func, **kw), r, w)

        def CP(eng, out, in_, r, w):
            if eng == 'act':
                S.op('act', lambda e: e.activation(out=out, in_=in_, func=AF.Copy), r, w)
            else:
                S.op(eng, lambda e: e.tensor_copy(out=out, in_=in_), r, w)

        def TT(eng, out, in0, in1, op, r, w):
            S.op(eng, lambda e: e.tensor_tensor(out=out, in0=in0, in1=in1, op=op), r, w)

        def TS(eng, out, in0, s1, s2, op0, op1, r, w):
            if op1 is None:
                S.op(eng, lambda e: e.tensor_scalar(out=out, in0=in0, scalar1=s1, scalar2=None, op0=op0), r, w)
            else:
                S.op(eng, lambda e: e.tensor_scalar(out=out, in0=in0, scalar1=s1, scalar2=s2, op0=op0, op1=op1), r, w)

        def STT(out, in0, scalar, in1, op0, op1, r, w):
            S.op('dve', lambda e: e.scalar_tensor_tensor(out=out, in0=in0, scalar=scalar, in1=in1, op0=op0, op1=op1), r, w)

        def MEMSET(eng, ap, val, w):
            S.op(eng, lambda e: e.memset(ap, val), (), w)

        psT = top.enter_context(nc.psum_tensor("psT", [128, 1024], BF16))
        BpsT = Buf("psT", excl=True)
        ps = [None] + [top.enter_context(nc.psum_tensor(f"ps{i}", [128, 512], F32)) for i in range(1, 8)]
        Bps = [None] + [Buf(f"ps{i}", excl=True) for i in range(1, 8)]

        identf = sbt(top, "identf", [128, 128], F32)
        identb = sbt(top, "identb", [128, 128], BF16)
        onec = sbt(top, "onec", [128, 1], F32)
        epsc = sbt(top, "epsc", [128, 1], F32)
        mhalf = sbt(top, "mhalf", [128, 1], F32)
        pvt = sbt(top, "pvt", [128, 1], F32)
        gmixT = sbt(top, "gmixT", [128, 8], F32)
        gffnT = sbt(top, "gffnT", [128, 8], F32)
        Bc = Buf("consts")

        top.enter_context(nc.Block())

        S.dma(identf[:], c_ident[:, :], writes=[Bc])
        S.dma(pvt[:], pv[:, :], writes=[Bc])
        S.dma(gmixT[:], g_mix.rearrange("(c p) -> p c", p=128), writes=[Bc], allow_slow_non_contiguous=True)
        S.dma(gffnT[:], g_ffn.rearrange("(c p) -> p c", p=128), writes=[Bc], allow_slow_non_contiguous=True)
        MEMSET('dve', onec[:], 1.0, [Bc])
        MEMSET('dve', epsc[:], EPS, [Bc])
        MEMSET('dve', mhalf[:], -0.5, [Bc])
        CP('dve', identb[:], identf[:], [Bc], [Bc])

        def rms_rstd(src_ap, L, junk_ap, ss_ap, rstd_ap, r, w, inv_n):
            ACT(junk_ap, src_ap, AF.Square, r, w, accum_out=ss_ap)
            TS('dve', ss_ap, ss_ap, inv_n, EPS, ALU.mult, ALU.add, w, w)
            TT('pool', rstd_ap, ss_ap, mhalf[:L, :], ALU.pow, w + [Bc], w)

        mixT = sbt(top, "mixT", [128, 8, NQ], BF16)
        BmixA = [Buf(f"mixA{p}") for p in range(4)]
        BmixS = Buf("mixS")
        smixT = sbt(top, "smixT", [128, 8, 32], BF16)
        BsmixA = [Buf() for _ in range(4)]
        BsmixS = Buf()
        sQT = sbt(top, "sQT", [128, 4, 2, 32], BF16)
        sKT = sbt(top, "sKT", [128, 4, 32], BF16)
        sVn = sbt(top, "sVn", [32, 8, 66], BF16)
        Bsq = Buf()
        MEMSET('pool', sQT[:].rearrange("p a h q -> p (a h q)"), 0.0, [Bsq])
        MEMSET('pool', sVn[:].rearrange("p h e -> p (h e)"), 1.0, [Bsq])

        sA = ExitStack()
        Bm = sbt(sA, "Bm", [128, 8, 3, 256], BF16)
        Bdg = sbt(sA, "Bdg", [128, 8, 2], BF16)
        negd = sbt(sA, "negd", [128, 2], F32)
        sA0 = ExitStack()
        rb = sbt(sA0, "rb", [32, 8], F32)
        oht = sbt(sA0, "oht", [32, 387], F32)
        gfull = sbt(sA0, "gfull", [8, 3, 384], F32)
        Bt = Buf()
        Bbmd = Buf()
        BBm = Buf()
        S.dma(rb[:], rel_bias[:, :], writes=[Bt])
        S.dma(oht[:], c_oh[:, :], writes=[Bt])
        S.dma(negd[:], c_negdiag[:, :], writes=[Bc])
        MEMSET('dve', gfull[:], NEG, [Bt])
        MM(ps[1][0:8, 0:387], rb[:, :], oht[:, :], True, True, [Bt], [Bps[1]])
        CP('dve', gfull[:, :, 127:256], ps[1][0:8, 0:387].rearrange("p (b j) -> p b j", b=3), [Bps[1], Bt], [Bt])
        gfull_b = sbt(sA0, "gfull_b", [8, 3, 384], BF16)
        CP('dve', gfull_b[:], gfull[:], [Bt], [Bt])
        S.dma(bm_d[:, :, :], gfull_b[:], reads=[Bt], writes=[Bbmd])

        S.barrier()
        sA0.close()

        def bm_dma(c):
            S.dma(Bm[c:c + 1, :, :, :], bm_d[:, :, 127 - c:127 - c + 256].unsqueeze(0), reads=[Bbmd], writes=[BBm])

        with ExitStack() as sa:
            w_in_sb = sbt(sa, "w_in_sb", [128, 8, IN_DIM], BF16)
            Bwin = Buf("w_in")
            for c in range(8):
                S.dma(w_in_sb[:, c, :], w_in[c * 128:(c + 1) * 128, :], writes=[Bwin], eng='pool')
            triu = sbt(sa, "triu", [128, 128], F32)
            ones = sbt(sa, "ones", [128, 128], F32)
            negtri_f = sbt(sa, "negtri_f", [128, 128], F32)
            negtri8 = sbt(sa, "negtri8", [128, 8, 128], BF16)
            cwT = sbt(sa, "cwT", [128, 8, 4], F32)
            cbT = sbt(sa, "cbT", [128, 8], F32)
            dtb = sbt(sa, "dtb", [128, 8], F32)
            A_b = sbt(sa, "A_b", [128, 8], F32)
            dskip_b = sbt(sa, "dskip_b", [128, 8], F32)
            gssm_b = sbt(sa, "gssm_b", [128, 512], F32)
            S.dma(triu[:], c_triu[:, :], writes=[Bc])
            S.dma(negtri_f[:], c_negtri[:, :], writes=[Bc])
            for k in range(4):
                S.dma(cwT[:, :, k], conv_w[k, :].rearrange("(c p) -> p c", p=128), writes=[Bc], allow_slow_non_contiguous=True)
            S.dma(cbT[:], conv_b.rearrange("(c p) -> p c", p=128), writes=[Bc], allow_slow_non_contiguous=True)
            S.dma(dtb[:], dt_bias.partition_broadcast(128), writes=[Bc])
            S.dma(A_b[:], a_log.partition_broadcast(128), writes=[Bc])
            S.dma(dskip_b[:], d_skip.partition_broadcast(128), writes=[Bc])
            S.dma(gssm_b[:], g_ssm.partition_broadcast(128), writes=[Bc])
            MEMSET('pool', ones[:], 1.0, [Bc])
            CP('dve', negtri8[:], negtri_f[:].unsqueeze(1).to_broadcast([128, 8, 128]), [Bc], [Bc])
            TS('dve', cwT[:], cwT[:], 0.5, None, ALU.mult, None, [Bc], [Bc])
            TS('dve', cbT[:], cbT[:], 0.5, None, ALU.mult, None, [Bc], [Bc])
            ACT(A_b[:], A_b[:], AF.Exp, [Bc], [Bc])
            TS('dve', A_b[:], A_b[:], -1.0, None, ALU.mult, None, [Bc], [Bc])

            xt = [sbt(sa, f"xt{i}", [128, D], F32) for i in range(2)]
            Bxt = [Buf() for _ in range(2)]
            xn = [sbt(sa, f"xn{i}", [128, D], BF16) for i in range(2)]
            Bxn = [Buf() for _ in range(2)]
            rs = [sbt(sa, f"rs{i}", [128, 2], F32) for i in range(2)]
            Brs = [Buf() for _ in range(2)]
            xnT = sbt(sa, "xnT", [128, 8, 512], BF16)
            BxnT = [Buf() for _ in range(4)]
            stg = [sbt(sa, f"stg{i}", [128, 512], BF16) for i in range(3)]
            Bstg = [Buf() for _ in range(3)]
            cbuf = sbt(sa, "cbuf", [128, 8, 515], F32)
            Bcb = [Buf() for _ in range(8)]
            acc = [sbt(sa, f"acc{i}", [128, 512], F32) for i in range(2)]
            Bacc = [Buf() for _ in range(2)]
            th0 = sbt(sa, "th0", [128, 512], F32)
            th = [th0, th0]
            Bth0 = Buf()
            Bth = [Bth0, Bth0]
            xbcT = sbt(sa, "xbcT", [128, 8, 512], BF16)
            Bxbc = [Buf() for _ in range(8)]
            kvo = [sbt(sa, f"kvo{i}", [128, 512], F32) for i in range(2)]
            Bkvo = [Buf() for _ in range(2)]
            dtall = sbt(sa, "dtall", [128, 4, 8], F32)
            Bdt = Buf()
            xsB2 = [sbt(sa, f"xsB{i}", [128, 768], BF16) for i in range(2)]
            BxsB2 = [Buf() for _ in range(2)]
            sm = sbt(sa, "sm", [128, 4, 4, 8], F32)
            Bsm = Buf()
            avall = sbt(sa, "avall", [128, 4, 8], F32)
            Bav = Buf()
            Rm2 = [sbt(sa, f"Rm{i}", [128, 8, 128], F32) for i in range(2)]
            BRm2 = [Buf() for _ in range(2)]
            Lm2 = [sbt(sa, f"Lm{i}", [128, 8, 128], F32) for i in range(2)]
            BLm2 = [Buf() for _ in range(2)]
            Mb2 = [sbt(sa, f"Mb{i}", [128, 8, 128], BF16) for i in range(2)]
            BMb2 = [Buf() for _ in range(2)]
            xdt2 = [sbt(sa, f"xdt{i}", [128, 8, 64], BF16) for i in range(2)]
            Bxdt2 = [Buf() for _ in range(2)]
            xde2 = [sbt(sa, f"xde{i}", [128, 8, 64], BF16) for i in range(2)]
            Bxde2 = [Buf() for _ in range(2)]
            xsk2 = [sbt(sa, f"xsk{i}", [128, 8, 64], BF16) for i in range(2)]
            Bxsk2 = [Buf() for _ in range(2)]
            stf = sbt(sa, "stf", [128, 8, 64], F32)
            Bstf = Buf()
            stb = sbt(sa, "stb", [128, 512], BF16)
            Bstb = Buf()
            ytl2 = [sbt(sa, f"ytl{i}", [128, 512], F32) for i in range(2)]
            Bytl2 = [Buf() for _ in range(2)]
            zs = sbt(sa, "zs", [128, 512], F32)
            Bzs = Buf()
            ss2 = sbt(sa, "ss2", [128, 4], F32)
            Bss2 = Buf()
            ynb = sbt(sa, "ynb", [128, 512], BF16)
            Bynb = Buf()
            sto = sbt(sa, "sto", [128, 4, 128], F32)
            Bsto = Buf()

            MEMSET('dve', cbuf[:, :, 0:3], 0.0, Bcb)
            MEMSET('dve', stf[:], 0.0, [Bstf])
            MEMSET('dve', stb[:], 0.0, [Bstb])

            tile_ctr = [0]
            ev_ctr = [0]

            def norm_transpose_tile(src_rows_ap, L, dst_cols, BdstT):
                s = tile_ctr[0] % 2
                tile_ctr[0] += 1
                S.dma(xt[s][:L, :], src_rows_ap, writes=[Bxt[s]])
                rms_rstd(xt[s][:L, :], L, xn[s][:L, :], rs[s][:L, 0:1], rs[s][:L, 1:2], [Bxt[s]], [Bxn[s], Brs[s]], 1.0 / D)
                ACT(xn[s][:L, :], xt[s][:L, :], AF.Copy, [Bxt[s], Brs[s]], [Bxn[s]], scale=rs[s][:L, 1:2])
                for c in range(8):
                    TR(psT[:, c * 128:c * 128 + L], xn[s][:L, c * 128:(c + 1) * 128], identb[:L, :L], [Bxn[s], Bc], [BpsT])
                pv3 = psT[:, :].rearrange("p (c t) -> p c t", c=8)[:, :, 0:L]
                TT('dve', xnT[:, :, dst_cols], pv3, gmixT[:].unsqueeze(2).to_broadcast([128, 8, L]), ALU.mult,
                   [BpsT, Bc], [BdstT])

            def ssd_prep(L, nt, dv_ap, main):
                TT('dve', avall[:L, 0:nt, :], dv_ap, A_b[:L, :].unsqueeze(1).to_broadcast([L, nt, 8]), ALU.mult, [Bdt, Bc], [Bav])
                rhs = avall[:L, 0:nt, :].rearrange("p i h -> p (i h)")
                MM(ps[5][:L, 32:32 + nt * 8], triu[:L, :L], rhs, True, True, [Bc, Bav], [Bps[5]])
                MM(ps[5][:, 64:64 + nt * 8], ones[:L, :], rhs, True, True, [Bc, Bav], [Bps[5]])
                pac = ps[5][:, 32:32 + nt * 8].rearrange("p (i h) -> p i h", h=8)
                pal = ps[5][:, 64:64 + nt * 8].rearrange("p (i h) -> p i h", h=8)
                TS('dve', sm[:L, 0, 0:nt, :], pac[:L], -1.0, None, ALU.mult, None, [Bps[5]], [Bsm])
                TT('dve', sm[:L, 2, 0:nt, :], pal[:L], sm[:L, 0, 0:nt, :], ALU.add, [Bps[5], Bsm], [Bsm])
                ACT(sm[:L, 2, 0:nt, :], sm[:L, 2, 0:nt, :], AF.Exp, [Bsm], [Bsm])
                TT('dve', sm[:L, 2, 0:nt, :], sm[:L, 2, 0:nt, :], dv_ap, ALU.mult, [Bsm, Bdt], [Bsm])
                ACT(sm[:, 3, 0:nt, :], pal, AF.Exp, [Bps[5]], [Bsm])
                if main:
                    ACT(sm[:L, 1, 0:nt, :], pac[:L], AF.Exp, [Bps[5]], [Bsm])

            def ssd_gen(L, cs, BxT, kind, dt_ap, mix_cols, ti, par, mixt=None, Bmix=None):
                main = (kind == 'main')
                xsB, BxsB = xsB2[par], BxsB2[par]
                Rm, BRm, Lm, BLm, Mb, BMb = Rm2[par], BRm2[par], Lm2[par], BLm2[par], Mb2[par], BMb2[par]
                xdt, Bxdt, xde, Bxde, xsk, Bxsk = xdt2[par], Bxdt2[par], xde2[par], Bxde2[par], xsk2[par], Bxsk2[par]
                ytl, Bytl = ytl2[par], Bytl2[par]
                for j in range(6):
                    TR(psT[:L, j * 128:(j + 1) * 128], xbcT[:, j, cs], identb[:, :], [Bxbc[j], Bc], [BpsT])
                CP('act', xsB[:L, :], psT[:L, 0:768], [BpsT], [BxsB])
                yield
                av = avall[:, ti, :]
                TT('dve', xde[:L], xsB[:L, 0:512].rearrange("p (h e) -> p h e", h=8),
                   sm[:L, 2, ti, :].unsqueeze(2).to_broadcast([L, 8, 64]), ALU.mult, [BxsB, Bsm], [Bxde])
                if main:
                    TT('dve', xdt[:L], xsB[:L, 0:512].rearrange("p (h e) -> p h e", h=8),
                       dt_ap.unsqueeze(2).to_broadcast([L, 8, 64]), ALU.mult, [BxsB, Bdt], [Bxdt])
                    TT('pool', xsk[:L], xsB[:L, 0:512].rearrange("p (h e) -> p h e", h=8),
                       dskip_b[:L, :].unsqueeze(2).to_broadcast([L, 8, 64]), ALU.mult, [BxsB, Bc], [Bxsk])
                    for g in range(2):
                        MM(ps[2][:L, g * 256:(g + 1) * 256], xbcT[:, 6 + g, cs], stb[:, g * 256:(g + 1) * 256], g == 0, g == 1,
                           [Bxbc[6 + g], Bstb], [Bps[2]])
                yield
                if main:
                    TT('dve', ytl[:L, :].rearrange("p (h e) -> p h e", h=8), ps[2][:L, :].rearrange("p (h e) -> p h e", h=8),
                       sm[:L, 1, ti, :].unsqueeze(2).to_broadcast([L, 8, 64]), ALU.mult, [Bps[2], Bsm], [Bytl])
                for g in range(2):
                    MM(ps[3][:, g * 256:(g + 1) * 256], xsB[:L, 512 + g * 128:512 + (g + 1) * 128],
                       xde[:L, g * 4:(g + 1) * 4, :].rearrange("p h e -> p (h e)"), g == 0, g == 1, [BxsB, Bxde], [Bps[3]])
                TT('dve', stf[:], stf[:], sm[:, 3, ti, :].unsqueeze(2).to_broadcast([128, 8, 64]), ALU.mult, [Bstf, Bsm], [Bstf])
                yield
                TT('dve', stf[:].rearrange("p h e -> p (h e)"), stf[:].rearrange("p h e -> p (h e)"), ps[3][:, :], ALU.add,
                   [Bstf, Bps[3]], [Bstf])
                CP('act', stb[:, :], stf[:].rearrange("p h e -> p (h e)"), [Bstf], [Bstb])
                yield 'S'
                if not main:
                    return
                TT('dve', Rm[:L, :, :L], triu[:L, :L].unsqueeze(1).to_broadcast([L, 8, L]),
                   av[:L, :].unsqueeze(2).to_broadcast([L, 8, L]), ALU.mult, [Bc, Bav], [BRm])
                yield
                for hb in range(2):
                    bank = ps[6 + hb]
                    o = bank[:, :].rearrange("p (h t) -> p h t", h=4)[:L, :, :L]
                    MM(o, ones[:L, :L], Rm[:L, hb * 4:(hb + 1) * 4, :L], True, False, [Bc, BRm], [Bps[6 + hb]])
                    MM(o, identb[:L, :L], negtri8[:L, hb * 4:(hb + 1) * 4, :L], False, True, [Bc], [Bps[6 + hb]])
                for g in range(2):
                    MM(ps[5][:L, 256 + g * 128:256 + g * 128 + L], xbcT[:, 4 + g, cs], xbcT[:, 6 + g, cs], True, True,
                       [Bxbc[4 + g], Bxbc[6 + g]], [Bps[5]])
                yield
                for h in range(8):
                    bank = ps[6 + h // 4]
                    o = bank[:, :].rearrange("p (h t) -> p h t", h=4)[:L, h % 4, :L]
                    ACT(Lm[:L, h, :L], o, AF.Exp, [Bps[6 + h // 4], Bsm], [BLm], bias=sm[:L, 0, ti, h:h + 1])
                    if h == 3:
                        yield
                yield
                cbv = ps[5][:, 256:512].rearrange("p (g t) -> p g t", g=2)[:L, :, :L]
                TT('dve', Mb[:L, :, :L].rearrange("p (g h) t -> p g h t", g=2),
                   Lm[:L, :, :L].rearrange("p (g h) t -> p g h t", g=2),
                   cbv.unsqueeze(2).to_broadcast([L, 2, 4, L]), ALU.mult, [BLm, Bps[5]], [BMb])
                yield
                for h in range(8):
                    MM(ps[1][:L, h * 64:(h + 1) * 64], Mb[:L, h, :L], xdt[:L, h, :], h == 0, False, [BMb, Bxdt], [Bps[1]])
                MM(ps[1][:L, :], identb[:L, :L], xsk[:L].rearrange("p h e -> p (h e)"), False, True, [Bc, Bxsk], [Bps[1]])
                for c in range(8):
                    MM(ps[4][:L, :], xnT[:, c, cs], w_in_sb[:, c, 1536:2048], c == 0, c == 7, [BxT, Bwin], [Bps[4]])
                yield
                TT('dve', ytl[:L, :], ytl[:L, :], ps[1][:L, :], ALU.add, [Bytl, Bps[1]], [Bytl])
                ACT(zs[:L, :], ps[4][:L, :], AF.Tanh, [Bps[4]], [Bzs], scale=0.5)
                yield
                STT(zs[:L, :], zs[:L, :], 1.0, ps[4][:L, :], ALU.add, ALU.mult, [Bzs, Bps[4]], [Bzs])
                yield
                STT(ytl[:L, :], zs[:L, :], 0.5, ytl[:L, :], ALU.mult, ALU.mult, [Bzs, Bytl], [Bytl])
                yield
                for g in range(2):
                    ACT(zs[:L, g * 256:(g + 1) * 256], ytl[:L, g * 256:(g + 1) * 256], AF.Square, [Bytl], [Bzs, Bss2],
                        accum_out=ss2[:L, g:g + 1])
                yield
                TS('dve', ss2[:L, 0:2], ss2[:L, 0:2], 1.0 / 256, EPS, ALU.mult, ALU.add, [Bss2], [Bss2])
                yield
                TT('pool', ss2[:L, 2:4], ss2[:L, 0:2], mhalf[:L, :].to_broadcast([L, 2]), ALU.pow, [Bss2, Bc], [Bss2])
                yield
                for g in range(2):
                    STT(ynb[:L, g * 256:(g + 1) * 256], ytl[:L, g * 256:(g + 1) * 256], ss2[:L, 2 + g:3 + g],
                        gssm_b[:L, g * 256:(g + 1) * 256], ALU.mult, ALU.mult, [Bytl, Bss2, Bc], [Bynb])
                yield
                for c in range(4):
                    TR(psT[:, c * 128:c * 128 + L], ynb[:L, c * 128:(c + 1) * 128], identb[:L, :L], [Bynb, Bc], [BpsT])
                mixt_ = mixT if mixt is None else mixt
                CP('act', mixt_[:, 4:8, mix_cols], psT[:, 0:512].rearrange("p (c t) -> p c t", c=4)[:, :, 0:L], [BpsT],
                   [BmixS if Bmix is None else Bmix])
                yield

            def run_chunks(gens, width=2):
                gens = list(gens)
                active = []
                nxt = [0]
                ready = [True]

                def start():
                    if nxt[0] < len(gens) and len(active) < width and ready[0]:
                        active.append(gens[nxt[0]])
                        nxt[0] += 1
                        ready[0] = False
                start()
                while active or nxt[0] < len(gens):
                    if not active:
                        ready[0] = True
                        start()
                    for g in list(active):
                        try:
                            r = next(g)
                        except StopIteration:
                            active.remove(g)
                            start()
                            continue
                        if r == 'S':
                            ready[0] = True
                            start()

            def emit_state(dst_ap):
                for c in range(4):
                    TR(ps[4][:, c * 128:(c + 1) * 128], stf[:, 2 * c:2 * c + 2, :].rearrange("p h e -> p (h e)"), identf[:, :],
                       [Bstf, Bc], [Bps[4]])
                CP('dve', sto[:].rearrange("p c n -> p (c n)"), ps[4][:, :], [Bps[4]], [Bsto])
                S.dma(dst_ap.rearrange("(c p) n -> p c n", p=128), sto[:], reads=[Bsto])

            def a1_supertile(t0, T, kind):
                nt = T // 128
                main = kind != 'prefix'
                for i in range(nt):
                    norm_transpose_tile(xl[t0 + i * 128:t0 + (i + 1) * 128, :], 128, slice(i * 128, (i + 1) * 128), BxnT[i])
                BxT_all = BxnT[:nt]
                nx = 8 if main else 6
                chunks = [('x', j, 2048 + j * 128) for j in range(8)]
                if main:
                    chunks += [('q', c, c * 128) for c in range(4)]
                chunks += [('k', c, 512 + c * 128) for c in range(4)]
                chunks += [('v', c, 1024 + c * 128) for c in range(4)]
                for ci, (knd, idx, col0) in enumerate(chunks):
                    bi = 1 + ci % 3
                    for c in range(8):
                        MM(ps[bi][:, 0:T], w_in_sb[:, c, col0:col0 + 128], xnT[:, c, 0:T], c == 0, c == 7, BxT_all + [Bwin], [Bps[bi]])
                    if knd == 'x':
                        eng = 'act' if (idx % 2 == 0) else 'dve'
                        CP(eng, cbuf[:, idx, 3:3 + T], ps[bi][:, 0:T], [Bps[bi]], [Bcb[idx]])
                    else:
                        si = ev_ctr[0] % 3
                        ev_ctr[0] += 1
                        if knd == 'q':
                            ACT(stg[si][:, 0:T], ps[bi][:, 0:T], AF.Copy, [Bps[bi]], [Bstg[si]], scale=0.125)
                            S.dma(qT_d[idx, :, t0 - NPRE:t0 - NPRE + T], stg[si][:, 0:T], reads=[Bstg[si]], writes=[Bqd])
                        else:
                            CP('dve' if knd == 'k' else 'act', stg[si][:, 0:T], ps[bi][:, 0:T], [Bps[bi]], [Bstg[si]])
                            dst = kT_d if knd == 'k' else vT_d
                            S.dma(dst[idx, :, t0:t0 + T], stg[si][:, 0:T], reads=[Bstg[si]], writes=[Bkd if knd == 'k' else Bvd])
                if kind == 'main':
                    for i in range(nt):
                        for wi, (col0, dst) in enumerate(((512, k_loc), (1024, v_loc))):
                            for c in range(8):
                                MM(ps[4][:, :], xnT[:, c, i * 128:(i + 1) * 128], w_in_sb[:, c, col0:col0 + 512], c == 0, c == 7,
                                   [BxnT[i], Bwin], [Bps[4]])
                            CP('dve' if wi == 0 else 'act', kvo[wi][:, :], ps[4][:, :], [Bps[4]], [Bkvo[wi]])
                            r0 = t0 - NPRE + i * 128
                            S.dma(dst[r0:r0 + 128, :], kvo[wi][:, :], reads=[Bkvo[wi]])
                for i in range(nt):
                    for c in range(8):
                        MM(ps[5][:, i * 8:(i + 1) * 8], xnT[:, c, i * 128:(i + 1) * 128], w_in_sb[:, c, 3072:3080], c == 0, c == 7,
                           [BxnT[i], Bwin], [Bps[5]])
                dv = dtall[:, 0:nt, :]
                TT('dve', dv, ps[5][:, 0:nt * 8].rearrange("p (i h) -> p i h", h=8), dtb[:].unsqueeze(1).to_broadcast([128, nt, 8]),
                   ALU.add, [Bps[5], Bc], [Bdt])
                ACT(dv, dv, AF.Exp, [Bdt], [Bdt])
                ACT(dv, dv, AF.Ln, [Bdt, Bc], [Bdt], bias=onec[:, :])
                if kind == 'prefix':
                    TS('dve', dv, dv, pvt[:, 0:1], None, ALU.mult, None, [Bdt, Bc], [Bdt])
                for j in range(nx):
                    a_ = j % 2
                    ACT(acc[a_][:, 0:T], cbuf[:, j, 3:3 + T], AF.Identity, [Bcb[j], Bc], [Bacc[a_]], scale=cwT[:, j, 3:4], bias=cbT[:, j:j + 1])
                    for k in range(3):
                        STT(acc[a_][:, 0:T], cbuf[:, j, k:k + T], cwT[:, j, k:k + 1], acc[a_][:, 0:T], ALU.mult, ALU.add,
                            [Bcb[j], Bc, Bacc[a_]], [Bacc[a_]])
                    ACT(th[a_][:, 0:T], acc[a_][:, 0:T], AF.Tanh, [Bacc[a_]], [Bth[a_]])
                    STT(xbcT[:, j, 0:T], th[a_][:, 0:T], 1.0, acc[a_][:, 0:T], ALU.add, ALU.mult, [Bth[a_], Bacc[a_]], [Bxbc[j]])
                if kind == 'main' and t0 + T == NPRE + NMAIN:
                    for hf in range(2):
                        for j in range(4):
                            TR(ps[4][0:3, j * 128:(j + 1) * 128], cbuf[:, hf * 4 + j, T:T + 3], identf[:, :], [Bcb[hf * 4 + j], Bc], [Bps[4]])
                        CP('dve', kvo[hf][0:3, :], ps[4][0:3, :], [Bps[4]], [Bkvo[hf]])
                        S.dma(conv_loc[:, hf * 512:(hf + 1) * 512], kvo[hf][0:3, :], reads=[Bkvo[hf]])
                for j in range(8):
                    CP('pool', cbuf[:, j, 0:3], cbuf[:, j, T:T + 3], [Bcb[j]], [Bcb[j]])
                ssd_prep(128, nt, dtall[:, 0:nt, :], main)
                run_chunks([ssd_gen(128, slice(i * 128, (i + 1) * 128), BxnT[i], 'main' if main else 'prefix', dtall[:, i, :],
                                    slice(t0 - NPRE + i * 128, t0 - NPRE + (i + 1) * 128), i, i % 2) for i in range(nt)])
                if kind == 'main' and t0 + T == NPRE + NMAIN:
                    emit_state(ssm_loc)

            Bqd, Bkd, Bvd = Buf("qd"), Buf("kd"), Buf("vd")
            st_list = [(s * 512, 512, 'prefix') for s in range(4)] + [(NPRE + s * 512, 512, 'main') for s in range(4)] + \
                      [(NPRE + NMAIN, 128, 'halo')]
            if skip_p:
                st_list = []
                for c_ in range(128):
                    bm_dma(c_)
            for si_, (t0, T, kind) in enumerate(st_list):
                a1_supertile(t0, T, kind)
                for c_ in range(si_ * 16, min(128, si_ * 16 + 16)):
                    bm_dma(c_)


            if not skip_a1:
                scb = sbt(sa, "scb", [128, 8, 4, 11], F32)
                Bscb = [Buf() for _ in range(8)]
                sc3 = sbt(sa, "sc3", [128, 8, 12], F32)
                Bsc3 = Buf()
                hs12 = xt[0]
                Bhs12 = Bxt[0]
                norm_transpose_tile(xs_d[:, :], 32, slice(0, 32), BxnT[0])
                if ks1 == 0.1:
                    S.barrier(); S.finish(); return nc
                schunks = [('q', c, c * 128) for c in range(4)] + [('k', c, 512 + c * 128) for c in range(4)] + \
                          [('x', j, 2048 + j * 128) for j in range(8)]
                for ci, (knd, idx, col0) in enumerate(schunks):
                    bi = 1 + ci % 3
                    for c in range(8):
                        MM(ps[bi][:, 0:32], w_in_sb[:, c, col0:col0 + 128], xnT[:, c, 0:32], c == 0, c == 7, [BxnT[0], Bwin], [Bps[bi]])
                    if knd == 'q':
                        ACT(sQT[0:64, idx, 0, :], ps[bi][0:64, 0:32], AF.Copy, [Bps[bi]], [Bsq], scale=0.125)
                        ACT(sQT[64:128, idx, 1, :], ps[bi][64:128, 0:32], AF.Copy, [Bps[bi]], [Bsq], scale=0.125)
                    elif knd == 'k':
                        CP('dve', sKT[:, idx, :], ps[bi][:, 0:32], [Bps[bi]], [Bsq])
                    else:
                        CP('act' if idx % 2 == 0 else 'dve', scb[:, idx, :, 3:11], ps[bi][:, 0:32].rearrange("p (b t) -> p b t", b=4),
                           [Bps[bi]], [Bscb[idx]])
                if ks1 == 0.2:
                    S.barrier(); S.finish(); return nc
                kvm = os.environ.get('KV', 'all')
                for wi, (col0, dst) in enumerate(((512, ks_d), (1024, vs_d))):
                    for c in range(8):
                        MM(ps[4][0:32, :], xnT[:, c, 0:32], w_in_sb[:, c, col0:col0 + 512], c == 0, c == 7, [BxnT[0], Bwin], [Bps[4]])
                    if kvm in ('all', 'cp', 'cpdma', 'cpsvn'):
                        CP('dve', kvo[wi][0:32, :], ps[4][0:32, :], [Bps[4]], [Bkvo[wi]])
                    if wi == 1 and kvm in ('all', 'cpsvn'):
                        CP('act', sVn[0:32, :, 0:64], ps[4][0:32, :].rearrange("p (h e) -> p h e", h=8), [Bps[4]], [Bsq])
                    if kvm in ('all', 'cpdma'):
                        S.dma(dst[:, :], kvo[wi][0:32, :], reads=[Bkvo[wi]])
                if ks1 == 1:
                    S.barrier(); S.finish(); return nc
                S.dma(hs12[0:12, :], sconv_d[:, :], writes=[Bhs12])
                for hf in range(2):
                    for j in range(4):
                        TR(ps[4][:, j * 12:(j + 1) * 12], hs12[0:12, (hf * 4 + j) * 128:(hf * 4 + j + 1) * 128], identf[0:12, 0:12],
                           [Bhs12, Bc], [Bps[4]])
                    CP('dve', scb[:, hf * 4:(hf + 1) * 4, :, 0:3], ps[4][:, 0:48].rearrange("p (j b t) -> p j b t", j=4, b=4),
                       [Bps[4]], Bscb[hf * 4:(hf + 1) * 4])
                for j in range(8):
                    a_ = j % 2
                    accv = acc[a_][:, 0:32].rearrange("p (b t) -> p b t", b=4)
                    ACT(accv, scb[:, j, :, 3:11], AF.Identity, [Bscb[j], Bc], [Bacc[a_]], scale=cwT[:, j, 3:4], bias=cbT[:, j:j + 1])
                    for k in range(3):
                        STT(accv, scb[:, j, :, k:k + 8], cwT[:, j, k:k + 1], accv, ALU.mult, ALU.add, [Bscb[j], Bc, Bacc[a_]], [Bacc[a_]])
                    ACT(th[a_][:, 0:32], acc[a_][:, 0:32], AF.Tanh, [Bacc[a_]], [Bth[a_]])
                    STT(xbcT[:, j, 0:32], th[a_][:, 0:32], 1.0, acc[a_][:, 0:32], ALU.add, ALU.mult, [Bth[a_], Bacc[a_]], [Bxbc[j]])
                CP('pool', sc3[:].rearrange("p j (b t) -> p j b t", b=4), scb[:, :, :, 8:11], Bscb, [Bsc3])
                for hf in range(2):
                    for j in range(4):
                        TR(ps[4][0:12, j * 128:(j + 1) * 128], sc3[:, hf * 4 + j, :], identf[:, :], [Bsc3, Bc], [Bps[4]])
                    CP('dve', hs12[0:12, hf * 512:(hf + 1) * 512], ps[4][0:12, :], [Bps[4]], [Bhs12])
                S.dma(conv_s_d[:, :], hs12[0:12, :], reads=[Bhs12])
                if ks1 == 2:
                    S.barrier(); S.finish(); return nc
                for b in range(4):
                    for c in range(8):
                        MM(ps[5][0:8, b * 8:(b + 1) * 8], xnT[:, c, b * 8:(b + 1) * 8], w_in_sb[:, c, 3072:3080], c == 0, c == 7,
                           [BxnT[0], Bwin], [Bps[5]])
                dvs = dtall[0:8, 0:4, :]
                TT('dve', dvs, ps[5][0:8, 0:32].rearrange("p (i h) -> p i h", h=8), dtb[0:8, :].unsqueeze(1).to_broadcast([8, 4, 8]),
                   ALU.add, [Bps[5], Bc], [Bdt])
                ACT(dvs, dvs, AF.Exp, [Bdt], [Bdt])
                ACT(dvs, dvs, AF.Ln, [Bdt, Bc], [Bdt], bias=onec[0:8, :])
                if ks1 == 3:
                    S.barrier(); S.finish(); return nc
                ssd_prep(8, 4, dtall[0:8, 0:4, :], True)
                for b in range(4):
                    S.dma(sto[:], sssm_d[b].rearrange("(c p) n -> p c n", p=128), writes=[Bsto])
                    for c in range(4):
                        TR(ps[4][:, c * 128:(c + 1) * 128], sto[:, c, :], identf[:, :], [Bsto, Bc], [Bps[4]])
                    CP('dve', stf[:].rearrange("p h e -> p (h e)"), ps[4][:, :], [Bps[4]], [Bstf])
                    CP('act', stb[:, :], stf[:].rearrange("p h e -> p (h e)"), [Bstf], [Bstb])
                    for _ in ssd_gen(8, slice(b * 8, (b + 1) * 8), BxnT[0], 'main', dtall[0:8, b, :], slice(b * 8, (b + 1) * 8), b, b % 2,
                                     mixt=smixT, Bmix=BsmixS):
                        pass
                    emit_state(ssm_s_d[b])

        S.barrier()
        if stage <= 1:
            S.finish()
            return nc

        with ExitStack() as s2:
            sel = sbt(s2, "sel", [128, 64], F32)
            QT = [sbt(s2, f"QT{i}", [128, 2, NQ], BF16) for i in range(2)]
            KT = [sbt(s2, f"KT{i}", [128, NLOC], BF16) for i in range(2)]
            VT = [sbt(s2, f"VT{i}", [128, NLOC], BF16) for i in range(2)]
            Bqkv = [Buf() for _ in range(2)]
            NVB = 70
            Vb = sbt(s2, "Vb", [128, NVB, 2, 66], BF16)
            BVb = Buf()
            accA = sbt(s2, "accA", [128, 2, NQ], F32)
            BaccA = Buf()
            PT = [sbt(s2, f"PT{i}", [128, 512], BF16) for i in range(4)]
            BPT = [Buf() for _ in range(4)]
            rc = [sbt(s2, f"rc{i}", [64, 512], F32) for i in range(2)]
            Brc = [Buf() for _ in range(2)]
            TT('dve', Bdg[:], Bm[:, :, 0, 0:2], negd[:].unsqueeze(1).to_broadcast([128, 8, 2]), ALU.add, [BBm, Bc], [Bc])
            if stage == 1.2:
                S.barrier(); S.finish(); return nc
            wtmp = [sbt(s2, f"wtmp{i}", [128, 8, 256], BF16) for i in range(2)]
            Bwtmp = [Buf(f"wtmp{i}") for i in range(2)]
            Bwup_d = Buf("wup_d")
            w_up_v = w_up.rearrange("(c p) n -> p c n", p=128)

            def prep_wup(j):
                s = j % 2
                S.dma(wtmp[s][:, :, 0:128], w_up_v[:, :, j * 128:(j + 1) * 128], writes=[Bwtmp[s]], eng='pool')
                S.dma(wtmp[s][:, :, 128:256], w_up_v[:, :, DFF + j * 128:DFF + (j + 1) * 128], writes=[Bwtmp[s]], eng='pool')
                S.dma(wup_d[j, :, :], wtmp[s][:].rearrange("p c n -> p (c n)"), reads=[Bwtmp[s]], writes=[Bwup_d])


            wtb = [sbt(s2, f"wtb{i}", [128, D], BF16) for i in range(2)]
            Bwtb = [Buf() for _ in range(2)]
            BwB_d = Buf("wB_d")
            wB_src = [w_out[c * 128:(c + 1) * 128, :] for c in range(8)] + [w_down[j * 128:(j + 1) * 128, :] for j in range(NG)] + \
                     [w_ple_proj[c * 128:(c + 1) * 128, :] for c in range(2)] + [w_ple_gate[c * 128:(c + 1) * 128, :] for c in range(8)]

            fcwT = sbt(s2, "fcwT", [128, 2 * NG, 3], F32)
            Bfcw = Buf()
            for k in range(3):
                for hf in range(2):
                    S.dma(fcwT[:, hf * NG:(hf + 1) * NG, k], ffn_conv_w[k, hf * DFF:(hf + 1) * DFF].rearrange("(c p) -> p c", p=128),
                          writes=[Bfcw], allow_slow_non_contiguous=True)
            dgs = [sbt(s2, f"dgs{i}", [128, 2, 3, 128], BF16) for i in range(2)]
            Bdgs = [Buf() for _ in range(2)]
            Bdg_d = Buf("dg_d")

            def prep_dg(j):
                sl = j % 2
                for k in range(3):
                    ACT(dgs[sl][:, 0, k, :], identb[:, :], AF.Copy, [Bc, Bfcw], [Bdgs[sl]], scale=fcwT[:, j, k:k + 1])
                    TS('dve', dgs[sl][:, 1, k, :], identb[:, :], fcwT[:, NG + j, k:k + 1], None, ALU.mult, None, [Bc, Bfcw], [Bdgs[sl]])
                S.dma(dg_d[j, :, :], dgs[sl][:].rearrange("p a k n -> p (a k n)"), reads=[Bdgs[sl]], writes=[Bdg_d])

            def prep_wB(i):
                sl = i % 2
                S.dma(wtb[sl][:, :], wB_src[i], writes=[Bwtb[sl]], eng='pool')
                S.dma(wB_d[i, :, :], wtb[sl][:, :], reads=[Bwtb[sl]], writes=[BwB_d])

            sKc = sbt(s2, "sKc", [128, 4, 13 * 128], BF16)
            BsKc = Buf()
            sVc = sbt(s2, "sVc", [128, 13, 8, 66], BF16)
            BsVc = Buf()
            kst = [sbt(s2, f"kst{i}", [128, 512], F32) for i in range(2)]
            Bkst = [Buf() for _ in range(2)]
            vst = [sbt(s2, f"vst{i}", [128, 512], F32) for i in range(2)]
            Bvst = [Buf() for _ in range(2)]
            accS = sbt(s2, "accS", [128, 4, 2, 32], F32)
            BaccS = [Buf() for _ in range(4)]
            cmt = sbt(s2, "cmt", [32, 3, 32], F32)
            Bown = sbt(s2, "Bown", [32, 8, 3, 32], BF16)
            S.dma(cmt[:], c_cm[:, :, :], writes=[Bc])
            TT('dve', Bown[:], Bm[0:32, :, 0, 0:32].unsqueeze(2).to_broadcast([32, 8, 3, 32]),
               cmt[:].unsqueeze(1).to_broadcast([32, 8, 3, 32]), ALU.add, [Bc], [Bc])
            MEMSET('pool', sVc[:].rearrange("p t h e -> p (t h e)"), 1.0, [BsVc])
            MEMSET('dve', sel[:], 0.0, [Bc])
            MEMSET('dve', sel[64:65, :], 1.0, [Bc])
            MEMSET('dve', accA[:], 1.0, [BaccA])
            for i_ in range(2):
                MEMSET('pool', QT[i_][:].rearrange("p h q -> p (h q)"), 0.0, [Bqkv[i_]])
            MEMSET('pool', Vb[:].rearrange("p b h e -> p (b h e)"), 1.0, [BVb])
            for (a0, a1) in ((0, 1), (18, 22), (38, 54)):
                TS('dve', Vb[:, a0:a1, :, 64:65], Vb[:, a0:a1, :, 64:65], pvt[:, 0:1], None, ALU.mult, None, [BVb, Bc], [BVb])

            vblocks = []
            for tau in range(15, 33):
                vblocks.append((tau - 15, 128 * tau, 1))
            for sg in range(3, 8):
                for r in range(4):
                    vblocks.append((18 + (sg - 3) * 4 + r, 512 * sg + r, 4))
            for z_ in range(2):
                for r in range(16):
                    vblocks.append((38 + 16 * z_ + r, 2048 * z_ + r, 16))
            assert len(vblocks) == NVB

            cnt = {'s': 0, 'o': 0, 'pt': 0, 'n': 0, 'ev': 0}

            def cols(c0, step, n):
                return slice(c0, c0 + step * (n - 1) + 1, step) if step > 1 else slice(c0, c0 + n)

            def attn_core(N, nkeys, k_aps, q_fn, v_fn, bm_ap, acc_ap, mode, pv_rep, rK, rQ, rV, wAcc):
                nk = len(k_aps)
                bi = 1 + cnt['s'] % 3
                cnt['s'] += 1
                psv = ps[bi][:, :].rearrange("p (h k q) -> p h k q", h=2, k=2)
                first = True
                for h2 in range(2):
                    for k in range(nk):
                        MM(psv[:nkeys, h2, k, 0:N], k_aps[k], q_fn(h2), first, False, rK + rQ, [Bps[bi]])
                        first = False
                MM(psv[:nkeys, :, 0:nk, 0:N], identb[:nkeys, :nkeys], bm_ap, False, True, [Bc], [Bps[bi]])
                pi = cnt['pt'] % 4
                cnt['pt'] += 1
                ptv = PT[pi][:, :].rearrange("p (h k q) -> p h k q", h=2, k=2)
                ACT(ptv[:nkeys, :, 0:nk, 0:N], psv[:nkeys, :, 0:nk, 0:N], AF.Exp, [Bps[bi]], [BPT[pi]])
                def stage2():
                    oi = 4 + cnt['o'] % 2
                    cnt['o'] += 1
                    pso = ps[oi][0:65, 0:256].rearrange("p (h q) -> p h q", h=2)
                    first2 = True
                    for h2 in range(2):
                        tot = nk * pv_rep
                        ii = 0
                        for k in range(nk):
                            for _rep in range(pv_rep):
                                ii += 1
                                MM(pso[:, h2, 0:N], v_fn(k, h2), ptv[:nkeys, h2, k, 0:N], first2, ii == tot, rV + [BPT[pi]], [Bps[oi]])
                                first2 = False
                    if mode == 'copy':
                        CP('act', acc_ap, pso[:, :, 0:N], [Bps[oi]], wAcc)
                    else:
                        TT('dve', acc_ap, acc_ap, pso[:, :, 0:N], ALU.add, [Bps[oi]] + wAcc, wAcc)

                if pending:
                    pending.pop(0)()
                pending.append(stage2)

            pending = []

            def flush_pending():
                while pending:
                    pending.pop(0)()

            def attn_unit(p, s, N, qc, kbs, bm_ap, accc, mode, pv_rep=1):
                attn_core(N, 128, [KT[s][:, cols(kc0, kst, 128)] for (kc0, kst, _) in kbs],
                          lambda h2: QT[s][:, h2, cols(qc[0], qc[1], N)],
                          lambda k, h2: Vb[:, kbs[k][2], h2, 0:65], bm_ap,
                          accA[0:65, :, cols(accc[0], accc[1], N)], mode, pv_rep, [Bqkv[s]], [], [BVb], [BaccA])

            def bm_std(p, br, N):
                return Bm[:, 2 * p:2 * p + 2, br, :].rearrange("p h (k q) -> p h k q", k=2)[:, :, :, 0:N]

            def bm_prev(p, br, N):
                return Bm[:, 2 * p:2 * p + 2, br, 128:128 + N].unsqueeze(2)

            for p in range(4):
                s = p % 2
                for j in range(p * 6, min(NG, p * 6 + 6)):
                    prep_wup(j)
                for j in range(p * 10, p * 10 + 10):
                    prep_wB(j)
                for j in range(p * 6, min(NG, p * 6 + 6)):
                    prep_dg(j)
                S.dma(QT[s][0:64, 0, :], qT_d[p, 0:64, :], reads=[Bqd], writes=[Bqkv[s]])
                S.dma(QT[s][64:128, 1, :], qT_d[p, 64:128, :], reads=[Bqd], writes=[Bqkv[s]])
                S.dma(KT[s][:, :], kT_d[p, :, :], reads=[Bkd], writes=[Bqkv[s]])
                S.dma(VT[s][:, :], vT_d[p, :, :], reads=[Bvd], writes=[Bqkv[s]])
                for g0 in range(0, NVB, 8):
                    grp = vblocks[g0:g0 + 8]
                    for sl, (vbi, c0, st_) in enumerate(grp):
                        TR(psT[:, sl * 128:(sl + 1) * 128], VT[s][:, cols(c0, st_, 128)], identb[:, :], [Bqkv[s], Bc], [BpsT])
                    n = len(grp)
                    eng = 'act' if (cnt['ev'] % 2 == 0) else 'dve'
                    cnt['ev'] += 1
                    CP(eng, Vb[:, g0:g0 + n, :, 0:64], psT[:, 0:n * 128].rearrange("p (b h e) -> p b h e", b=n, h=2), [BpsT], [BVb])
                if stage == 1.4:
                    S.barrier(); S.finish(); return nc
                for n in range(16):
                    attn_unit(p, s, 128, (128 * n, 1), [(NPRE + 128 * n, 1, n + 1), (NPRE + 128 * (n - 1), 1, n)],
                              bm_std(p, 0, 128), (128 * n, 1), 'copy')
                attn_unit(p, s, 2, (2048, 1), [(NPRE + 2048, 1, 17), (NPRE + 1920, 1, 16)], bm_std(p, 0, 2), (2048, 1), 'copy')
                if stage == 1.6:
                    S.barrier(); S.finish(); return nc
                for sg in range(4):
                    for r in range(4):
                        attn_unit(p, s, 128, (512 * sg + r, 4),
                                  [(NPRE + 512 * sg + r, 4, 18 + (sg + 1) * 4 + r), (NPRE + 512 * (sg - 1) + r, 4, 18 + sg * 4 + r)],
                                  bm_std(p, 1, 128), (512 * sg + r, 4), 'add')
                for r in range(16):
                    attn_unit(p, s, 128, (r, 16), [(NPRE + r, 16, 38 + 16 + r), (r, 16, 38 + r)], bm_std(p, 2, 128), (r, 16), 'add')
                for qi in range(2):
                    attn_unit(p, s, 1, (2048 + qi, 1), [(NPRE + 1536 + qi, 4, 18 + 16 + qi)], bm_prev(p, 1, 1), (2048 + qi, 1), 'add')
                    attn_unit(p, s, 1, (2048 + qi, 1), [(NPRE + qi, 16, 38 + 16 + qi)], bm_prev(p, 2, 1), (2048 + qi, 1), 'add')
                attn_unit(p, s, 2, (2048, 1), [(NPRE + 2048, 1, 17)], Bdg[:, 2 * p:2 * p + 2, 0:2].unsqueeze(2), (2048, 1), 'add', pv_rep=2)
                flush_pending()
                for h2 in range(2):
                    for c0 in range(0, NQ, 512):
                        n = min(512, NQ - c0)
                        bi = 6 + cnt['n'] % 2
                        ri = cnt['n'] % 2
                        cnt['n'] += 1
                        MM(ps[bi][0:64, 0:n], sel[0:65, 0:64], accA[0:65, h2, c0:c0 + n], True, True, [Bc, BaccA], [Bps[bi]])
                        S.op('dve', lambda e, o=rc[ri][0:64, 0:n], i_=ps[bi][0:64, 0:n]: e.reciprocal(out=o, in_=i_), [Bps[bi]], [Brc[ri]])
                        TT('dve', mixT[h2 * 64:(h2 + 1) * 64, p, c0:c0 + n], accA[0:64, h2, c0:c0 + n], rc[ri][0:64, 0:n], ALU.mult,
                           [BaccA, Brc[ri]], [BmixA[p]])

            if not skip_a1:
                for p in range(4):
                    for br in range(3):
                        attn_core(32, 32, [sKT[:, p, 0:32]], lambda h2, p=p: sQT[:, p, h2, 0:32],
                                  lambda k, h2, p=p: sVn[0:32, 2 * p + h2, 0:65], Bown[0:32, 2 * p:2 * p + 2, br, :].unsqueeze(2),
                                  accS[0:65, p, :, :], 'copy' if br == 0 else 'add', 1, [Bsq], [], [Bsq], [BaccS[p]])
                flush_pending()
                tix = [0]
                for b in range(4):
                    tiles = [(1920, 1)] + [(1536 + r, 4) for r in range(4)] + [(r, 16) for r in range(8)]
                    for ti, (r0, st_) in enumerate(tiles):
                        sl = tix[0] % 2
                        tix[0] += 1
                        rows = slice(r0, r0 + 127 * st_ + 1, st_) if st_ > 1 else slice(r0, r0 + 128)
                        S.dma(kst[sl][:, :], ck_d[b, rows, :], writes=[Bkst[sl]])
                        S.dma(vst[sl][:, :], cv_d[b, rows, :], writes=[Bvst[sl]])
                        for c in range(4):
                            TR(ps[7][:, c * 128:(c + 1) * 128], kst[sl][:, c * 128:(c + 1) * 128], identf[:, :], [Bkst[sl], Bc], [Bps[7]])
                        CP('act' if ti % 2 == 0 else 'dve', sKc[:, :, ti * 128:(ti + 1) * 128],
                           ps[7][:, :].rearrange("p (c k) -> p c k", c=4), [Bps[7]], [BsKc])
                        CP('pool', sVc[:, ti, :, 0:64], vst[sl][:, :].rearrange("p (h e) -> p h e", h=8), [Bvst[sl]], [BsVc])
                    for p in range(4):
                        def unit(ti, N, qcol0, qstep, br, p=p, b=b):
                            attn_core(N, 128, [sKc[:, p, ti * 128:(ti + 1) * 128]],
                                      lambda h2: sQT[:, p, h2, cols(b * 8 + qcol0, qstep, N)],
                                      lambda k, h2: sVc[:, ti, 2 * p + h2, 0:65], bm_prev(p, br, N),
                                      accS[0:65, p, :, cols(b * 8 + qcol0, qstep, N)], 'add', 1, [BsKc], [Bsq], [BsVc], [BaccS[p]])
                        unit(0, 8, 0, 1, 0)
                        for r in range(4):
                            unit(1 + r, 2, r, 4, 1)
                        for r in range(8):
                            unit(5 + r, 1, r, 1, 2)
                    flush_pending()
                for p in range(4):
                    for h2 in range(2):
                        bi = 6 + cnt['n'] % 2
                        ri = cnt['n'] % 2
                        cnt['n'] += 1
                        MM(ps[bi][0:64, 0:32], sel[0:65, 0:64], accS[0:65, p, h2, :], True, True, [Bc, BaccS[p]], [Bps[bi]])
                        S.op('dve', lambda e, o=rc[ri][0:64, 0:32], i_=ps[bi][0:64, 0:32]: e.reciprocal(out=o, in_=i_), [Bps[bi]], [Brc[ri]])
                        TT('dve', smixT[h2 * 64:(h2 + 1) * 64, p, :], accS[0:64, p, h2, :], rc[ri][0:64, 0:32], ALU.mult,
                           [BaccS[p], Brc[ri]], [BsmixA[p]])
        sA.close()
        S.barrier()
        if stage <= 2:
            S.finish()
            return nc

        with ExitStack() as s3:
            w_out_sb = sbt(s3, "w_out_sb", [128, 8, D], BF16)
            w_dn_sb = sbt(s3, "w_dn_sb", [128, NG, D], BF16)
            w_pp_sb = sbt(s3, "w_pp_sb", [128, 2, D], BF16)
            w_pg_sb = sbt(s3, "w_pg_sb", [128, 8, D], BF16)
            BwB = Buf()
            S.dma(w_out_sb[:], wB_d[0:8, :, :].rearrange("c p n -> p c n"), reads=[BwB_d], writes=[BwB])
            S.dma(w_dn_sb[:], wB_d[8:30, :, :].rearrange("c p n -> p c n"), reads=[BwB_d], writes=[BwB])
            S.dma(w_pp_sb[:], wB_d[30:32, :, :].rearrange("c p n -> p c n"), reads=[BwB_d], writes=[BwB])
            S.dma(w_pg_sb[:], wB_d[32:40, :, :].rearrange("c p n -> p c n"), reads=[BwB_d], writes=[BwB])
            fcbT = sbt(s3, "fcbT", [128, 2 * NG], F32)
            gple_b = sbt(s3, "gple_b", [128, D], F32)
            gfin_b = sbt(s3, "gfin_b", [128, D], F32)
            for hf in range(2):
                S.dma(fcbT[:, hf * NG:(hf + 1) * NG], ffn_conv_b[hf * DFF:(hf + 1) * DFF].rearrange("(c p) -> p c", p=128),
                      writes=[Bc], allow_slow_non_contiguous=True)
            S.dma(gple_b[:], g_ple.partition_broadcast(128), writes=[Bc])
            S.dma(gfin_b[:], g_final.partition_broadcast(128), writes=[Bc])

            TB = 256
            xtb = [sbt(s3, f"xtb{i}", [128, D], F32) for i in range(2)]
            Bxtb = [Buf() for _ in range(2)]
            ptb = [sbt(s3, f"ptb{i}", [128, 256], F32) for i in range(2)]
            Bptb = [Buf() for _ in range(2)]
            hh = sbt(s3, "hh", [128, 2, D], F32)
            Bhh = [Buf() for _ in range(2)]
            hn2 = [sbt(s3, f"hn{i}", [128, D], BF16) for i in range(2)]
            Bhn2 = [Buf() for _ in range(2)]
            rsb2 = [sbt(s3, f"rsb{i}", [128, 8], F32) for i in range(2)]
            Brsb2 = [Buf() for _ in range(2)]

            def lockstep(gens):
                gens = list(gens)
                while gens:
                    for g in list(gens):
                        try:
                            next(g)
                        except StopIteration:
                            gens.remove(g)

            hnT = sbt(s3, "hnT", [128, 8, TB], BF16)
            BhnT = [Buf() for _ in range(2)]
            wg = [sbt(s3, f"wg{i}", [128, 8, 256], BF16) for i in range(3)]
            Bwg = [Buf() for _ in range(3)]
            ub = [sbt(s3, f"ub{i}", [128, 2, TB + 2], BF16) for i in range(2)]
            Bub = [Buf() for _ in range(2)]
            hist = sbt(s3, "hist", [128, NG, 2, 2], BF16)
            Bhist = [Buf() for _ in range(NG)]
            ffo = sbt(s3, "ffo", [128, 2, NG, 2], F32)
            Bffo = Buf()
            ffs = sbt(s3, "ffs", [8, 512], F32)
            ffo_s = sbt(s3, "ffo_s", [128, 2, NG, 4, 2], F32)
            hist_s = sbt(s3, "hist_s", [128, 2, NG, 4, 2], BF16)
            Bhist_s = Buf()
            hs8 = sbt(s3, "hs8", [8, 512], F32)
            Bhs8 = Buf()
            Bffs = Buf()
            dg = [sbt(s3, f"dg{i}", [128, 2, 3, 128], BF16) for i in range(2)]
            Bdgm = [Buf() for _ in range(2)]
            sa_ = [sbt(s3, f"sa{i}", [128, TB], F32) for i in range(2)]
            Bsa = [Buf() for _ in range(2)]
            gT = sbt(s3, "gT", [128, NG, TB], BF16)
            BgT = Buf()
            ppb2 = [sbt(s3, f"ppb{i}", [128, 256], BF16) for i in range(2)]
            Bppb2 = [Buf() for _ in range(2)]
            ppT2 = [sbt(s3, f"ppT{i}", [128, 2, 128], BF16) for i in range(2)]
            BppT2 = [Buf() for _ in range(2)]
            t12 = xtb
            Bt12 = Bxtb
            h2T2 = [sbt(s3, f"h2T{i}", [128, 8, 128], BF16) for i in range(2)]
            Bh2T2 = [Buf() for _ in range(2)]
            gt2 = [sbt(s3, f"gt{i}", [128, D], F32) for i in range(2)]
            Bgt2 = [Buf() for _ in range(2)]
            MEMSET('dve', hist[:].rearrange("p j a t -> p (j a t)"), 0.0, Bhist)

            tcb = [0]

            def b_supertile(t0, T, samp=False):
                L = 32 if samp else 128
                nt = 1 if samp else T // 128
                mixsrc = smixT if samp else mixT
                Bmixsrc = ([BsmixS] + BsmixA) if samp else ([BmixS] + BmixA)
                def head_gen(i):
                    s = i % 2
                    hn, Bhn, rsb, Brsb = hn2[i], Bhn2[i], rsb2[i], Brsb2[i]
                    r0 = t0 + i * 128
                    xsrc = xs_d[0:32, :] if samp else xl[NPRE + r0:NPRE + r0 + 128, :]
                    S.dma(xtb[s][:L, :], xsrc, writes=[Bxtb[s]])
                    for hf in range(2):
                        bk = 1 + 2 * (i % 2) + hf
                        for c in range(8):
                            MM(ps[bk][:L, :], mixsrc[:, c, r0:r0 + L], w_out_sb[:, c, hf * 512:(hf + 1) * 512], c == 0, c == 7,
                               [BwB] + Bmixsrc, [Bps[bk]])
                        TT('dve', hh[:L, i, hf * 512:(hf + 1) * 512], xtb[s][:L, hf * 512:(hf + 1) * 512], ps[bk][:L, :], ALU.add,
                           [Bxtb[s], Bps[bk]], [Bhh[i]])
                        yield
                    rms_rstd(hh[:L, i, :], L, hn[:L, :], rsb[:L, 0:1], rsb[:L, 1:2], [Bhh[i]], [Bhn, Brsb], 1.0 / D)
                    yield
                    ACT(hn[:L, :], hh[:L, i, :], AF.Copy, [Bhh[i], Brsb], [Bhn], scale=rsb[:L, 1:2])
                    yield
                    for c in range(8):
                        TR(psT[:, c * 128:c * 128 + L], hn[:L, c * 128:(c + 1) * 128], identb[:L, :L], [Bhn, Bc], [BpsT])
                    TT('dve', hnT[:, :, i * 128:i * 128 + L], psT[:, :].rearrange("p (c t) -> p c t", c=8)[:, :, 0:L],
                       gffnT[:].unsqueeze(2).to_broadcast([128, 8, L]), ALU.mult, [BpsT, Bc], [BhnT[i]])
                    yield

                lockstep([head_gen(i) for i in range(nt)])
                last_main = (t0 + T == NMAIN) or samp

                def up_stage(j):
                    ws = j % 3
                    S.dma(wg[ws][:].rearrange("p c n -> p (c n)"), wup_d[j, :, :], reads=[Bwup_d], writes=[Bwg[ws]])
                    us = j % 2
                    if samp:
                        ubv = ub[us][:, :, 0:40].rearrange("p a (b t) -> p a b t", b=4)
                        CP('pool', ubv[:, :, :, 0:2], hist_s[:, :, j, :, :], [Bhist_s], [Bub[us]])
                    else:
                        CP('pool', ub[us][:, :, 0:2], hist[:, j, :, :], [Bhist[j]], [Bub[us]])
                    for ab in range(2):
                        ubk = (3 + ab) if j % 2 == 0 else (1 + ab)
                        for c in range(8):
                            MM(ps[ubk][:, 0:T], wg[ws][:, c, ab * 128:(ab + 1) * 128], hnT[:, c, 0:T], c == 0, c == 7,
                               [Bwg[ws]] + BhnT[:nt], [Bps[ubk]])
                        if samp:
                            pv4 = ps[ubk][:, 0:32].rearrange("p (b t) -> p b t", b=4)
                            CP('act' if ab == 0 else 'dve', ubv[:, ab, :, 2:10], pv4, [Bps[ubk]], [Bub[us]])
                            CP('dve', ffo_s[:, ab, j, :, :], pv4[:, :, 6:8], [Bps[ubk]], [Bffo])
                        else:
                            CP('act' if ab == 0 else 'dve', ub[us][:, ab, 2:2 + T], ps[ubk][:, 0:T], [Bps[ubk]], [Bub[us]])
                            if last_main:
                                CP('dve', ffo[:, ab, j, :], ps[ubk][:, T - 2:T], [Bps[ubk]], [Bffo])
                    if not samp:
                        CP('pool', hist[:, j, :, :], ub[us][:, :, T:T + 2], [Bub[us]], [Bhist[j]])
                    S.dma(dg[us][:].rearrange("p a k n -> p (a k n)"), dg_d[j, :, :], reads=[Bdg_d], writes=[Bdgm[us]])

                def conv_stage(j):
                    us = j % 2
                    for ab in range(2):
                        for k in range(3):
                            if samp:
                                ubv = ub[us][:, :, 0:40].rearrange("p a (b t) -> p a b t", b=4)
                                rhs = ubv[:, ab, :, k:k + 8]
                                o_ = ps[5 + ab][:, 0:32].rearrange("p (b t) -> p b t", b=4)
                            else:
                                rhs = ub[us][:, ab, k:k + T]
                                o_ = ps[5 + ab][:, 0:T]
                            MM(o_, dg[us][:, ab, k, :], rhs, k == 0, k == 2, [Bdgm[us], Bub[us]], [Bps[5 + ab]])
                    ACT(sa_[us][:, 0:T], ps[5][:, 0:T], AF.Silu, [Bps[5], Bc], [Bsa[us]], bias=fcbT[:, j:j + 1])
                    STT(gT[:, j, 0:T], ps[6][:, 0:T], fcbT[:, NG + j:NG + j + 1], sa_[us][:, 0:T], ALU.add, ALU.mult,
                        [Bps[6], Bc, Bsa[us]], [BgT])

                for j in range(NG):
                    up_stage(j)
                    if j > 0:
                        conv_stage(j - 1)
                conv_stage(NG - 1)

                def tail_gen(i):
                    s = i % 2
                    hn, Bhn, rsb, Brsb = hn2[i], Bhn2[i], rsb2[i], Brsb2[i]
                    ppb, Bppb, ppT, BppT = ppb2[i], Bppb2[i], ppT2[i], BppT2[i]
                    t1, Bt1, h2T, Bh2T, gt, Bgt = t12[i], Bt12[i], h2T2[i], Bh2T2[i], gt2[i], Bgt2[i]
                    b1 = [1 + 2 * (i % 2), 2 + 2 * (i % 2)]
                    b2 = [5, 6] if i % 2 == 0 else [7, 6]
                    r0 = t0 + i * 128
                    psrc = psm_d[0:32, :] if samp else pl[r0:r0 + 128, :]
                    S.dma(ptb[s][:L, :], psrc, writes=[Bptb[s]])
                    for hf in range(2):
                        for j in range(NG):
                            MM(ps[b1[hf]][:L, :], gT[:, j, i * 128:i * 128 + L], w_dn_sb[:, j, hf * 512:(hf + 1) * 512], j == 0, j == NG - 1,
                               [BgT, BwB], [Bps[b1[hf]]])
                        TT('dve', hh[:L, i, hf * 512:(hf + 1) * 512], hh[:L, i, hf * 512:(hf + 1) * 512], ps[b1[hf]][:L, :], ALU.add,
                           [Bhh[i], Bps[b1[hf]]], [Bhh[i]])
                        yield
                    CP('act', ppb[:L, :], ptb[s][:L, :], [Bptb[s]], [Bppb])
                    yield
                    for c in range(2):
                        TR(psT[:, c * 128:c * 128 + L], ppb[:L, c * 128:(c + 1) * 128], identb[:L, :L], [Bppb, Bc], [BpsT])
                    CP('dve', ppT[:, :, 0:L], psT[:, 0:256].rearrange("p (c t) -> p c t", c=2)[:, :, 0:L], [BpsT], [BppT])
                    yield
                    for hf in range(2):
                        for c in range(2):
                            MM(ps[b1[hf]][:L, :], ppT[:, c, 0:L], w_pp_sb[:, c, hf * 512:(hf + 1) * 512], c == 0, c == 1, [BppT, BwB], [Bps[b1[hf]]])
                        ACT(t1[:L, hf * 512:(hf + 1) * 512], ps[b1[hf]][:L, :], AF.Square, [Bps[b1[hf]]], [Bt1, Brsb], accum_out=rsb[:L, 2 + hf:3 + hf])
                        yield
                    TT('dve', rsb[:L, 4:5], rsb[:L, 2:3], rsb[:L, 3:4], ALU.add, [Brsb], [Brsb])
                    TS('dve', rsb[:L, 4:5], rsb[:L, 4:5], 1.0 / D, EPS, ALU.mult, ALU.add, [Brsb], [Brsb])
                    yield
                    TT('pool', rsb[:L, 5:6], rsb[:L, 4:5], mhalf[:L, :], ALU.pow, [Brsb, Bc], [Brsb])
                    yield
                    for hf in range(2):
                        STT(t1[:L, hf * 512:(hf + 1) * 512], ps[b1[hf]][:L, :], rsb[:L, 5:6], gple_b[:L, hf * 512:(hf + 1) * 512], ALU.mult, ALU.mult,
                            [Bps[b1[hf]], Brsb, Bc], [Bt1])
                    yield
                    CP('act', hn[:L, :], hh[:L, i, :], [Bhh[i]], [Bhn])
                    yield
                    for c in range(8):
                        TR(psT[:, c * 128:c * 128 + L], hn[:L, c * 128:(c + 1) * 128], identb[:L, :L], [Bhn, Bc], [BpsT])
                    CP('dve', h2T[:, :, 0:L], psT[:, :].rearrange("p (c t) -> p c t", c=8)[:, :, 0:L], [BpsT], [Bh2T])
                    yield
                    for hf in range(2):
                        for c in range(8):
                            MM(ps[b2[hf]][:L, :], h2T[:, c, 0:L], w_pg_sb[:, c, hf * 512:(hf + 1) * 512], c == 0, c == 7, [Bh2T, BwB], [Bps[b2[hf]]])
                        ACT(gt[:L, hf * 512:(hf + 1) * 512], ps[b2[hf]][:L, :], AF.Tanh, [Bps[b2[hf]]], [Bgt], scale=0.5)
                        yield
                    STT(gt[:L, :], gt[:L, :], 1.0, t1[:L, :], ALU.add, ALU.mult, [Bgt, Bt1], [Bgt])
                    yield
                    STT(hh[:L, i, :], gt[:L, :], 0.5, hh[:L, i, :], ALU.mult, ALU.add, [Bgt, Bhh[i]], [Bhh[i]])
                    yield
                    rms_rstd(hh[:L, i, :], L, hn[:L, :], rsb[:L, 6:7], rsb[:L, 7:8], [Bhh[i]], [Bhn, Brsb], 1.0 / D)
                    yield
                    STT(gt[:L, :], hh[:L, i, :], rsb[:L, 7:8], gfin_b[:L, :], ALU.mult, ALU.mult, [Bhh[i], Brsb, Bc], [Bgt])
                    ydst = ys_d[0:32, :] if samp else y_loc[r0:r0 + 128, :]
                    S.dma(ydst, gt[:L, :], reads=[Bgt])
                    yield

                lockstep([tail_gen(i) for i in range(nt)])
                if last_main:
                    nr = 8 if samp else 2
                    for rd in range(11):
                        for q4 in range(4):
                            ch = rd * 4 + q4
                            src = ffo_s[:, ch // NG, ch % NG, :, :].rearrange("p b t -> p (b t)") if samp else ffo[:, ch // NG, ch % NG, :]
                            TR(ps[7][0:nr, q4 * 128:(q4 + 1) * 128], src, identf[:, :], [Bffo, Bc], [Bps[7]])
                        CP('dve', ffs[0:nr, :], ps[7][0:nr, :], [Bps[7]], [Bffs])
                        fdst = ffn_s_d if samp else ffn_loc
                        S.dma(fdst[:, rd * 512:(rd + 1) * 512], ffs[0:nr, :], reads=[Bffs])

            for t0 in range(0, NMAIN, TB):
                b_supertile(t0, TB)
            b_supertile(NMAIN, NHALO)
            if not skip_a1:
                for rd in range(11):
                    S.dma(hs8[:, :], sffn_d[:, rd * 512:(rd + 1) * 512], writes=[Bhs8])
                    for q4 in range(4):
                        TR(ps[7][:, q4 * 8:(q4 + 1) * 8], hs8[0:8, q4 * 128:(q4 + 1) * 128], identf[0:8, 0:8], [Bhs8, Bc], [Bps[7]])
                    for q4 in range(4):
                        ch = rd * 4 + q4
                        CP('dve', hist_s[:, ch // NG, ch % NG, :, :], ps[7][:, q4 * 8:(q4 + 1) * 8].rearrange("p (b t) -> p b t", b=4),
                           [Bps[7]], [Bhist_s])
                b_supertile(0, 32, samp=True)

        S.finish()
    return nc


_NC_CACHE = {}


def _get_nc():
    if 'nc' not in _NC_CACHE:
        import os
        _NC_CACHE['nc'] = build_nc(float(os.environ.get('KSTAGE', '99')))
    return _NC_CACHE['nc']


def _prep_inputs(inputs):
    f = lambda a: np.ascontiguousarray(np.asarray(a, dtype=np.float32))
    x = f(inputs["x_prompt"]); p = f(inputs["p_prompt"])[0]
    consts = host_consts()
    shared = dict(consts)
    for k in ["rel_bias"]:
        shared[k] = f(inputs[k])
    for k in ["g_mix", "w_in", "conv_w", "conv_b", "dt_bias", "a_log", "d_skip", "g_ssm", "w_out", "g_ffn", "w_up",
              "ffn_conv_w", "ffn_conv_b", "w_down", "w_ple_proj", "g_ple", "w_ple_gate"]:
        shared[k] = f(inputs[k])[0]
    shared["g_final"] = f(inputs["g_final"])
    xs = f(inputs["x_sample"]); psm = f(inputs["p_sample"])[0]
    ck = f(inputs["cache_k"])[0]; cv = f(inputs["cache_v"])[0]
    sssm = f(inputs["state_ssm"])[0]; sconv = f(inputs["state_conv"])[0]; sffn = f(inputs["state_ffn_conv"])[0]
    in_maps = []
    for core in range(8):
        b, half = core // 2, core % 2
        if half == 0:
            xl = np.concatenate([np.zeros((NPRE, D), np.float32), x[b, 0:NMAIN + NHALO]], axis=0)
            pl = p[b, 0:NQ]
        else:
            xl = np.concatenate([x[b], np.zeros((NHALO, D), np.float32)], axis=0)
            pl = np.concatenate([p[b, NPRE:], np.zeros((NHALO, 256), np.float32)], axis=0)
        m = dict(shared)
        m["xl"] = np.ascontiguousarray(xl)
        m["pl"] = np.ascontiguousarray(pl)
        m["pv"] = np.full((128, 1), float(half), np.float32)
        sq = slice(4 * core, 4 * core + 4)
        m["xs"] = np.ascontiguousarray(xs[sq].reshape(32, D))
        m["psm"] = np.ascontiguousarray(psm[sq].reshape(32, 256))
        m["ck"] = np.ascontiguousarray(ck[sq].reshape(4, 2048, 512))
        m["cv"] = np.ascontiguousarray(cv[sq].reshape(4, 2048, 512))
        m["sssm"] = np.ascontiguousarray(sssm[sq].reshape(4, 512, 128))
        m["sconv"] = np.ascontiguousarray(sconv[sq].reshape(12, 1024))
        m["sffn"] = np.ascontiguousarray(sffn[sq].reshape(8, 2 * DFF))
        in_maps.append(m)
    return in_maps


def kernel(**inputs):
    in_maps = _prep_inputs(inputs)
    nc = _get_nc()
    res = run_bass_kernel_spmd(nc, in_maps, core_ids=list(range(8)))
    R = res.results
    y_prompt = np.zeros((4, 4096, D), np.float32)
    k_prompt = np.zeros((1, 4, 2048, 8, 64), np.float32)
    v_prompt = np.zeros((1, 4, 2048, 8, 64), np.float32)
    ssm_prompt = np.zeros((1, 4, 8, 64, 128), np.float32)
    conv_prompt = np.zeros((1, 4, 3, 1024), np.float32)
    ffn_prompt = np.zeros((1, 4, 2, 2 * DFF), np.float32)
    for b in range(4):
        A, Bc = R[2 * b], R[2 * b + 1]
        y_prompt[b, 0:NMAIN + 2] = A["y_loc"][0:NMAIN + 2]
        y_prompt[b, NMAIN + 2:] = Bc["y_loc"][2:NMAIN]
        k_prompt[0, b] = Bc["k_loc"].reshape(2048, 8, 64)
        v_prompt[0, b] = Bc["v_loc"].reshape(2048, 8, 64)
        ssm_prompt[0, b] = Bc["ssm_loc"].reshape(8, 64, 128)
        conv_prompt[0, b] = Bc["conv_loc"]
        ffn_prompt[0, b] = Bc["ffn_loc"]
    y_sample = np.zeros((32, 8, D), np.float32)
    k_sample = np.zeros((1, 32, 8, 8, 64), np.float32)
    v_sample = np.zeros((1, 32, 8, 8, 64), np.float32)
    ssm_sample = np.zeros((1, 32, 8, 64, 128), np.float32)
    conv_sample = np.zeros((1, 32, 3, 1024), np.float32)
    ffn_sample = np.zeros((1, 32, 2, 2 * DFF), np.float32)
    for core in range(8):
        sq = slice(4 * core, 4 * core + 4)
        r = R[core]
        y_sample[sq] = r["ys"].reshape(4, 8, D)
        k_sample[0, sq] = r["ks"].reshape(4, 8, 8, 64)
        v_sample[0, sq] = r["vs"].reshape(4, 8, 8, 64)
        ssm_sample[0, sq] = r["ssm_s"].reshape(4, 8, 64, 128)
        conv_sample[0, sq] = r["conv_s"].reshape(4, 3, 1024)
        ffn_sample[0, sq] = r["ffn_s"].reshape(4, 2, 2 * DFF)
    return (y_prompt, y_sample, k_prompt, v_prompt, k_sample, v_sample, ssm_prompt, ssm_sample,
            conv_prompt, conv_sample, ffn_prompt, ffn_sample)
```

```python
import numpy as np
from contextlib import ExitStack
import concourse.bass as bass
import concourse.mybir as mybir
from concourse.bass_utils import run_bass_kernel_spmd

F32 = mybir.dt.float32
BF16 = mybir.dt.bfloat16
AF = mybir.ActivationFunctionType
ALU = mybir.AluOpType

NEG = -30000.0
D = 1024
NPRE = 2048
NMAIN = 2048
NHALO = 128
NLOC = NPRE + NMAIN + NHALO
NQ = NMAIN + NHALO
IN_DIM = 3080
DFF = 2816
NG = 22
EPS = 1e-6
BRANCH_D = (1, 4, 16)


class Buf:
    def __init__(self, name="b", excl=False):
        self.name = name
        self.w = None
        self.r = {}
        self.excl = excl


class Sched:
    NDQ = 28
    NSP = 20

    def __init__(self, nc, stack):
        self.nc = nc
        self.h = {'pe': nc.tensor, 'act': nc.scalar, 'dve': nc.vector, 'pool': nc.gpsimd, 'sp': nc.sync}
        self.sem = {k: stack.enter_context(nc.semaphore(k + "_sem")) for k in self.h}
        self.cnt = {k: 0 for k in self.h}
        self.seen = {k: {} for k in self.h}
        for i in range(self.NDQ):
            self.sem[('dq', i)] = stack.enter_context(nc.semaphore(f"dq{i}"))
        self.dcnt = [0] * self.NDQ
        self.rr = 0
        self.rr_pool = 0
        self.nwait = 0
        self.ndma = 0

    def _deps(self, reads, writes):
        deps = {}
        for b in reads:
            if b.w is not None:
                k, v = b.w
                if deps.get(k, 0) < v:
                    deps[k] = v
            if b.excl:
                for k, v in b.r.items():
                    if deps.get(k, 0) < v:
                        deps[k] = v
        for b in writes:
            if b.w is not None:
                k, v = b.w
                if deps.get(k, 0) < v:
                    deps[k] = v
            for k, v in b.r.items():
                if deps.get(k, 0) < v:
                    deps[k] = v
        return deps

    def _wait(self, eng, deps):
        h = self.h[eng]
        seen = self.seen[eng]
        for k, v in deps.items():
            if k == eng and eng in ('pe', 'sp'):
                continue
            if seen.get(k, 0) >= v:
                continue
            h.wait_ge(self.sem[k], v)
            self.nwait += 1
            seen[k] = v

    def _mark(self, ev, reads, writes):
        k, v = ev
        for b in reads:
            if b.r.get(k, 0) < v:
                b.r[k] = v
        for b in writes:
            b.w = ev
            b.r = {}

    def op(self, eng, fn, reads=(), writes=()):
        self._wait(eng, self._deps(reads, writes))
        ins = fn(self.h[eng])
        self.cnt[eng] += 1
        ins.then_inc(self.sem[eng], 1)
        self._mark((eng, self.cnt[eng]), reads, writes)

    def dma(self, out, in_, reads=(), writes=(), eng='sp', **kw):
        if eng == 'pool':
            i = self.NSP + self.rr_pool
            self.rr_pool = (self.rr_pool + 1) % (self.NDQ - self.NSP)
        else:
            i = self.rr
            self.rr = (i + 1) % self.NSP
        deps = self._deps(reads, writes)
        if self.dcnt[i] > 0:
            k = ('dq', i)
            deps[k] = max(deps.get(k, 0), 16 * self.dcnt[i])
        self._wait(eng, deps)
        ins = self.h[eng].dma_start(out=out, in_=in_, **kw)
        self.dcnt[i] += 1
        self.ndma += 1
        ins.then_inc(self.sem[('dq', i)], 16)
        self._mark((('dq', i), 16 * self.dcnt[i]), reads, writes)

    def pe_mode(self, mode):
        if getattr(self, 'cur_mode', None) is not None and self.cur_mode != mode and self.cnt['pe'] > 0:
            self.h['pe'].wait_ge(self.sem['pe'], self.cnt['pe'])
            self.h['pe'].drain()
            self.nwait += 1
            self.ndrain = getattr(self, 'ndrain', 0) + 1
        self.cur_mode = mode

    def barrier(self):
        for eng in self.h:
            deps = {k: self.cnt[k] for k in self.h if k != eng and self.cnt[k] > 0}
            for i in range(self.NDQ):
                if self.dcnt[i] > 0:
                    deps[('dq', i)] = 16 * self.dcnt[i]
            self._wait(eng, deps)

    def finish(self):
        deps = {('dq', i): 16 * self.dcnt[i] for i in range(self.NDQ) if self.dcnt[i] > 0}
        for k in self.h:
            if k != 'sp' and self.cnt[k] > 0:
                deps[k] = self.cnt[k]
        self._wait('sp', deps)


def rel_bucket_np(dist):
    dist = np.asarray(dist, np.int64)
    d = np.maximum(dist, 1).astype(np.float32)
    large = 16 + (np.log(d / np.float32(16)) / np.float32(np.log(2048 / 16)) * np.float32(16)).astype(np.int32)
    large = np.minimum(large, 31)
    return np.where(dist < 16, dist, large).astype(np.int64)


def host_consts():
    ident = np.eye(128, dtype=np.float32)
    triu = np.triu(np.ones((128, 128), np.float32))
    negtri = np.where(triu > 0, 0.0, NEG).astype(np.float32)
    oh = np.zeros((32, 3 * 129), np.float32)
    for bi, d in enumerate(BRANCH_D):
        bk = rel_bucket_np(np.arange(129) * d)
        for j in range(129):
            oh[bk[j], bi * 129 + j] = 1.0
    negdiag = np.full((128, 2), NEG, np.float32)
    negdiag[0, 0] = 0.0
    negdiag[1, 1] = 0.0
    cm = np.full((32, 3, 32), NEG, np.float32)
    for kg in range(32):
        for qg in range(32):
            if kg // 8 != qg // 8:
                continue
            dist = qg % 8 - kg % 8
            if dist < 0:
                continue
            cm[kg, 0, qg] = 0.0
            if dist in (0, 4):
                cm[kg, 1, qg] = 0.0
            if dist == 0:
                cm[kg, 2, qg] = 0.0
    return dict(ident=ident, triu=triu, negtri=negtri, oh=oh, negdiag=negdiag, cm=cm)


def build_nc(stage=99):
    global _LAST_S
    import os
    skip_p = os.environ.get('KSKIPP', '0') == '1'
    skip_a1 = os.environ.get('KNOSAMP', '0') == '1'
    ks1 = float(os.environ.get('KS1', '99'))
    nc = bass.Bass("TRN2", target_bir_lowering=False)

    def din(name, shape):
        return nc.dram_tensor(name, list(shape), F32, kind="ExternalInput").ap()

    def dout(name, shape):
        return nc.dram_tensor(name, list(shape), F32, kind="ExternalOutput").ap()

    xl = din("xl", [NLOC, D])
    pl = din("pl", [NQ, 256])
    pv = din("pv", [128, 1])
    rel_bias = din("rel_bias", [32, 8])
    g_mix = din("g_mix", [D])
    w_in = din("w_in", [D, IN_DIM])
    conv_w = din("conv_w", [4, 1024])
    conv_b = din("conv_b", [1024])
    dt_bias = din("dt_bias", [8])
    a_log = din("a_log", [8])
    d_skip = din("d_skip", [8])
    g_ssm = din("g_ssm", [512])
    w_out = din("w_out", [D, D])
    g_ffn = din("g_ffn", [D])
    w_up = din("w_up", [D, 2 * DFF])
    ffn_conv_w = din("ffn_conv_w", [3, 2 * DFF])
    ffn_conv_b = din("ffn_conv_b", [2 * DFF])
    w_down = din("w_down", [DFF, D])
    w_ple_proj = din("w_ple_proj", [256, D])
    g_ple = din("g_ple", [D])
    w_ple_gate = din("w_ple_gate", [D, D])
    g_final = din("g_final", [D])
    c_ident = din("ident", [128, 128])
    c_triu = din("triu", [128, 128])
    c_negtri = din("negtri", [128, 128])
    c_oh = din("oh", [32, 387])
    c_negdiag = din("negdiag", [128, 2])

    xs_d = din("xs", [32, D])
    psm_d = din("psm", [32, 256])
    ck_d = din("ck", [4, 2048, 512])
    cv_d = din("cv", [4, 2048, 512])
    sssm_d = din("sssm", [4, 512, 128])
    sconv_d = din("sconv", [12, 1024])
    sffn_d = din("sffn", [8, 2 * DFF])
    c_cm = din("cm", [32, 3, 32])
    ys_d = dout("ys", [32, D])
    ks_d = dout("ks", [32, 512])
    vs_d = dout("vs", [32, 512])
    ssm_s_d = dout("ssm_s", [4, 512, 128])
    conv_s_d = dout("conv_s", [12, 1024])
    ffn_s_d = dout("ffn_s", [8, 2 * DFF])

    y_loc = dout("y_loc", [NQ, D])
    k_loc = dout("k_loc", [NMAIN, 512])
    v_loc = dout("v_loc", [NMAIN, 512])
    ssm_loc = dout("ssm_loc", [512, 128])
    conv_loc = dout("conv_loc", [3, 1024])
    ffn_loc = dout("ffn_loc", [2, 2 * DFF])

    qT_d = nc.dram_tensor("qT_d", [4, 128, NQ], BF16, kind="Internal").ap()
    kT_d = nc.dram_tensor("kT_d", [4, 128, NLOC], BF16, kind="Internal").ap()
    vT_d = nc.dram_tensor("vT_d", [4, 128, NLOC], BF16, kind="Internal").ap()
    bm_d = nc.dram_tensor("bm_d", [8, 3, 384], BF16, kind="Internal").ap()
    wup_d = nc.dram_tensor("wup_d", [NG, 128, 8 * 256], BF16, kind="Internal").ap()
    wB_d = nc.dram_tensor("wB_d", [40, 128, D], BF16, kind="Internal").ap()
    dg_d = nc.dram_tensor("dg_d", [NG, 128, 768], BF16, kind="Internal").ap()

    with ExitStack() as top:
        S = Sched(nc, top)
        _LAST_S = S

        def sbt(stack, name, shape, dt):
            return stack.enter_context(nc.sbuf_tensor("s_" + name, list(shape), dt))

        def _ru(x):
            return 32 if x <= 32 else (64 if x <= 64 else 128)

        def _mode(lhsT):
            shp = lhsT.shape
            m = 1
            for d_ in shp[1:]:
                m *= int(d_)
            return (_ru(int(shp[0])), _ru(m), lhsT.dtype == F32)

        def MM(out, lhsT, rhs, start, stop, r, w):
            md = _mode(lhsT)
            S.pe_mode(md if md[:2] != (128, 128) else (128, 128))
            S.op('pe', lambda e: e.matmul(out, lhsT=lhsT, rhs=rhs, start=start, stop=stop), r, w)

        def TR(out, in_, ident, r, w):
            md = _mode(in_)
            S.pe_mode((md + ('T',)) if md[:2] != (128, 128) else (128, 128))
            S.op('pe', lambda e: e.transpose(out=out, in_=in_, identity=ident), r, w)

        def ACT(out, in_, func, r, w, **kw):
            S.op('act', lambda e: e.activation(out=out, in_=in_, func=func, **kw), r, w)

        def CP(eng, out, in_, r, w):
            if eng == 'act':
                S.op('act', lambda e: e.activation(out=out, in_=in_, func=AF.Copy), r, w)
            else:
                S.op(eng, lambda e: e.tensor_copy(out=out, in_=in_), r, w)

        def TT(eng, out, in0, in1, op, r, w):
            S.op(eng, lambda e: e.tensor_tensor(out=out, in0=in0, in1=in1, op=op), r, w)

        def TS(eng, out, in0, s1, s2, op0, op1, r, w):
            if op1 is None:
                S.op(eng, lambda e: e.tensor_scalar(out=out, in0=in0, scalar1=s1, scalar2=None, op0=op0), r, w)
            else:
                S.op(eng, lambda e: e.tensor_scalar(out=out, in0=in0, scalar1=s1, scalar2=s2, op0=op0, op1=op1), r, w)

        def STT(out, in0, scalar, in1, op0, op1, r, w):
            S.op('dve', lambda e: e.scalar_tensor_tensor(out=out, in0=in0, scalar=scalar, in1=in1, op0=op0, op1=op1), r, w)

        def MEMSET(eng, ap, val, w):
            S.op(eng, lambda e: e.memset(ap, val), (), w)

        psT = top.enter_context(nc.psum_tensor("psT", [128, 1024], BF16))
        BpsT = Buf("psT", excl=True)
        ps = [None] + [top.enter_context(nc.psum_tensor(f"ps{i}", [128, 512], F32)) for i in range(1, 8)]
        Bps = [None] + [Buf(f"ps{i}", excl=True) for i in range(1, 8)]

        identf = sbt(top, "identf", [128, 128], F32)
        identb = sbt(top, "identb", [128, 128], BF16)
        onec = sbt(top, "onec", [128, 1], F32)
        epsc = sbt(top, "epsc", [128, 1], F32)
        mhalf = sbt(top, "mhalf", [128, 1], F32)
        pvt = sbt(top, "pvt", [128, 1], F32)
        gmixT = sbt(top, "gmixT", [128, 8], F32)
        gffnT = sbt(top, "gffnT", [128, 8], F32)
        Bc = Buf("consts")

        top.enter_context(nc.Block())

        S.dma(identf[:], c_ident[:, :], writes=[Bc])
        S.dma(pvt[:], pv[:, :], writes=[Bc])
        S.dma(gmixT[:], g_mix.rearrange("(c p) -> p c", p=128), writes=[Bc], allow_slow_non_contiguous=True)
        S.dma(gffnT[:], g_ffn.rearrange("(c p) -> p c", p=128), writes=[Bc], allow_slow_non_contiguous=True)
        MEMSET('dve', onec[:], 1.0, [Bc])
        MEMSET('dve', epsc[:], EPS, [Bc])
        MEMSET('dve', mhalf[:], -0.5, [Bc])
        CP('dve', identb[:], identf[:], [Bc], [Bc])

        def rms_rstd(src_ap, L, junk_ap, ss_ap, rstd_ap, r, w, inv_n):
            ACT(junk_ap, src_ap, AF.Square, r, w, accum_out=ss_ap)
            TS('dve', ss_ap, ss_ap, inv_n, EPS, ALU.mult, ALU.add, w, w)
            TT('pool', rstd_ap, ss_ap, mhalf[:L, :], ALU.pow, w + [Bc], w)

        mixT = sbt(top, "mixT", [128, 8, NQ], BF16)
        BmixA = [Buf(f"mixA{p}") for p in range(4)]
        BmixS = Buf("mixS")
        smixT = sbt(top, "smixT", [128, 8, 32], BF16)
        BsmixA = [Buf() for _ in range(4)]
        BsmixS = Buf()
        sQT = sbt(top, "sQT", [128, 4, 2, 32], BF16)
        sKT = sbt(top, "sKT", [128, 4, 32], BF16)
        sVn = sbt(top, "sVn", [32, 8, 66], BF16)
        Bsq = Buf()
        MEMSET('pool', sQT[:].rearrange("p a h q -> p (a h q)"), 0.0, [Bsq])
        MEMSET('pool', sVn[:].rearrange("p h e -> p (h e)"), 1.0, [Bsq])

        sA = ExitStack()
        Bm = sbt(sA, "Bm", [128, 8, 3, 256], BF16)
        Bdg = sbt(sA, "Bdg", [128, 8, 2], BF16)
        negd = sbt(sA, "negd", [128, 2], F32)
        sA0 = ExitStack()
        rb = sbt(sA0, "rb", [32, 8], F32)
        oht = sbt(sA0, "oht", [32, 387], F32)
        gfull = sbt(sA0, "gfull", [8, 3, 384], F32)
        Bt = Buf()
        Bbmd = Buf()
        BBm = Buf()
        S.dma(rb[:], rel_bias[:, :], writes=[Bt])
        S.dma(oht[:], c_oh[:, :], writes=[Bt])
        S.dma(negd[:], c_negdiag[:, :], writes=[Bc])
        MEMSET('dve', gfull[:], NEG, [Bt])
        MM(ps[1][0:8, 0:387], rb[:, :], oht[:, :], True, True, [Bt], [Bps[1]])
        CP('dve', gfull[:, :, 127:256], ps[1][0:8, 0:387].rearrange("p (b j) -> p b j", b=3), [Bps[1], Bt], [Bt])
        gfull_b = sbt(sA0, "gfull_b", [8, 3, 384], BF16)
        CP('dve', gfull_b[:], gfull[:], [Bt], [Bt])
        S.dma(bm_d[:, :, :], gfull_b[:], reads=[Bt], writes=[Bbmd])

        S.barrier()
        sA0.close()

        def bm_dma(c):
            S.dma(Bm[c:c + 1, :, :, :], bm_d[:, :, 127 - c:127 - c + 256].unsqueeze(0), reads=[Bbmd], writes=[BBm])

        with ExitStack() as sa:
            w_in_sb = sbt(sa, "w_in_sb", [128, 8, IN_DIM], BF16)
            Bwin = Buf("w_in")
            for c in range(8):
                S.dma(w_in_sb[:, c, :], w_in[c * 128:(c + 1) * 128, :], writes=[Bwin], eng='pool')
            triu = sbt(sa, "triu", [128, 128], F32)
            ones = sbt(sa, "ones", [128, 128], F32)
            negtri_f = sbt(sa, "negtri_f", [128, 128], F32)
            negtri8 = sbt(sa, "negtri8", [128, 8, 128], BF16)
            cwT = sbt(sa, "cwT", [128, 8, 4], F32)
            cbT = sbt(sa, "cbT", [128, 8], F32)
            dtb = sbt(sa, "dtb", [128, 8], F32)
            A_b = sbt(sa, "A_b", [128, 8], F32)
            dskip_b = sbt(sa, "dskip_b", [128, 8], F32)
            gssm_b = sbt(sa, "gssm_b", [128, 512], F32)
            S.dma(triu[:], c_triu[:, :], writes=[Bc])
            S.dma(negtri_f[:], c_negtri[:, :], writes=[Bc])
            for k in range(4):
                S.dma(cwT[:, :, k], conv_w[k, :].rearrange("(c p) -> p c", p=128), writes=[Bc], allow_slow_non_contiguous=True)
            S.dma(cbT[:], conv_b.rearrange("(c p) -> p c", p=128), writes=[Bc], allow_slow_non_contiguous=True)
            S.dma(dtb[:], dt_bias.partition_broadcast(128), writes=[Bc])
            S.dma(A_b[:], a_log.partition_broadcast(128), writes=[Bc])
            S.dma(dskip_b[:], d_skip.partition_broadcast(128), writes=[Bc])
            S.dma(gssm_b[:], g_ssm.partition_broadcast(128), writes=[Bc])
            MEMSET('pool', ones[:], 1.0, [Bc])
            CP('dve', negtri8[:], negtri_f[:].unsqueeze(1).to_broadcast([128, 8, 128]), [Bc], [Bc])
            TS('dve', cwT[:], cwT[:], 0.5, None, ALU.mult, None, [Bc], [Bc])
            TS('dve', cbT[:], cbT[:], 0.5, None, ALU.mult, None, [Bc], [Bc])
            ACT(A_b[:], A_b[:], AF.Exp, [Bc], [Bc])
            TS('dve', A_b[:], A_b[:], -1.0, None, ALU.mult, None, [Bc], [Bc])

            xt = [sbt(sa, f"xt{i}", [128, D], F32) for i in range(2)]
            Bxt = [Buf() for _ in range(2)]
            xn = [sbt(sa, f"xn{i}", [128, D], BF16) for i in range(2)]
            Bxn = [Buf() for _ in range(2)]
            rs = [sbt(sa, f"rs{i}", [128, 2], F32) for i in range(2)]
            Brs = [Buf() for _ in range(2)]
            xnT = sbt(sa, "xnT", [128, 8, 512], BF16)
            BxnT = [Buf() for _ in range(4)]
            stg = [sbt(sa, f"stg{i}", [128, 512], BF16) for i in range(3)]
            Bstg = [Buf() for _ in range(3)]
            cbuf = sbt(sa, "cbuf", [128, 8, 515], F32)
            Bcb = [Buf() for _ in range(8)]
            acc = [sbt(sa, f"acc{i}", [128, 512], F32) for i in range(2)]
            Bacc = [Buf() for _ in range(2)]
            th0 = sbt(sa, "th0", [128, 512], F32)
            th = [th0, th0]
            Bth0 = Buf()
            Bth = [Bth0, Bth0]
            xbcT = sbt(sa, "xbcT", [128, 8, 512], BF16)
            Bxbc = [Buf() for _ in range(8)]
            kvo = [sbt(sa, f"kvo{i}", [128, 512], F32) for i in range(2)]
            Bkvo = [Buf() for _ in range(2)]
            dtall = sbt(sa, "dtall", [128, 4, 8], F32)
            Bdt = Buf()
            xsB2 = [sbt(sa, f"xsB{i}", [128, 768], BF16) for i in range(2)]
            BxsB2 = [Buf() for _ in range(2)]
            sm = sbt(sa, "sm", [128, 4, 4, 8], F32)
            Bsm = Buf()
            avall = sbt(sa, "avall", [128, 4, 8], F32)
            Bav = Buf()
            Rm2 = [sbt(sa, f"Rm{i}", [128, 8, 128], F32) for i in range(2)]
            BRm2 = [Buf() for _ in range(2)]
            Lm2 = [sbt(sa, f"Lm{i}", [128, 8, 128], F32) for i in range(2)]
            BLm2 = [Buf() for _ in range(2)]
            Mb2 = [sbt(sa, f"Mb{i}", [128, 8, 128], BF16) for i in range(2)]
            BMb2 = [Buf() for _ in range(2)]
            xdt2 = [sbt(sa, f"xdt{i}", [128, 8, 64], BF16) for i in range(2)]
            Bxdt2 = [Buf() for _ in range(2)]
            xde2 = [sbt(sa, f"xde{i}", [128, 8, 64], BF16) for i in range(2)]
            Bxde2 = [Buf() for _ in range(2)]
            xsk2 = [sbt(sa, f"xsk{i}", [128, 8, 64], BF16) for i in range(2)]
            Bxsk2 = [Buf() for _ in range(2)]
            stf = sbt(sa, "stf", [128, 8, 64], F32)
            Bstf = Buf()
            stb = sbt(sa, "stb", [128, 512], BF16)
            Bstb = Buf()
            ytl2 = [sbt(sa, f"ytl{i}", [128, 512], F32) for i in range(2)]
            Bytl2 = [Buf() for _ in range(2)]
            zs = sbt(sa, "zs", [128, 512], F32)
            Bzs = Buf()
            ss2 = sbt(sa, "ss2", [128, 4], F32)
            Bss2 = Buf()
            ynb = sbt(sa, "ynb", [128, 512], BF16)
            Bynb = Buf()
            sto = sbt(sa, "sto", [128, 4, 128], F32)
            Bsto = Buf()

            MEMSET('dve', cbuf[:, :, 0:3], 0.0, Bcb)
            MEMSET('dve', stf[:], 0.0, [Bstf])
            MEMSET('dve', stb[:], 0.0, [Bstb])

            tile_ctr = [0]
            ev_ctr = [0]

            def norm_transpose_tile(src_rows_ap, L, dst_cols, BdstT):
                s = tile_ctr[0] % 2
                tile_ctr[0] += 1
                S.dma(xt[s][:L, :], src_rows_ap, writes=[Bxt[s]])
                rms_rstd(xt[s][:L, :], L, xn[s][:L, :], rs[s][:L, 0:1], rs[s][:L, 1:2], [Bxt[s]], [Bxn[s], Brs[s]], 1.0 / D)
                ACT(xn[s][:L, :], xt[s][:L, :], AF.Copy, [Bxt[s], Brs[s]], [Bxn[s]], scale=rs[s][:L, 1:2])
                for c in range(8):
                    TR(psT[:, c * 128:c * 128 + L], xn[s][:L, c * 128:(c + 1) * 128], identb[:L, :L], [Bxn[s], Bc], [BpsT])
                pv3 = psT[:, :].rearrange("p (c t) -> p c t", c=8)[:, :, 0:L]
                TT('dve', xnT[:, :, dst_cols], pv3, gmixT[:].unsqueeze(2).to_broadcast([128, 8, L]), ALU.mult,
                   [BpsT, Bc], [BdstT])

            def ssd_prep(L, nt, dv_ap, main):
                TT('dve', avall[:L, 0:nt, :], dv_ap, A_b[:L, :].unsqueeze(1).to_broadcast([L, nt, 8]), ALU.mult, [Bdt, Bc], [Bav])
                rhs = avall[:L, 0:nt, :].rearrange("p i h -> p (i h)")
                MM(ps[5][:L, 32:32 + nt * 8], triu[:L, :L], rhs, True, True, [Bc, Bav], [Bps[5]])
                MM(ps[5][:, 64:64 + nt * 8], ones[:L, :], rhs, True, True, [Bc, Bav], [Bps[5]])
                pac = ps[5][:, 32:32 + nt * 8].rearrange("p (i h) -> p i h", h=8)
                pal = ps[5][:, 64:64 + nt * 8].rearrange("p (i h) -> p i h", h=8)
                TS('dve', sm[:L, 0, 0:nt, :], pac[:L], -1.0, None, ALU.mult, None, [Bps[5]], [Bsm])
                TT('dve', sm[:L, 2, 0:nt, :], pal[:L], sm[:L, 0, 0:nt, :], ALU.add, [Bps[5], Bsm], [Bsm])
                ACT(sm[:L, 2, 0:nt, :], sm[:L, 2, 0:nt, :], AF.Exp, [Bsm], [Bsm])
                TT('dve', sm[:L, 2, 0:nt, :], sm[:L, 2, 0:nt, :], dv_ap, ALU.mult, [Bsm, Bdt], [Bsm])
                ACT(sm[:, 3, 0:nt, :], pal, AF.Exp, [Bps[5]], [Bsm])
                if main:
                    ACT(sm[:L, 1, 0:nt, :], pac[:L], AF.Exp, [Bps[5]], [Bsm])

            def ssd_gen(L, cs, BxT, kind, dt_ap, mix_cols, ti, par, mixt=None, Bmix=None):
                main = (kind == 'main')
                xsB, BxsB = xsB2[par], BxsB2[par]
                Rm, BRm, Lm, BLm, Mb, BMb = Rm2[par], BRm2[par], Lm2[par], BLm2[par], Mb2[par], BMb2[par]
                xdt, Bxdt, xde, Bxde, xsk, Bxsk = xdt2[par], Bxdt2[par], xde2[par], Bxde2[par], xsk2[par], Bxsk2[par]
                ytl, Bytl = ytl2[par], Bytl2[par]
                for j in range(6):
                    TR(psT[:L, j * 128:(j + 1) * 128], xbcT[:, j, cs], identb[:, :], [Bxbc[j], Bc], [BpsT])
                CP('act', xsB[:L, :], psT[:L, 0:768], [BpsT], [BxsB])
                yield
                av = avall[:, ti, :]
                TT('dve', xde[:L], xsB[:L, 0:512].rearrange("p (h e) -> p h e", h=8),
                   sm[:L, 2, ti, :].unsqueeze(2).to_broadcast([L, 8, 64]), ALU.mult, [BxsB, Bsm], [Bxde])
                if main:
                    TT('dve', xdt[:L], xsB[:L, 0:512].rearrange("p (h e) -> p h e", h=8),
                       dt_ap.unsqueeze(2).to_broadcast([L, 8, 64]), ALU.mult, [BxsB, Bdt], [Bxdt])
                    TT('pool', xsk[:L], xsB[:L, 0:512].rearrange("p (h e) -> p h e", h=8),
                       dskip_b[:L, :].unsqueeze(2).to_broadcast([L, 8, 64]), ALU.mult, [BxsB, Bc], [Bxsk])
                    for g in range(2):
                        MM(ps[2][:L, g * 256:(g + 1) * 256], xbcT[:, 6 + g, cs], stb[:, g * 256:(g + 1) * 256], g == 0, g == 1,
                           [Bxbc[6 + g], Bstb], [Bps[2]])
                yield
                if main:
                    TT('dve', ytl[:L, :].rearrange("p (h e) -> p h e", h=8), ps[2][:L, :].rearrange("p (h e) -> p h e", h=8),
                       sm[:L, 1, ti, :].unsqueeze(2).to_broadcast([L, 8, 64]), ALU.mult, [Bps[2], Bsm], [Bytl])
                for g in range(2):
                    MM(ps[3][:, g * 256:(g + 1) * 256], xsB[:L, 512 + g * 128:512 + (g + 1) * 128],
                       xde[:L, g * 4:(g + 1) * 4, :].rearrange("p h e -> p (h e)"), g == 0, g == 1, [BxsB, Bxde], [Bps[3]])
                TT('dve', stf[:], stf[:], sm[:, 3, ti, :].unsqueeze(2).to_broadcast([128, 8, 64]), ALU.mult, [Bstf, Bsm], [Bstf])
                yield
                TT('dve', stf[:].rearrange("p h e -> p (h e)"), stf[:].rearrange("p h e -> p (h e)"), ps[3][:, :], ALU.add,
                   [Bstf, Bps[3]], [Bstf])
                CP('act', stb[:, :], stf[:].rearrange("p h e -> p (h e)"), [Bstf], [Bstb])
                yield 'S'
                if not main:
                    return
                TT('dve', Rm[:L, :, :L], triu[:L, :L].unsqueeze(1).to_broadcast([L, 8, L]),
                   av[:L, :].unsqueeze(2).to_broadcast([L, 8, L]), ALU.mult, [Bc, Bav], [BRm])
                yield
                for hb in range(2):
                    bank = ps[6 + hb]
                    o = bank[:, :].rearrange("p (h t) -> p h t", h=4)[:L, :, :L]
                    if L == 128:
                        MM(o, ones[:L, :L], Rm[:L, hb * 4:(hb + 1) * 4, :L], True, False, [Bc, BRm], [Bps[6 + hb]])
                        MM(o, identb[:L, :L], negtri8[:L, hb * 4:(hb + 1) * 4, :L], False, True, [Bc], [Bps[6 + hb]])
                    else:
                        for h4 in range(4):
                            MM(o[:, h4, :], ones[:L, :L], Rm[:L, hb * 4 + h4, :L], h4 == 0, False, [Bc, BRm], [Bps[6 + hb]])
                        for h4 in range(4):
                            MM(o[:, h4, :], identb[:L, :L], negtri8[:L, hb * 4 + h4, :L], False, h4 == 3, [Bc], [Bps[6 + hb]])
                for g in range(2):
                    MM(ps[5][:L, 256 + g * 128:256 + g * 128 + L], xbcT[:, 4 + g, cs], xbcT[:, 6 + g, cs], True, True,
                       [Bxbc[4 + g], Bxbc[6 + g]], [Bps[5]])
                yield
                for h in range(8):
                    bank = ps[6 + h // 4]
                    o = bank[:, :].rearrange("p (h t) -> p h t", h=4)[:L, h % 4, :L]
                    ACT(Lm[:L, h, :L], o, AF.Exp, [Bps[6 + h // 4], Bsm], [BLm], bias=sm[:L, 0, ti, h:h + 1])
                    if h == 3:
                        yield
                yield
                cbv = ps[5][:, 256:512].rearrange("p (g t) -> p g t", g=2)[:L, :, :L]
                TT('dve', Mb[:L, :, :L].rearrange("p (g h) t -> p g h t", g=2),
                   Lm[:L, :, :L].rearrange("p (g h) t -> p g h t", g=2),
                   cbv.unsqueeze(2).to_broadcast([L, 2, 4, L]), ALU.mult, [BLm, Bps[5]], [BMb])
                yield
                for h in range(8):
                    MM(ps[1][:L, h * 64:(h + 1) * 64], Mb[:L, h, :L], xdt[:L, h, :], h == 0, False, [BMb, Bxdt], [Bps[1]])
                MM(ps[1][:L, :], identb[:L, :L], xsk[:L].rearrange("p h e -> p (h e)"), False, True, [Bc, Bxsk], [Bps[1]])
                for c in range(8):
                    MM(ps[4][:L, :], xnT[:, c, cs], w_in_sb[:, c, 1536:2048], c == 0, c == 7, [BxT, Bwin], [Bps[4]])
                yield
                TT('dve', ytl[:L, :], ytl[:L, :], ps[1][:L, :], ALU.add, [Bytl, Bps[1]], [Bytl])
                ACT(zs[:L, :], ps[4][:L, :], AF.Tanh, [Bps[4]], [Bzs], scale=0.5)
                yield
                STT(zs[:L, :], zs[:L, :], 1.0, ps[4][:L, :], ALU.add, ALU.mult, [Bzs, Bps[4]], [Bzs])
                yield
                STT(ytl[:L, :], zs[:L, :], 0.5, ytl[:L, :], ALU.mult, ALU.mult, [Bzs, Bytl], [Bytl])
                yield
                for g in range(2):
                    ACT(zs[:L, g * 256:(g + 1) * 256], ytl[:L, g * 256:(g + 1) * 256], AF.Square, [Bytl], [Bzs, Bss2],
                        accum_out=ss2[:L, g:g + 1])
                yield
                TS('dve', ss2[:L, 0:2], ss2[:L, 0:2], 1.0 / 256, EPS, ALU.mult, ALU.add, [Bss2], [Bss2])
                yield
                TT('pool', ss2[:L, 2:4], ss2[:L, 0:2], mhalf[:L, :].to_broadcast([L, 2]), ALU.pow, [Bss2, Bc], [Bss2])
                yield
                for g in range(2):
                    STT(ynb[:L, g * 256:(g + 1) * 256], ytl[:L, g * 256:(g + 1) * 256], ss2[:L, 2 + g:3 + g],
                        gssm_b[:L, g * 256:(g + 1) * 256], ALU.mult, ALU.mult, [Bytl, Bss2, Bc], [Bynb])
                yield
                for c in range(4):
                    TR(psT[:, c * 128:c * 128 + L], ynb[:L, c * 128:(c + 1) * 128], identb[:L, :L], [Bynb, Bc], [BpsT])
                mixt_ = mixT if mixt is None else mixt
                CP('act', mixt_[:, 4:8, mix_cols], psT[:, 0:512].rearrange("p (c t) -> p c t", c=4)[:, :, 0:L], [BpsT],
                   [BmixS if Bmix is None else Bmix])
                yield

            def run_chunks(gens, width=2):
                gens = list(gens)
                active = []
                nxt = [0]
                ready = [True]

                def start():
                    if nxt[0] < len(gens) and len(active) < width and ready[0]:
                        active.append(gens[nxt[0]])
                        nxt[0] += 1
                        ready[0] = False
                start()
                while active or nxt[0] < len(gens):
                    if not active:
                        ready[0] = True
                        start()
                    for g in list(active):
                        try:
                            r = next(g)
                        except StopIteration:
                            active.remove(g)
                            start()
                            continue
                        if r == 'S':
                            ready[0] = True
                            start()

            def emit_state(dst_ap):
                for c in range(4):
                    TR(ps[4][:, c * 128:(c + 1) * 128], stf[:, 2 * c:2 * c + 2, :].rearrange("p h e -> p (h e)"), identf[:, :],
                       [Bstf, Bc], [Bps[4]])
                CP('dve', sto[:].rearrange("p c n -> p (c n)"), ps[4][:, :], [Bps[4]], [Bsto])
                S.dma(dst_ap.rearrange("(c p) n -> p c n", p=128), sto[:], reads=[Bsto])

            def a1_supertile(t0, T, kind):
                nt = T // 128
                main = kind != 'prefix'
                for i in range(nt):
                    norm_transpose_tile(xl[t0 + i * 128:t0 + (i + 1) * 128, :], 128, slice(i * 128, (i + 1) * 128), BxnT[i])
                BxT_all = BxnT[:nt]
                nx = 8 if main else 6
                chunks = [('x', j, 2048 + j * 128) for j in range(8)]
                if main:
                    chunks += [('q', c, c * 128) for c in range(4)]
                chunks += [('k', c, 512 + c * 128) for c in range(4)]
                chunks += [('v', c, 1024 + c * 128) for c in range(4)]
                for ci, (knd, idx, col0) in enumerate(chunks):
                    bi = 1 + ci % 3
                    for c in range(8):
                        MM(ps[bi][:, 0:T], w_in_sb[:, c, col0:col0 + 128], xnT[:, c, 0:T], c == 0, c == 7, BxT_all + [Bwin], [Bps[bi]])
                    if knd == 'x':
                        eng = 'act' if (idx % 2 == 0) else 'dve'
                        CP(eng, cbuf[:, idx, 3:3 + T], ps[bi][:, 0:T], [Bps[bi]], [Bcb[idx]])
                    else:
                        si = ev_ctr[0] % 3
                        ev_ctr[0] += 1
                        if knd == 'q':
                            ACT(stg[si][:, 0:T], ps[bi][:, 0:T], AF.Copy, [Bps[bi]], [Bstg[si]], scale=0.125)
                            S.dma(qT_d[idx, :, t0 - NPRE:t0 - NPRE + T], stg[si][:, 0:T], reads=[Bstg[si]], writes=[Bqd])
                        else:
                            CP('dve' if knd == 'k' else 'act', stg[si][:, 0:T], ps[bi][:, 0:T], [Bps[bi]], [Bstg[si]])
                            dst = kT_d if knd == 'k' else vT_d
                            S.dma(dst[idx, :, t0:t0 + T], stg[si][:, 0:T], reads=[Bstg[si]], writes=[Bkd if knd == 'k' else Bvd])
                if kind == 'main':
                    for i in range(nt):
                        for wi, (col0, dst) in enumerate(((512, k_loc), (1024, v_loc))):
                            for c in range(8):
                                MM(ps[4][:, :], xnT[:, c, i * 128:(i + 1) * 128], w_in_sb[:, c, col0:col0 + 512], c == 0, c == 7,
                                   [BxnT[i], Bwin], [Bps[4]])
                            CP('dve' if wi == 0 else 'act', kvo[wi][:, :], ps[4][:, :], [Bps[4]], [Bkvo[wi]])
                            r0 = t0 - NPRE + i * 128
                            S.dma(dst[r0:r0 + 128, :], kvo[wi][:, :], reads=[Bkvo[wi]])
                for i in range(nt):
                    for c in range(8):
                        MM(ps[5][:, i * 8:(i + 1) * 8], xnT[:, c, i * 128:(i + 1) * 128], w_in_sb[:, c, 3072:3080], c == 0, c == 7,
                           [BxnT[i], Bwin], [Bps[5]])
                dv = dtall[:, 0:nt, :]
                TT('dve', dv, ps[5][:, 0:nt * 8].rearrange("p (i h) -> p i h", h=8), dtb[:].unsqueeze(1).to_broadcast([128, nt, 8]),
                   ALU.add, [Bps[5], Bc], [Bdt])
                ACT(dv, dv, AF.Exp, [Bdt], [Bdt])
                ACT(dv, dv, AF.Ln, [Bdt, Bc], [Bdt], bias=onec[:, :])
                if kind == 'prefix':
                    TS('dve', dv, dv, pvt[:, 0:1], None, ALU.mult, None, [Bdt, Bc], [Bdt])
                for j in range(nx):
                    a_ = j % 2
                    ACT(acc[a_][:, 0:T], cbuf[:, j, 3:3 + T], AF.Identity, [Bcb[j], Bc], [Bacc[a_]], scale=cwT[:, j, 3:4], bias=cbT[:, j:j + 1])
                    for k in range(3):
                        STT(acc[a_][:, 0:T], cbuf[:, j, k:k + T], cwT[:, j, k:k + 1], acc[a_][:, 0:T], ALU.mult, ALU.add,
                            [Bcb[j], Bc, Bacc[a_]], [Bacc[a_]])
                    ACT(th[a_][:, 0:T], acc[a_][:, 0:T], AF.Tanh, [Bacc[a_]], [Bth[a_]])
                    STT(xbcT[:, j, 0:T], th[a_][:, 0:T], 1.0, acc[a_][:, 0:T], ALU.add, ALU.mult, [Bth[a_], Bacc[a_]], [Bxbc[j]])
                if kind == 'main' and t0 + T == NPRE + NMAIN:
                    for hf in range(2):
                        for j in range(4):
                            TR(ps[4][0:3, j * 128:(j + 1) * 128], cbuf[:, hf * 4 + j, T:T + 3], identf[:, :], [Bcb[hf * 4 + j], Bc], [Bps[4]])
                        CP('dve', kvo[hf][0:3, :], ps[4][0:3, :], [Bps[4]], [Bkvo[hf]])
                        S.dma(conv_loc[:, hf * 512:(hf + 1) * 512], kvo[hf][0:3, :], reads=[Bkvo[hf]])
                for j in range(8):
                    CP('pool', cbuf[:, j, 0:3], cbuf[:, j, T:T + 3], [Bcb[j]], [Bcb[j]])
                ssd_prep(128, nt, dtall[:, 0:nt, :], main)
                run_chunks([ssd_gen(128, slice(i * 128, (i + 1) * 128), BxnT[i], 'main' if main else 'prefix', dtall[:, i, :],
                                    slice(t0 - NPRE + i * 128, t0 - NPRE + (i + 1) * 128), i, i % 2) for i in range(nt)])
                if kind == 'main' and t0 + T == NPRE + NMAIN:
                    emit_state(ssm_loc)

            Bqd, Bkd, Bvd = Buf("qd"), Buf("kd"), Buf("vd")
            st_list = [(s * 512, 512, 'prefix') for s in range(4)] + [(NPRE + s * 512, 512, 'main') for s in range(4)] + \
                      [(NPRE + NMAIN, 128, 'halo')]
            if skip_p:
                st_list = []
                for c_ in range(128):
                    bm_dma(c_)
            for si_, (t0, T, kind) in enumerate(st_list):
                a1_supertile(t0, T, kind)
                for c_ in range(si_ * 16, min(128, si_ * 16 + 16)):
                    bm_dma(c_)


            if not skip_a1:
                scb = sbt(sa, "scb", [128, 8, 4, 11], F32)
                Bscb = [Buf() for _ in range(8)]
                sc3 = sbt(sa, "sc3", [128, 8, 12], F32)
                Bsc3 = Buf()
                hs12 = xt[0]
                Bhs12 = Bxt[0]
                norm_transpose_tile(xs_d[:, :], 32, slice(0, 32), BxnT[0])
                if ks1 == 0.1:
                    S.barrier(); S.finish(); return nc
                schunks = [('q', c, c * 128) for c in range(4)] + [('k', c, 512 + c * 128) for c in range(4)] + \
                          [('x', j, 2048 + j * 128) for j in range(8)]
                for ci, (knd, idx, col0) in enumerate(schunks):
                    bi = 1 + ci % 3
                    for c in range(8):
                        MM(ps[bi][:, 0:32], w_in_sb[:, c, col0:col0 + 128], xnT[:, c, 0:32], c == 0, c == 7, [BxnT[0], Bwin], [Bps[bi]])
                    if knd == 'q':
                        ACT(sQT[0:64, idx, 0, :], ps[bi][0:64, 0:32], AF.Copy, [Bps[bi]], [Bsq], scale=0.125)
                        ACT(sQT[64:128, idx, 1, :], ps[bi][64:128, 0:32], AF.Copy, [Bps[bi]], [Bsq], scale=0.125)
                    elif knd == 'k':
                        CP('dve', sKT[:, idx, :], ps[bi][:, 0:32], [Bps[bi]], [Bsq])
                    else:
                        CP('act' if idx % 2 == 0 else 'dve', scb[:, idx, :, 3:11], ps[bi][:, 0:32].rearrange("p (b t) -> p b t", b=4),
                           [Bps[bi]], [Bscb[idx]])
                if ks1 == 0.2:
                    S.barrier(); S.finish(); return nc
                kvm = os.environ.get('KV', 'all')
                for wi, (col0, dst) in enumerate(((512, ks_d), (1024, vs_d))):
                    for c in range(8):
                        MM(ps[4][0:32, :], xnT[:, c, 0:32], w_in_sb[:, c, col0:col0 + 512], c == 0, c == 7, [BxnT[0], Bwin], [Bps[4]])
                    if kvm in ('all', 'cp', 'cpdma', 'cpsvn'):
                        CP('dve', kvo[wi][0:32, :], ps[4][0:32, :], [Bps[4]], [Bkvo[wi]])
                    if wi == 1 and kvm in ('all', 'cpsvn'):
                        CP('act', sVn[0:32, :, 0:64], ps[4][0:32, :].rearrange("p (h e) -> p h e", h=8), [Bps[4]], [Bsq])
                    if kvm in ('all', 'cpdma'):
                        S.dma(dst[:, :], kvo[wi][0:32, :], reads=[Bkvo[wi]])
                if ks1 == 1:
                    S.barrier(); S.finish(); return nc
                S.dma(hs12[0:12, :], sconv_d[:, :], writes=[Bhs12])
                for hf in range(2):
                    for j in range(4):
                        TR(ps[4][:, j * 12:(j + 1) * 12], hs12[0:12, (hf * 4 + j) * 128:(hf * 4 + j + 1) * 128], identf[0:12, 0:12],
                           [Bhs12, Bc], [Bps[4]])
                    CP('dve', scb[:, hf * 4:(hf + 1) * 4, :, 0:3], ps[4][:, 0:48].rearrange("p (j b t) -> p j b t", j=4, b=4),
                       [Bps[4]], Bscb[hf * 4:(hf + 1) * 4])
                for j in range(8):
                    a_ = j % 2
                    accv = acc[a_][:, 0:32].rearrange("p (b t) -> p b t", b=4)
                    ACT(accv, scb[:, j, :, 3:11], AF.Identity, [Bscb[j], Bc], [Bacc[a_]], scale=cwT[:, j, 3:4], bias=cbT[:, j:j + 1])
                    for k in range(3):
                        STT(accv, scb[:, j, :, k:k + 8], cwT[:, j, k:k + 1], accv, ALU.mult, ALU.add, [Bscb[j], Bc, Bacc[a_]], [Bacc[a_]])
                    ACT(th[a_][:, 0:32], acc[a_][:, 0:32], AF.Tanh, [Bacc[a_]], [Bth[a_]])
                    STT(xbcT[:, j, 0:32], th[a_][:, 0:32], 1.0, acc[a_][:, 0:32], ALU.add, ALU.mult, [Bth[a_], Bacc[a_]], [Bxbc[j]])
                CP('pool', sc3[:].rearrange("p j (b t) -> p j b t", b=4), scb[:, :, :, 8:11], Bscb, [Bsc3])
                for hf in range(2):
                    for j in range(4):
                        TR(ps[4][0:12, j * 128:(j + 1) * 128], sc3[:, hf * 4 + j, :], identf[:, :], [Bsc3, Bc], [Bps[4]])
                    CP('dve', hs12[0:12, hf * 512:(hf + 1) * 512], ps[4][0:12, :], [Bps[4]], [Bhs12])
                S.dma(conv_s_d[:, :], hs12[0:12, :], reads=[Bhs12])
                if ks1 == 2:
                    S.barrier(); S.finish(); return nc
                for b in range(4):
                    for c in range(8):
                        MM(ps[5][0:8, b * 8:(b + 1) * 8], xnT[:, c, b * 8:(b + 1) * 8], w_in_sb[:, c, 3072:3080], c == 0, c == 7,
                           [BxnT[0], Bwin], [Bps[5]])
                dvs = dtall[0:8, 0:4, :]
                TT('dve', dvs, ps[5][0:8, 0:32].rearrange("p (i h) -> p i h", h=8), dtb[0:8, :].unsqueeze(1).to_broadcast([8, 4, 8]),
                   ALU.add, [Bps[5], Bc], [Bdt])
                ACT(dvs, dvs, AF.Exp, [Bdt], [Bdt])
                ACT(dvs, dvs, AF.Ln, [Bdt, Bc], [Bdt], bias=onec[0:8, :])
                if ks1 == 3:
                    S.barrier(); S.finish(); return nc
                ssd_prep(8, 4, dtall[0:8, 0:4, :], True)
                for b in range(4):
                    S.dma(sto[:], sssm_d[b].rearrange("(c p) n -> p c n", p=128), writes=[Bsto])
                    for c in range(4):
                        TR(ps[4][:, c * 128:(c + 1) * 128], sto[:, c, :], identf[:, :], [Bsto, Bc], [Bps[4]])
                    CP('dve', stf[:].rearrange("p h e -> p (h e)"), ps[4][:, :], [Bps[4]], [Bstf])
                    CP('act', stb[:, :], stf[:].rearrange("p h e -> p (h e)"), [Bstf], [Bstb])
                    for _ in ssd_gen(8, slice(b * 8, (b + 1) * 8), BxnT[0], 'main', dtall[0:8, b, :], slice(b * 8, (b + 1) * 8), b, b % 2,
                                     mixt=smixT, Bmix=BsmixS):
                        pass
                    emit_state(ssm_s_d[b])

        S.barrier()
        if stage <= 1:
            S.finish()
            return nc

        with ExitStack() as s2:
            sel = sbt(s2, "sel", [128, 64], F32)
            QT = [sbt(s2, f"QT{i}", [128, 2, NQ], BF16) for i in range(2)]
            KT = [sbt(s2, f"KT{i}", [128, NLOC], BF16) for i in range(2)]
            VT = [sbt(s2, f"VT{i}", [128, NLOC], BF16) for i in range(2)]
            Bqkv = [Buf() for _ in range(2)]
            NVB = 70
            Vb = sbt(s2, "Vb", [128, NVB, 2, 66], BF16)
            BVb = Buf()
            accA = sbt(s2, "accA", [128, 2, NQ], F32)
            BaccA = Buf()
            PT = [sbt(s2, f"PT{i}", [128, 512], BF16) for i in range(4)]
            BPT = [Buf() for _ in range(4)]
            rc = [sbt(s2, f"rc{i}", [64, 512], F32) for i in range(2)]
            Brc = [Buf() for _ in range(2)]
            TT('dve', Bdg[:], Bm[:, :, 0, 0:2], negd[:].unsqueeze(1).to_broadcast([128, 8, 2]), ALU.add, [BBm, Bc], [Bc])
            if stage == 1.2:
                S.barrier(); S.finish(); return nc
            wtmp = [sbt(s2, f"wtmp{i}", [128, 8, 256], BF16) for i in range(2)]
            Bwtmp = [Buf(f"wtmp{i}") for i in range(2)]
            Bwup_d = Buf("wup_d")
            w_up_v = w_up.rearrange("(c p) n -> p c n", p=128)

            def prep_wup(j):
                s = j % 2
                S.dma(wtmp[s][:, :, 0:128], w_up_v[:, :, j * 128:(j + 1) * 128], writes=[Bwtmp[s]], eng='pool')
                S.dma(wtmp[s][:, :, 128:256], w_up_v[:, :, DFF + j * 128:DFF + (j + 1) * 128], writes=[Bwtmp[s]], eng='pool')
                S.dma(wup_d[j, :, :], wtmp[s][:].rearrange("p c n -> p (c n)"), reads=[Bwtmp[s]], writes=[Bwup_d])


            wtb = [sbt(s2, f"wtb{i}", [128, D], BF16) for i in range(2)]
            Bwtb = [Buf() for _ in range(2)]
            BwB_d = Buf("wB_d")
            wB_src = [w_out[c * 128:(c + 1) * 128, :] for c in range(8)] + [w_down[j * 128:(j + 1) * 128, :] for j in range(NG)] + \
                     [w_ple_proj[c * 128:(c + 1) * 128, :] for c in range(2)] + [w_ple_gate[c * 128:(c + 1) * 128, :] for c in range(8)]

            fcwT = sbt(s2, "fcwT", [128, 2 * NG, 3], F32)
            Bfcw = Buf()
            for k in range(3):
                for hf in range(2):
                    S.dma(fcwT[:, hf * NG:(hf + 1) * NG, k], ffn_conv_w[k, hf * DFF:(hf + 1) * DFF].rearrange("(c p) -> p c", p=128),
                          writes=[Bfcw], allow_slow_non_contiguous=True)
            dgs = [sbt(s2, f"dgs{i}", [128, 2, 3, 128], BF16) for i in range(2)]
            Bdgs = [Buf() for _ in range(2)]
            Bdg_d = Buf("dg_d")

            def prep_dg(j):
                sl = j % 2
                for k in range(3):
                    ACT(dgs[sl][:, 0, k, :], identb[:, :], AF.Copy, [Bc, Bfcw], [Bdgs[sl]], scale=fcwT[:, j, k:k + 1])
                    TS('dve', dgs[sl][:, 1, k, :], identb[:, :], fcwT[:, NG + j, k:k + 1], None, ALU.mult, None, [Bc, Bfcw], [Bdgs[sl]])
                S.dma(dg_d[j, :, :], dgs[sl][:].rearrange("p a k n -> p (a k n)"), reads=[Bdgs[sl]], writes=[Bdg_d])

            def prep_wB(i):
                sl = i % 2
                S.dma(wtb[sl][:, :], wB_src[i], writes=[Bwtb[sl]], eng='pool')
                S.dma(wB_d[i, :, :], wtb[sl][:, :], reads=[Bwtb[sl]], writes=[BwB_d])

            sKc = sbt(s2, "sKc", [128, 4, 13 * 128], BF16)
            BsKc = Buf()
            sVc = sbt(s2, "sVc", [128, 13, 8, 66], BF16)
            BsVc = Buf()
            kst = [sbt(s2, f"kst{i}", [128, 512], F32) for i in range(2)]
            Bkst = [Buf() for _ in range(2)]
            vst = [sbt(s2, f"vst{i}", [128, 512], F32) for i in range(2)]
            Bvst = [Buf() for _ in range(2)]
            accS = sbt(s2, "accS", [128, 4, 2, 32], F32)
            BaccS = [Buf() for _ in range(4)]
            cmt = sbt(s2, "cmt", [32, 3, 32], F32)
            Bown = sbt(s2, "Bown", [32, 8, 3, 32], BF16)
            S.dma(cmt[:], c_cm[:, :, :], writes=[Bc])
            TT('dve', Bown[:], Bm[0:32, :, 0, 0:32].unsqueeze(2).to_broadcast([32, 8, 3, 32]),
               cmt[:].unsqueeze(1).to_broadcast([32, 8, 3, 32]), ALU.add, [Bc], [Bc])
            MEMSET('pool', sVc[:].rearrange("p t h e -> p (t h e)"), 1.0, [BsVc])
            MEMSET('dve', sel[:], 0.0, [Bc])
            MEMSET('dve', sel[64:65, :], 1.0, [Bc])
            MEMSET('dve', accA[:], 1.0, [BaccA])
            for i_ in range(2):
                MEMSET('pool', QT[i_][:].rearrange("p h q -> p (h q)"), 0.0, [Bqkv[i_]])
            MEMSET('pool', Vb[:].rearrange("p b h e -> p (b h e)"), 1.0, [BVb])
            for (a0, a1) in ((0, 1), (18, 22), (38, 54)):
                TS('dve', Vb[:, a0:a1, :, 64:65], Vb[:, a0:a1, :, 64:65], pvt[:, 0:1], None, ALU.mult, None, [BVb, Bc], [BVb])

            vblocks = []
            for tau in range(15, 33):
                vblocks.append((tau - 15, 128 * tau, 1))
            for sg in range(3, 8):
                for r in range(4):
                    vblocks.append((18 + (sg - 3) * 4 + r, 512 * sg + r, 4))
            for z_ in range(2):
                for r in range(16):
                    vblocks.append((38 + 16 * z_ + r, 2048 * z_ + r, 16))
            assert len(vblocks) == NVB

            cnt = {'s': 0, 'o': 0, 'pt': 0, 'n': 0, 'ev': 0}

            def cols(c0, step, n):
                return slice(c0, c0 + step * (n - 1) + 1, step) if step > 1 else slice(c0, c0 + n)

            def attn_core(N, nkeys, k_aps, q_fn, v_fn, bm_ap, acc_ap, mode, pv_rep, rK, rQ, rV, wAcc):
                nk = len(k_aps)
                bi = 1 + cnt['s'] % 3
                cnt['s'] += 1
                psv = ps[bi][:, :].rearrange("p (h k q) -> p h k q", h=2, k=2)
                first = True
                for h2 in range(2):
                    for k in range(nk):
                        MM(psv[:nkeys, h2, k, 0:N], k_aps[k], q_fn(h2), first, False, rK + rQ, [Bps[bi]])
                        first = False
                if N == 128 and nk == 2:
                    MM(psv[:nkeys, :, 0:nk, 0:N], identb[:nkeys, :nkeys], bm_ap, False, True, [Bc], [Bps[bi]])
                else:
                    for h2 in range(2):
                        for k in range(nk):
                            MM(psv[:nkeys, h2, k, 0:N], identb[:nkeys, :nkeys], bm_ap[:, h2, k, :], False, (h2 == 1 and k == nk - 1),
                               [Bc], [Bps[bi]])
                pi = cnt['pt'] % 4
                cnt['pt'] += 1
                ptv = PT[pi][:, :].rearrange("p (h k q) -> p h k q", h=2, k=2)
                ACT(ptv[:nkeys, :, 0:nk, 0:N], psv[:nkeys, :, 0:nk, 0:N], AF.Exp, [Bps[bi]], [BPT[pi]])
                def stage2():
                    oi = 4 + cnt['o'] % 2
                    cnt['o'] += 1
                    pso = ps[oi][0:65, 0:256].rearrange("p (h q) -> p h q", h=2)
                    first2 = True
                    for h2 in range(2):
                        tot = nk * pv_rep
                        ii = 0
                        for k in range(nk):
                            for _rep in range(pv_rep):
                                ii += 1
                                MM(pso[:, h2, 0:N], v_fn(k, h2), ptv[:nkeys, h2, k, 0:N], first2, ii == tot, rV + [BPT[pi]], [Bps[oi]])
                                first2 = False
                    if mode == 'copy':
                        CP('act', acc_ap, pso[:, :, 0:N], [Bps[oi]], wAcc)
                    else:
                        TT('dve', acc_ap, acc_ap, pso[:, :, 0:N], ALU.add, [Bps[oi]] + wAcc, wAcc)

                if pending:
                    pending.pop(0)()
                pending.append(stage2)

            pending = []

            def flush_pending():
                while pending:
                    pending.pop(0)()

            def attn_unit(p, s, N, qc, kbs, bm_ap, accc, mode, pv_rep=1):
                attn_core(N, 128, [KT[s][:, cols(kc0, kst, 128)] for (kc0, kst, _) in kbs],
                          lambda h2: QT[s][:, h2, cols(qc[0], qc[1], N)],
                          lambda k, h2: Vb[:, kbs[k][2], h2, 0:65], bm_ap,
                          accA[0:65, :, cols(accc[0], accc[1], N)], mode, pv_rep, [Bqkv[s]], [], [BVb], [BaccA])

            def bm_std(p, br, N):
                return Bm[:, 2 * p:2 * p + 2, br, :].rearrange("p h (k q) -> p h k q", k=2)[:, :, :, 0:N]

            def bm_prev(p, br, N):
                return Bm[:, 2 * p:2 * p + 2, br, 128:128 + N].unsqueeze(2)

            for p in range(4):
                s = p % 2
                for j in range(p * 6, min(NG, p * 6 + 6)):
                    prep_wup(j)
                for j in range(p * 10, p * 10 + 10):
                    prep_wB(j)
                for j in range(p * 6, min(NG, p * 6 + 6)):
                    prep_dg(j)
                S.dma(QT[s][0:64, 0, :], qT_d[p, 0:64, :], reads=[Bqd], writes=[Bqkv[s]])
                S.dma(QT[s][64:128, 1, :], qT_d[p, 64:128, :], reads=[Bqd], writes=[Bqkv[s]])
                S.dma(KT[s][:, :], kT_d[p, :, :], reads=[Bkd], writes=[Bqkv[s]])
                S.dma(VT[s][:, :], vT_d[p, :, :], reads=[Bvd], writes=[Bqkv[s]])
                for g0 in range(0, NVB, 8):
                    grp = vblocks[g0:g0 + 8]
                    for sl, (vbi, c0, st_) in enumerate(grp):
                        TR(psT[:, sl * 128:(sl + 1) * 128], VT[s][:, cols(c0, st_, 128)], identb[:, :], [Bqkv[s], Bc], [BpsT])
                    n = len(grp)
                    eng = 'act' if (cnt['ev'] % 2 == 0) else 'dve'
                    cnt['ev'] += 1
                    CP(eng, Vb[:, g0:g0 + n, :, 0:64], psT[:, 0:n * 128].rearrange("p (b h e) -> p b h e", b=n, h=2), [BpsT], [BVb])
                if stage == 1.4:
                    S.barrier(); S.finish(); return nc
                for n in range(16):
                    attn_unit(p, s, 128, (128 * n, 1), [(NPRE + 128 * n, 1, n + 1), (NPRE + 128 * (n - 1), 1, n)],
                              bm_std(p, 0, 128), (128 * n, 1), 'copy')
                attn_unit(p, s, 2, (2048, 1), [(NPRE + 2048, 1, 17), (NPRE + 1920, 1, 16)], bm_std(p, 0, 2), (2048, 1), 'copy')
                if stage == 1.6:
                    S.barrier(); S.finish(); return nc
                for sg in range(4):
                    for r in range(4):
                        attn_unit(p, s, 128, (512 * sg + r, 4),
                                  [(NPRE + 512 * sg + r, 4, 18 + (sg + 1) * 4 + r), (NPRE + 512 * (sg - 1) + r, 4, 18 + sg * 4 + r)],
                                  bm_std(p, 1, 128), (512 * sg + r, 4), 'add')
                for r in range(16):
                    attn_unit(p, s, 128, (r, 16), [(NPRE + r, 16, 38 + 16 + r), (r, 16, 38 + r)], bm_std(p, 2, 128), (r, 16), 'add')
                for qi in range(2):
                    attn_unit(p, s, 1, (2048 + qi, 1), [(NPRE + 1536 + qi, 4, 18 + 16 + qi)], bm_prev(p, 1, 1), (2048 + qi, 1), 'add')
                    attn_unit(p, s, 1, (2048 + qi, 1), [(NPRE + qi, 16, 38 + 16 + qi)], bm_prev(p, 2, 1), (2048 + qi, 1), 'add')
                attn_unit(p, s, 2, (2048, 1), [(NPRE + 2048, 1, 17)], Bdg[:, 2 * p:2 * p + 2, 0:2].unsqueeze(2), (2048, 1), 'add', pv_rep=2)
                flush_pending()
                for h2 in range(2):
                    for c0 in range(0, NQ, 512):
                        n = min(512, NQ - c0)
                        bi = 6 + cnt['n'] % 2
                        ri = cnt['n'] % 2
                        cnt['n'] += 1
                        MM(ps[bi][0:64, 0:n], sel[0:65, 0:64], accA[0:65, h2, c0:c0 + n], True, True, [Bc, BaccA], [Bps[bi]])
                        S.op('dve', lambda e, o=rc[ri][0:64, 0:n], i_=ps[bi][0:64, 0:n]: e.reciprocal(out=o, in_=i_), [Bps[bi]], [Brc[ri]])
                        TT('dve', mixT[h2 * 64:(h2 + 1) * 64, p, c0:c0 + n], accA[0:64, h2, c0:c0 + n], rc[ri][0:64, 0:n], ALU.mult,
                           [BaccA, Brc[ri]], [BmixA[p]])

            if not skip_a1:
                for p in range(4):
                    for br in range(3):
                        attn_core(32, 32, [sKT[:, p, 0:32]], lambda h2, p=p: sQT[:, p, h2, 0:32],
                                  lambda k, h2, p=p: sVn[0:32, 2 * p + h2, 0:65], Bown[0:32, 2 * p:2 * p + 2, br, :].unsqueeze(2),
                                  accS[0:65, p, :, :], 'copy' if br == 0 else 'add', 1, [Bsq], [], [Bsq], [BaccS[p]])
                flush_pending()
                tix = [0]
                for b in range(4):
                    tiles = [(1920, 1)] + [(1536 + r, 4) for r in range(4)] + [(r, 16) for r in range(8)]
                    for ti, (r0, st_) in enumerate(tiles):
                        sl = tix[0] % 2
                        tix[0] += 1
                        rows = slice(r0, r0 + 127 * st_ + 1, st_) if st_ > 1 else slice(r0, r0 + 128)
                        S.dma(kst[sl][:, :], ck_d[b, rows, :], writes=[Bkst[sl]])
                        S.dma(vst[sl][:, :], cv_d[b, rows, :], writes=[Bvst[sl]])
                        for c in range(4):
                            TR(ps[7][:, c * 128:(c + 1) * 128], kst[sl][:, c * 128:(c + 1) * 128], identf[:, :], [Bkst[sl], Bc], [Bps[7]])
                        CP('act' if ti % 2 == 0 else 'dve', sKc[:, :, ti * 128:(ti + 1) * 128],
                           ps[7][:, :].rearrange("p (c k) -> p c k", c=4), [Bps[7]], [BsKc])
                        CP('pool', sVc[:, ti, :, 0:64], vst[sl][:, :].rearrange("p (h e) -> p h e", h=8), [Bvst[sl]], [BsVc])
                    for p in range(4):
                        def unit(ti, N, qcol0, qstep, br, p=p, b=b):
                            attn_core(N, 128, [sKc[:, p, ti * 128:(ti + 1) * 128]],
                                      lambda h2: sQT[:, p, h2, cols(b * 8 + qcol0, qstep, N)],
                                      lambda k, h2: sVc[:, ti, 2 * p + h2, 0:65], bm_prev(p, br, N),
                                      accS[0:65, p, :, cols(b * 8 + qcol0, qstep, N)], 'add', 1, [BsKc], [Bsq], [BsVc], [BaccS[p]])
                        unit(0, 8, 0, 1, 0)
                        for r in range(4):
                            unit(1 + r, 2, r, 4, 1)
                        for r in range(8):
                            unit(5 + r, 1, r, 1, 2)
                    flush_pending()
                for p in range(4):
                    for h2 in range(2):
                        bi = 6 + cnt['n'] % 2
                        ri = cnt['n'] % 2
                        cnt['n'] += 1
                        MM(ps[bi][0:64, 0:32], sel[0:65, 0:64], accS[0:65, p, h2, :], True, True, [Bc, BaccS[p]], [Bps[bi]])
                        S.op('dve', lambda e, o=rc[ri][0:64, 0:32], i_=ps[bi][0:64, 0:32]: e.reciprocal(out=o, in_=i_), [Bps[bi]], [Brc[ri]])
                        TT('dve', smixT[h2 * 64:(h2 + 1) * 64, p, :], accS[0:64, p, h2, :], rc[ri][0:64, 0:32], ALU.mult,
                           [BaccS[p], Brc[ri]], [BsmixA[p]])
        sA.close()
        S.barrier()
        if stage <= 2:
            S.finish()
            return nc

        with ExitStack() as s3:
            w_out_sb = sbt(s3, "w_out_sb", [128, 8, D], BF16)
            w_dn_sb = sbt(s3, "w_dn_sb", [128, NG, D], BF16)
            w_pp_sb = sbt(s3, "w_pp_sb", [128, 2, D], BF16)
            w_pg_sb = sbt(s3, "w_pg_sb", [128, 8, D], BF16)
            BwB = Buf()
            S.dma(w_out_sb[:], wB_d[0:8, :, :].rearrange("c p n -> p c n"), reads=[BwB_d], writes=[BwB])
            S.dma(w_dn_sb[:], wB_d[8:30, :, :].rearrange("c p n -> p c n"), reads=[BwB_d], writes=[BwB])
            S.dma(w_pp_sb[:], wB_d[30:32, :, :].rearrange("c p n -> p c n"), reads=[BwB_d], writes=[BwB])
            S.dma(w_pg_sb[:], wB_d[32:40, :, :].rearrange("c p n -> p c n"), reads=[BwB_d], writes=[BwB])
            fcbT = sbt(s3, "fcbT", [128, 2 * NG], F32)
            gple_b = sbt(s3, "gple_b", [128, D], F32)
            gfin_b = sbt(s3, "gfin_b", [128, D], F32)
            for hf in range(2):
                S.dma(fcbT[:, hf * NG:(hf + 1) * NG], ffn_conv_b[hf * DFF:(hf + 1) * DFF].rearrange("(c p) -> p c", p=128),
                      writes=[Bc], allow_slow_non_contiguous=True)
            S.dma(gple_b[:], g_ple.partition_broadcast(128), writes=[Bc])
            S.dma(gfin_b[:], g_final.partition_broadcast(128), writes=[Bc])

            TB = 256
            xtb = [sbt(s3, f"xtb{i}", [128, D], F32) for i in range(2)]
            Bxtb = [Buf() for _ in range(2)]
            ptb = [sbt(s3, f"ptb{i}", [128, 256], F32) for i in range(2)]
            Bptb = [Buf() for _ in range(2)]
            hh = sbt(s3, "hh", [128, 2, D], F32)
            Bhh = [Buf() for _ in range(2)]
            hn2 = [sbt(s3, f"hn{i}", [128, D], BF16) for i in range(2)]
            Bhn2 = [Buf() for _ in range(2)]
            rsb2 = [sbt(s3, f"rsb{i}", [128, 8], F32) for i in range(2)]
            Brsb2 = [Buf() for _ in range(2)]

            def lockstep(gens):
                gens = list(gens)
                while gens:
                    for g in list(gens):
                        try:
                            next(g)
                        except StopIteration:
                            gens.remove(g)

            hnT = sbt(s3, "hnT", [128, 8, TB], BF16)
            BhnT = [Buf() for _ in range(2)]
            wg = [sbt(s3, f"wg{i}", [128, 8, 256], BF16) for i in range(3)]
            Bwg = [Buf() for _ in range(3)]
            ub = [sbt(s3, f"ub{i}", [128, 2, TB + 2], BF16) for i in range(2)]
            Bub = [Buf() for _ in range(2)]
            hist = sbt(s3, "hist", [128, NG, 2, 2], BF16)
            Bhist = [Buf() for _ in range(NG)]
            ffo = sbt(s3, "ffo", [128, 2, NG, 2], F32)
            Bffo = Buf()
            ffs = sbt(s3, "ffs", [8, 512], F32)
            ffo_s = sbt(s3, "ffo_s", [128, 2, NG, 4, 2], F32)
            hist_s = sbt(s3, "hist_s", [128, 2, NG, 4, 2], BF16)
            Bhist_s = Buf()
            hs8 = sbt(s3, "hs8", [8, 512], F32)
            Bhs8 = Buf()
            Bffs = Buf()
            dg = [sbt(s3, f"dg{i}", [128, 2, 3, 128], BF16) for i in range(4)]
            Bdgm = [Buf() for _ in range(4)]
            sa_ = [sbt(s3, f"sa{i}", [128, TB], F32) for i in range(2)]
            Bsa = [Buf() for _ in range(2)]
            gT = sbt(s3, "gT", [128, NG, TB], BF16)
            BgT = Buf()
            ppb2 = [sbt(s3, f"ppb{i}", [128, 256], BF16) for i in range(2)]
            Bppb2 = [Buf() for _ in range(2)]
            ppT2 = [sbt(s3, f"ppT{i}", [128, 2, 128], BF16) for i in range(2)]
            BppT2 = [Buf() for _ in range(2)]
            t12 = xtb
            Bt12 = Bxtb
            h2T2 = [sbt(s3, f"h2T{i}", [128, 8, 128], BF16) for i in range(2)]
            Bh2T2 = [Buf() for _ in range(2)]
            gt2 = [sbt(s3, f"gt{i}", [128, D], F32) for i in range(2)]
            Bgt2 = [Buf() for _ in range(2)]
            MEMSET('dve', hist[:].rearrange("p j a t -> p (j a t)"), 0.0, Bhist)

            tcb = [0]

            def b_supertile(t0, T, samp=False):
                L = 32 if samp else 128
                nt = 1 if samp else T // 128
                mixsrc = smixT if samp else mixT
                Bmixsrc = ([BsmixS] + BsmixA) if samp else ([BmixS] + BmixA)
                def head_gen(i):
                    s = i % 2
                    hn, Bhn, rsb, Brsb = hn2[i], Bhn2[i], rsb2[i], Brsb2[i]
                    r0 = t0 + i * 128
                    xsrc = xs_d[0:32, :] if samp else xl[NPRE + r0:NPRE + r0 + 128, :]
                    S.dma(xtb[s][:L, :], xsrc, writes=[Bxtb[s]])
                    for hf in range(2):
                        bk = 1 + 2 * (i % 2) + hf
                        for c in range(8):
                            MM(ps[bk][:L, :], mixsrc[:, c, r0:r0 + L], w_out_sb[:, c, hf * 512:(hf + 1) * 512], c == 0, c == 7,
                               [BwB] + Bmixsrc, [Bps[bk]])
                        TT('dve', hh[:L, i, hf * 512:(hf + 1) * 512], xtb[s][:L, hf * 512:(hf + 1) * 512], ps[bk][:L, :], ALU.add,
                           [Bxtb[s], Bps[bk]], [Bhh[i]])
                        yield
                    rms_rstd(hh[:L, i, :], L, hn[:L, :], rsb[:L, 0:1], rsb[:L, 1:2], [Bhh[i]], [Bhn, Brsb], 1.0 / D)
                    yield
                    ACT(hn[:L, :], hh[:L, i, :], AF.Copy, [Bhh[i], Brsb], [Bhn], scale=rsb[:L, 1:2])
                    yield
                    for c in range(8):
                        TR(psT[:, c * 128:c * 128 + L], hn[:L, c * 128:(c + 1) * 128], identb[:L, :L], [Bhn, Bc], [BpsT])
                    TT('dve', hnT[:, :, i * 128:i * 128 + L], psT[:, :].rearrange("p (c t) -> p c t", c=8)[:, :, 0:L],
                       gffnT[:].unsqueeze(2).to_broadcast([128, 8, L]), ALU.mult, [BpsT, Bc], [BhnT[i]])
                    yield

                lockstep([head_gen(i) for i in range(nt)])
                last_main = (t0 + T == NMAIN) or samp

                def up_stage(j):
                    ws = j % 3
                    S.dma(wg[ws][:].rearrange("p c n -> p (c n)"), wup_d[j, :, :], reads=[Bwup_d], writes=[Bwg[ws]])
                    us = j % 2
                    if samp:
                        ubv = ub[us][:, :, 0:40].rearrange("p a (b t) -> p a b t", b=4)
                        CP('pool', ubv[:, :, :, 0:2], hist_s[:, :, j, :, :], [Bhist_s], [Bub[us]])
                    else:
                        CP('pool', ub[us][:, :, 0:2], hist[:, j, :, :], [Bhist[j]], [Bub[us]])
                    for ab in range(2):
                        ubk = (3 + ab) if j % 2 == 0 else (1 + ab)
                        for c in range(8):
                            MM(ps[ubk][:, 0:T], wg[ws][:, c, ab * 128:(ab + 1) * 128], hnT[:, c, 0:T], c == 0, c == 7,
                               [Bwg[ws]] + BhnT[:nt], [Bps[ubk]])
                        if samp:
                            pv4 = ps[ubk][:, 0:32].rearrange("p (b t) -> p b t", b=4)
                            CP('act' if ab == 0 else 'dve', ubv[:, ab, :, 2:10], pv4, [Bps[ubk]], [Bub[us]])
                            CP('dve', ffo_s[:, ab, j, :, :], pv4[:, :, 6:8], [Bps[ubk]], [Bffo])
                        else:
                            CP('act' if ab == 0 else 'dve', ub[us][:, ab, 2:2 + T], ps[ubk][:, 0:T], [Bps[ubk]], [Bub[us]])
                            if last_main:
                                CP('dve', ffo[:, ab, j, :], ps[ubk][:, T - 2:T], [Bps[ubk]], [Bffo])
                    if not samp:
                        CP('pool', hist[:, j, :, :], ub[us][:, :, T:T + 2], [Bub[us]], [Bhist[j]])
                    S.dma(dg[j % 4][:].rearrange("p a k n -> p (a k n)"), dg_d[j, :, :], reads=[Bdg_d], writes=[Bdgm[j % 4]])

                def conv_stage(j):
                    us = j % 2
                    for ab in range(2):
                        for k in range(3):
                            if samp:
                                ubv = ub[us][:, :, 0:40].rearrange("p a (b t) -> p a b t", b=4)
                                rhs = ubv[:, ab, :, k:k + 8]
                                o_ = ps[5 + ab][:, 0:32].rearrange("p (b t) -> p b t", b=4)
                            else:
                                rhs = ub[us][:, ab, k:k + T]
                                o_ = ps[5 + ab][:, 0:T]
                            MM(o_, dg[j % 4][:, ab, k, :], rhs, k == 0, k == 2, [Bdgm[j % 4], Bub[us]], [Bps[5 + ab]])
                    ACT(sa_[us][:, 0:T], ps[5][:, 0:T], AF.Silu, [Bps[5], Bc], [Bsa[us]], bias=fcbT[:, j:j + 1])
                    STT(gT[:, j, 0:T], ps[6][:, 0:T], fcbT[:, NG + j:NG + j + 1], sa_[us][:, 0:T], ALU.add, ALU.mult,
                        [Bps[6], Bc, Bsa[us]], [BgT])

                for j in range(NG):
                    up_stage(j)
                    if j > 0:
                        conv_stage(j - 1)
                conv_stage(NG - 1)

                def tail_gen(i):
                    s = i % 2
                    hn, Bhn, rsb, Brsb = hn2[i], Bhn2[i], rsb2[i], Brsb2[i]
                    ppb, Bppb, ppT, BppT = ppb2[i], Bppb2[i], ppT2[i], BppT2[i]
                    t1, Bt1, h2T, Bh2T, gt, Bgt = t12[i], Bt12[i], h2T2[i], Bh2T2[i], gt2[i], Bgt2[i]
                    b1 = [1 + 2 * (i % 2), 2 + 2 * (i % 2)]
                    b2 = [5, 6] if i % 2 == 0 else [7, 6]
                    r0 = t0 + i * 128
                    psrc = psm_d[0:32, :] if samp else pl[r0:r0 + 128, :]
                    S.dma(ptb[s][:L, :], psrc, writes=[Bptb[s]])
                    for hf in range(2):
                        for j in range(NG):
                            MM(ps[b1[hf]][:L, :], gT[:, j, i * 128:i * 128 + L], w_dn_sb[:, j, hf * 512:(hf + 1) * 512], j == 0, j == NG - 1,
                               [BgT, BwB], [Bps[b1[hf]]])
                        TT('dve', hh[:L, i, hf * 512:(hf + 1) * 512], hh[:L, i, hf * 512:(hf + 1) * 512], ps[b1[hf]][:L, :], ALU.add,
                           [Bhh[i], Bps[b1[hf]]], [Bhh[i]])
                        yield
                    CP('act', ppb[:L, :], ptb[s][:L, :], [Bptb[s]], [Bppb])
                    yield
                    for c in range(2):
                        TR(psT[:, c * 128:c * 128 + L], ppb[:L, c * 128:(c + 1) * 128], identb[:L, :L], [Bppb, Bc], [BpsT])
                    CP('dve', ppT[:, :, 0:L], psT[:, 0:256].rearrange("p (c t) -> p c t", c=2)[:, :, 0:L], [BpsT], [BppT])
                    yield
                    for hf in range(2):
                        for c in range(2):
                            MM(ps[b1[hf]][:L, :], ppT[:, c, 0:L], w_pp_sb[:, c, hf * 512:(hf + 1) * 512], c == 0, c == 1, [BppT, BwB], [Bps[b1[hf]]])
                        ACT(t1[:L, hf * 512:(hf + 1) * 512], ps[b1[hf]][:L, :], AF.Square, [Bps[b1[hf]]], [Bt1, Brsb], accum_out=rsb[:L, 2 + hf:3 + hf])
                        yield
                    TT('dve', rsb[:L, 4:5], rsb[:L, 2:3], rsb[:L, 3:4], ALU.add, [Brsb], [Brsb])
                    TS('dve', rsb[:L, 4:5], rsb[:L, 4:5], 1.0 / D, EPS, ALU.mult, ALU.add, [Brsb], [Brsb])
                    yield
                    TT('pool', rsb[:L, 5:6], rsb[:L, 4:5], mhalf[:L, :], ALU.pow, [Brsb, Bc], [Brsb])
                    yield
                    for hf in range(2):
                        STT(t1[:L, hf * 512:(hf + 1) * 512], ps[b1[hf]][:L, :], rsb[:L, 5:6], gple_b[:L, hf * 512:(hf + 1) * 512], ALU.mult, ALU.mult,
                            [Bps[b1[hf]], Brsb, Bc], [Bt1])
                    yield
                    CP('act', hn[:L, :], hh[:L, i, :], [Bhh[i]], [Bhn])
                    yield
                    for c in range(8):
                        TR(psT[:, c * 128:c * 128 + L], hn[:L, c * 128:(c + 1) * 128], identb[:L, :L], [Bhn, Bc], [BpsT])
                    CP('dve', h2T[:, :, 0:L], psT[:, :].rearrange("p (c t) -> p c t", c=8)[:, :, 0:L], [BpsT], [Bh2T])
                    yield
                    for hf in range(2):
                        for c in range(8):
                            MM(ps[b2[hf]][:L, :], h2T[:, c, 0:L], w_pg_sb[:, c, hf * 512:(hf + 1) * 512], c == 0, c == 7, [Bh2T, BwB], [Bps[b2[hf]]])
                        ACT(gt[:L, hf * 512:(hf + 1) * 512], ps[b2[hf]][:L, :], AF.Tanh, [Bps[b2[hf]]], [Bgt], scale=0.5)
                        yield
                    STT(gt[:L, :], gt[:L, :], 1.0, t1[:L, :], ALU.add, ALU.mult, [Bgt, Bt1], [Bgt])
                    yield
                    STT(hh[:L, i, :], gt[:L, :], 0.5, hh[:L, i, :], ALU.mult, ALU.add, [Bgt, Bhh[i]], [Bhh[i]])
                    yield
                    rms_rstd(hh[:L, i, :], L, hn[:L, :], rsb[:L, 6:7], rsb[:L, 7:8], [Bhh[i]], [Bhn, Brsb], 1.0 / D)
                    yield
                    STT(gt[:L, :], hh[:L, i, :], rsb[:L, 7:8], gfin_b[:L, :], ALU.mult, ALU.mult, [Bhh[i], Brsb, Bc], [Bgt])
                    ydst = ys_d[0:32, :] if samp else y_loc[r0:r0 + 128, :]
                    S.dma(ydst, gt[:L, :], reads=[Bgt])
                    yield

                lockstep([tail_gen(i) for i in range(nt)])
                if last_main:
                    nr = 8 if samp else 2
                    for rd in range(11):
                        for q4 in range(4):
                            ch = rd * 4 + q4
                            src = ffo_s[:, ch // NG, ch % NG, :, :].rearrange("p b t -> p (b t)") if samp else ffo[:, ch // NG, ch % NG, :]
                            TR(ps[7][0:nr, q4 * 128:(q4 + 1) * 128], src, identf[:, :], [Bffo, Bc], [Bps[7]])
                        CP('dve', ffs[0:nr, :], ps[7][0:nr, :], [Bps[7]], [Bffs])
                        fdst = ffn_s_d if samp else ffn_loc
                        S.dma(fdst[:, rd * 512:(rd + 1) * 512], ffs[0:nr, :], reads=[Bffs])

            for t0 in range(0, NMAIN, TB):
                b_supertile(t0, TB)
            b_supertile(NMAIN, NHALO)
            if not skip_a1:
                for rd in range(11):
                    S.dma(hs8[:, :], sffn_d[:, rd * 512:(rd + 1) * 512], writes=[Bhs8])
                    for q4 in range(4):
                        TR(ps[7][:, q4 * 8:(q4 + 1) * 8], hs8[0:8, q4 * 128:(q4 + 1) * 128], identf[0:8, 0:8], [Bhs8, Bc], [Bps[7]])
                    for q4 in range(4):
                        ch = rd * 4 + q4
                        CP('dve', hist_s[:, ch // NG, ch % NG, :, :], ps[7][:, q4 * 8:(q4 + 1) * 8].rearrange("p (b t) -> p b t", b=4),
                           [Bps[7]], [Bhist_s])
                b_supertile(0, 32, samp=True)

        S.finish()
    return nc


_NC_CACHE = {}


def _get_nc():
    if 'nc' not in _NC_CACHE:
        import os
        _NC_CACHE['nc'] = build_nc(float(os.environ.get('KSTAGE', '99')))
    return _NC_CACHE['nc']


def _prep_inputs(inputs):
    f = lambda a: np.ascontiguousarray(np.asarray(a, dtype=np.float32))
    x = f(inputs["x_prompt"]); p = f(inputs["p_prompt"])[0]
    consts = host_consts()
    shared = dict(consts)
    for k in ["rel_bias"]:
        shared[k] = f(inputs[k])
    for k in ["g_mix", "w_in", "conv_w", "conv_b", "dt_bias", "a_log", "d_skip", "g_ssm", "w_out", "g_ffn", "w_up",
              "ffn_conv_w", "ffn_conv_b", "w_down", "w_ple_proj", "g_ple", "w_ple_gate"]:
        shared[k] = f(inputs[k])[0]
    shared["g_final"] = f(inputs["g_final"])
    xs = f(inputs["x_sample"]); psm = f(inputs["p_sample"])[0]
    ck = f(inputs["cache_k"])[0]; cv = f(inputs["cache_v"])[0]
    sssm = f(inputs["state_ssm"])[0]; sconv = f(inputs["state_conv"])[0]; sffn = f(inputs["state_ffn_conv"])[0]
    in_maps = []
    for core in range(8):
        b, half = core // 2, core % 2
        if half == 0:
            xl = np.concatenate([np.zeros((NPRE, D), np.float32), x[b, 0:NMAIN + NHALO]], axis=0)
            pl = p[b, 0:NQ]
        else:
            xl = np.concatenate([x[b], np.zeros((NHALO, D), np.float32)], axis=0)
            pl = np.concatenate([p[b, NPRE:], np.zeros((NHALO, 256), np.float32)], axis=0)
        m = dict(shared)
        m["xl"] = np.ascontiguousarray(xl)
        m["pl"] = np.ascontiguousarray(pl)
        m["pv"] = np.full((128, 1), float(half), np.float32)
        sq = slice(4 * core, 4 * core + 4)
        m["xs"] = np.ascontiguousarray(xs[sq].reshape(32, D))
        m["psm"] = np.ascontiguousarray(psm[sq].reshape(32, 256))
        m["ck"] = np.ascontiguousarray(ck[sq].reshape(4, 2048, 512))
        m["cv"] = np.ascontiguousarray(cv[sq].reshape(4, 2048, 512))
        m["sssm"] = np.ascontiguousarray(sssm[sq].reshape(4, 512, 128))
        m["sconv"] = np.ascontiguousarray(sconv[sq].reshape(12, 1024))
        m["sffn"] = np.ascontiguousarray(sffn[sq].reshape(8, 2 * DFF))
        in_maps.append(m)
    return in_maps


def kernel(**inputs):
    in_maps = _prep_inputs(inputs)
    nc = _get_nc()
    res = run_bass_kernel_spmd(nc, in_maps, core_ids=list(range(8)))
    R = res.results
    y_prompt = np.zeros((4, 4096, D), np.float32)
    k_prompt = np.zeros((1, 4, 2048, 8, 64), np.float32)
    v_prompt = np.zeros((1, 4, 2048, 8, 64), np.float32)
    ssm_prompt = np.zeros((1, 4, 8, 64, 128), np.float32)
    conv_prompt = np.zeros((1, 4, 3, 1024), np.float32)
    ffn_prompt = np.zeros((1, 4, 2, 2 * DFF), np.float32)
    for b in range(4):
        A, Bc = R[2 * b], R[2 * b + 1]
        y_prompt[b, 0:NMAIN + 2] = A["y_loc"][0:NMAIN + 2]
        y_prompt[b, NMAIN + 2:] = Bc["y_loc"][2:NMAIN]
        k_prompt[0, b] = Bc["k_loc"].reshape(2048, 8, 64)
        v_prompt[0, b] = Bc["v_loc"].reshape(2048, 8, 64)
        ssm_prompt[0, b] = Bc["ssm_loc"].reshape(8, 64, 128)
        conv_prompt[0, b] = Bc["conv_loc"]
        ffn_prompt[0, b] = Bc["ffn_loc"]
    y_sample = np.zeros((32, 8, D), np.float32)
    k_sample = np.zeros((1, 32, 8, 8, 64), np.float32)
    v_sample = np.zeros((1, 32, 8, 8, 64), np.float32)
    ssm_sample = np.zeros((1, 32, 8, 64, 128), np.float32)
    conv_sample = np.zeros((1, 32, 3, 1024), np.float32)
    ffn_sample = np.zeros((1, 32, 2, 2 * DFF), np.float32)
    for core in range(8):
        sq = slice(4 * core, 4 * core + 4)
        r = R[core]
        y_sample[sq] = r["ys"].reshape(4, 8, D)
        k_sample[0, sq] = r["ks"].reshape(4, 8, 8, 64)
        v_sample[0, sq] = r["vs"].reshape(4, 8, 8, 64)
        ssm_sample[0, sq] = r["ssm_s"].reshape(4, 8, 64, 128)
        conv_sample[0, sq] = r["conv_s"].reshape(4, 3, 1024)
        ffn_sample[0, sq] = r["ffn_s"].reshape(4, 2, 2 * DFF)
    return (y_prompt, y_sample, k_prompt, v_prompt, k_sample, v_sample, ssm_prompt, ssm_sample,
            conv_prompt, conv_sample, ffn_prompt, ffn_sample)
```

```python
import numpy as np
from contextlib import ExitStack
import concourse.bass as bass
import concourse.mybir as mybir
from concourse.bass_utils import run_bass_kernel_spmd

F32 = mybir.dt.float32
BF16 = mybir.dt.bfloat16
AF = mybir.ActivationFunctionType
ALU = mybir.AluOpType

NEG = -30000.0
D = 1024
NPRE = 2048
NMAIN = 2048
NHALO = 128
NLOC = NPRE + NMAIN + NHALO
NQ = NMAIN + NHALO
IN_DIM = 3080
DFF = 2816
NG = 22
EPS = 1e-6
BRANCH_D = (1, 4, 16)


class Buf:
    def __init__(self, name="b", excl=False):
        self.name = name
        self.w = None
        self.r = {}
        self.excl = excl


class Sched:
    NDQ = 28
    NSP = 20

    def __init__(self, nc, stack):
        self.nc = nc
        self.h = {'pe': nc.tensor, 'act': nc.scalar, 'dve': nc.vector, 'pool': nc.gpsimd, 'sp': nc.sync}
        self.sem = {k: stack.enter_context(nc.semaphore(k + "_sem")) for k in self.h}
        self.cnt = {k: 0 for k in self.h}
        self.seen = {k: {} for k in self.h}
        for i in range(self.NDQ):
            self.sem[('dq', i)] = stack.enter_context(nc.semaphore(f"dq{i}"))
        self.dcnt = [0] * self.NDQ
        self.rr = 0
        self.rr_pool = 0
        self.nwait = 0
        self.ndma = 0

    def _deps(self, reads, writes):
        deps = {}
        for b in reads:
            if b.w is not None:
                k, v = b.w
                if deps.get(k, 0) < v:
                    deps[k] = v
            if b.excl:
                for k, v in b.r.items():
                    if deps.get(k, 0) < v:
                        deps[k] = v
        for b in writes:
            if b.w is not None:
                k, v = b.w
                if deps.get(k, 0) < v:
                    deps[k] = v
            for k, v in b.r.items():
                if deps.get(k, 0) < v:
                    deps[k] = v
        return deps

    def _wait(self, eng, deps):
        h = self.h[eng]
        seen = self.seen[eng]
        for k, v in deps.items():
            if k == eng and eng in ('pe', 'sp'):
                continue
            if seen.get(k, 0) >= v:
                continue
            h.wait_ge(self.sem[k], v)
            self.nwait += 1
            seen[k] = v

    def _mark(self, ev, reads, writes):
        k, v = ev
        for b in reads:
            if b.r.get(k, 0) < v:
                b.r[k] = v
        for b in writes:
            b.w = ev
            b.r = {}

    def op(self, eng, fn, reads=(), writes=()):
        self._wait(eng, self._deps(reads, writes))
        ins = fn(self.h[eng])
        self.cnt[eng] += 1
        ins.then_inc(self.sem[eng], 1)
        self._mark((eng, self.cnt[eng]), reads, writes)

    def dma(self, out, in_, reads=(), writes=(), eng='sp', **kw):
        if eng == 'pool':
            i = self.NSP + self.rr_pool
            self.rr_pool = (self.rr_pool + 1) % (self.NDQ - self.NSP)
        else:
            i = self.rr
            self.rr = (i + 1) % self.NSP
        deps = self._deps(reads, writes)
        if self.dcnt[i] > 0:
            k = ('dq', i)
            deps[k] = max(deps.get(k, 0), 16 * self.dcnt[i])
        self._wait(eng, deps)
        ins = self.h[eng].dma_start(out=out, in_=in_, **kw)
        self.dcnt[i] += 1
        self.ndma += 1
        ins.then_inc(self.sem[('dq', i)], 16)
        self._mark((('dq', i), 16 * self.dcnt[i]), reads, writes)

    def pe_mode(self, mode):
        if getattr(self, 'cur_mode', None) is not None and self.cur_mode != mode and self.cnt['pe'] > 0:
            self.h['pe'].wait_ge(self.sem['pe'], self.cnt['pe'])
            self.h['pe'].drain()
            self.nwait += 1
            self.ndrain = getattr(self, 'ndrain', 0) + 1
        self.cur_mode = mode

    def barrier(self):
        for eng in self.h:
            deps = {k: self.cnt[k] for k in self.h if k != eng and self.cnt[k] > 0}
            for i in range(self.NDQ):
                if self.dcnt[i] > 0:
                    deps[('dq', i)] = 16 * self.dcnt[i]
            self._wait(eng, deps)

    def finish(self):
        deps = {('dq', i): 16 * self.dcnt[i] for i in range(self.NDQ) if self.dcnt[i] > 0}
        for k in self.h:
            if k != 'sp' and self.cnt[k] > 0:
                deps[k] = self.cnt[k]
        self._wait('sp', deps)


def rel_bucket_np(dist):
    dist = np.asarray(dist, np.int64)
    d = np.maximum(dist, 1).astype(np.float32)
    large = 16 + (np.log(d / np.float32(16)) / np.float32(np.log(2048 / 16)) * np.float32(16)).astype(np.int32)
    large = np.minimum(large, 31)
    return np.where(dist < 16, dist, large).astype(np.int64)


def host_consts():
    ident = np.eye(128, dtype=np.float32)
    triu = np.triu(np.ones((128, 128), np.float32))
    negtri = np.where(triu > 0, 0.0, NEG).astype(np.float32)
    oh = np.zeros((32, 3 * 129), np.float32)
    for bi, d in enumerate(BRANCH_D):
        bk = rel_bucket_np(np.arange(129) * d)
        for j in range(129):
            oh[bk[j], bi * 129 + j] = 1.0
    negdiag = np.full((128, 2), NEG, np.float32)
    negdiag[0, 0] = 0.0
    negdiag[1, 1] = 0.0
    cm = np.full((32, 3, 32), NEG, np.float32)
    for kg in range(32):
        for qg in range(32):
            if kg // 8 != qg // 8:
                continue
            dist = qg % 8 - kg % 8
            if dist < 0:
                continue
            cm[kg, 0, qg] = 0.0
            if dist in (0, 4):
                cm[kg, 1, qg] = 0.0
            if dist == 0:
                cm[kg, 2, qg] = 0.0
    return dict(ident=ident, triu=triu, negtri=negtri, oh=oh, negdiag=negdiag, cm=cm)


def build_nc(stage=99):
    global _LAST_S
    import os
    skip_p = os.environ.get('KSKIPP', '0') == '1'
    skip_a1 = os.environ.get('KNOSAMP', '0') == '1'
    ks1 = float(os.environ.get('KS1', '99'))
    nc = bass.Bass("TRN2", target_bir_lowering=False)

    def din(name, shape):
        return nc.dram_tensor(name, list(shape), F32, kind="ExternalInput").ap()

    def dout(name, shape):
        return nc.dram_tensor(name, list(shape), F32, kind="ExternalOutput").ap()

    xl = din("xl", [NLOC, D])
    pl = din("pl", [NQ, 256])
    pv = din("pv", [128, 1])
    rel_bias = din("rel_bias", [32, 8])
    g_mix = din("g_mix", [D])
    w_in = din("w_in", [D, IN_DIM])
    conv_w = din("conv_w", [4, 1024])
    conv_b = din("conv_b", [1024])
    dt_bias = din("dt_bias", [8])
    a_log = din("a_log", [8])
    d_skip = din("d_skip", [8])
    g_ssm = din("g_ssm", [512])
    w_out = din("w_out", [D, D])
    g_ffn = din("g_ffn", [D])
    w_up = din("w_up", [D, 2 * DFF])
    ffn_conv_w = din("ffn_conv_w", [3, 2 * DFF])
    ffn_conv_b = din("ffn_conv_b", [2 * DFF])
    w_down = din("w_down", [DFF, D])
    w_ple_proj = din("w_ple_proj", [256, D])
    g_ple = din("g_ple", [D])
    w_ple_gate = din("w_ple_gate", [D, D])
    g_final = din("g_final", [D])
    c_ident = din("ident", [128, 128])
    c_triu = din("triu", [128, 128])
    c_negtri = din("negtri", [128, 128])
    c_oh = din("oh", [32, 387])
    c_negdiag = din("negdiag", [128, 2])

    xs_d = din("xs", [32, D])
    psm_d = din("psm", [32, 256])
    ck_d = din("ck", [4, 2048, 512])
    cv_d = din("cv", [4, 2048, 512])
    sssm_d = din("sssm", [4, 512, 128])
    sconv_d = din("sconv", [12, 1024])
    sffn_d = din("sffn", [8, 2 * DFF])
    c_cm = din("cm", [32, 3, 32])
    ys_d = dout("ys", [32, D])
    ks_d = dout("ks", [32, 512])
    vs_d = dout("vs", [32, 512])
    ssm_s_d = dout("ssm_s", [4, 512, 128])
    conv_s_d = dout("conv_s", [12, 1024])
    ffn_s_d = dout("ffn_s", [8, 2 * DFF])

    y_loc = dout("y_loc", [NQ, D])
    k_loc = dout("k_loc", [NMAIN, 512])
    v_loc = dout("v_loc", [NMAIN, 512])
    ssm_loc = dout("ssm_loc", [512, 128])
    conv_loc = dout("conv_loc", [3, 1024])
    ffn_loc = dout("ffn_loc", [2, 2 * DFF])

    qT_d = nc.dram_tensor("qT_d", [4, 128, NQ], BF16, kind="Internal").ap()
    kT_d = nc.dram_tensor("kT_d", [4, 128, NLOC], BF16, kind="Internal").ap()
    vT_d = nc.dram_tensor("vT_d", [4, 128, NLOC], BF16, kind="Internal").ap()
    bm_d = nc.dram_tensor("bm_d", [8, 3, 384], BF16, kind="Internal").ap()
    wup_d = nc.dram_tensor("wup_d", [NG, 128, 8 * 256], BF16, kind="Internal").ap()
    wB_d = nc.dram_tensor("wB_d", [40, 128, D], BF16, kind="Internal").ap()
    dg_d = nc.dram_tensor("dg_d", [NG, 128, 768], BF16, kind="Internal").ap()

    with ExitStack() as top:
        S = Sched(nc, top)
        _LAST_S = S

        def sbt(stack, name, shape, dt):
            return stack.enter_context(nc.sbuf_tensor("s_" + name, list(shape), dt))

        def _ru(x):
            return 32 if x <= 32 else (64 if x <= 64 else 128)

        def _mode(lhsT):
            shp = lhsT.shape
            m = 1
            for d_ in shp[1:]:
                m *= int(d_)
            return (_ru(int(shp[0])), _ru(m), lhsT.dtype == F32)

        def MM(out, lhsT, rhs, start, stop, r, w):
            md = _mode(lhsT)
            S.pe_mode(md if md[:2] != (128, 128) else (128, 128))
            S.op('pe', lambda e: e.matmul(out, lhsT=lhsT, rhs=rhs, start=start, stop=stop, skip_group_check=True), r, w)

        def TR(out, in_, ident, r, w):
            md = _mode(in_)
            S.pe_mode((md + ('T',)) if md[:2] != (128, 128) else (128, 128))
            S.op('pe', lambda e: e.transpose(out=out, in_=in_, identity=ident), r, w)

        def ACT(out, in_, func, r, w, **kw):
            S.op('act', lambda e: e.activation(out=out, in_=in_, func=func, **kw), r, w)

        def CP(eng, out, in_, r, w):
            if eng == 'act':
                S.op('act', lambda e: e.activation(out=out, in_=in_, func=AF.Copy), r, w)
            else:
                S.op(eng, lambda e: e.tensor_copy(out=out, in_=in_), r, w)

        def TT(eng, out, in0, in1, op, r, w):
            S.op(eng, lambda e: e.tensor_tensor(out=out, in0=in0, in1=in1, op=op), r, w)

        def TS(eng, out, in0, s1, s2, op0, op1, r, w):
            if op1 is None:
                S.op(eng, lambda e: e.tensor_scalar(out=out, in0=in0, scalar1=s1, scalar2=None, op0=op0), r, w)
            else:
                S.op(eng, lambda e: e.tensor_scalar(out=out, in0=in0, scalar1=s1, scalar2=s2, op0=op0, op1=op1), r, w)

        def STT(out, in0, scalar, in1, op0, op1, r, w):
            S.op('dve', lambda e: e.scalar_tensor_tensor(out=out, in0=in0, scalar=scalar, in1=in1, op0=op0, op1=op1), r, w)

        def MEMSET(eng, ap, val, w):
            S.op(eng, lambda e: e.memset(ap, val), (), w)

        psT = top.enter_context(nc.psum_tensor("psT", [128, 1024], BF16))
        BpsT = Buf("psT", excl=True)
        ps = [None] + [top.enter_context(nc.psum_tensor(f"ps{i}", [128, 512], F32)) for i in range(1, 8)]
        Bps = [None] + [Buf(f"ps{i}", excl=True) for i in range(1, 8)]

        identf = sbt(top, "identf", [128, 128], F32)
        identb = sbt(top, "identb", [128, 128], BF16)
        onec = sbt(top, "onec", [128, 1], F32)
        epsc = sbt(top, "epsc", [128, 1], F32)
        mhalf = sbt(top, "mhalf", [128, 1], F32)
        pvt = sbt(top, "pvt", [128, 1], F32)
        gmixT = sbt(top, "gmixT", [128, 8], F32)
        gffnT = sbt(top, "gffnT", [128, 8], F32)
        Bc = Buf("consts")

        top.enter_context(nc.Block())

        S.dma(identf[:], c_ident[:, :], writes=[Bc])
        S.dma(pvt[:], pv[:, :], writes=[Bc])
        S.dma(gmixT[:], g_mix.rearrange("(c p) -> p c", p=128), writes=[Bc], allow_slow_non_contiguous=True)
        S.dma(gffnT[:], g_ffn.rearrange("(c p) -> p c", p=128), writes=[Bc], allow_slow_non_contiguous=True)
        MEMSET('dve', onec[:], 1.0, [Bc])
        MEMSET('dve', epsc[:], EPS, [Bc])
        MEMSET('dve', mhalf[:], -0.5, [Bc])
        CP('dve', identb[:], identf[:], [Bc], [Bc])

        def rms_rstd(src_ap, L, junk_ap, ss_ap, rstd_ap, r, w, inv_n):
            ACT(junk_ap, src_ap, AF.Square, r, w, accum_out=ss_ap)
            TS('dve', ss_ap, ss_ap, inv_n, EPS, ALU.mult, ALU.add, w, w)
            TT('pool', rstd_ap, ss_ap, mhalf[:L, :], ALU.pow, w + [Bc], w)

        mixT = sbt(top, "mixT", [128, 8, NQ], BF16)
        BmixA = [Buf(f"mixA{p}") for p in range(4)]
        BmixS = Buf("mixS")
        smixT = sbt(top, "smixT", [128, 8, 32], BF16)
        BsmixA = [Buf() for _ in range(4)]
        BsmixS = Buf()
        sQT = sbt(top, "sQT", [128, 4, 2, 32], BF16)
        sKT = sbt(top, "sKT", [128, 4, 32], BF16)
        sVn = sbt(top, "sVn", [32, 8, 66], BF16)
        Bsq = Buf()
        MEMSET('pool', sQT[:].rearrange("p a h q -> p (a h q)"), 0.0, [Bsq])
        MEMSET('pool', sVn[:].rearrange("p h e -> p (h e)"), 1.0, [Bsq])

        sA = ExitStack()
        Bm = sbt(sA, "Bm", [128, 8, 3, 256], BF16)
        Bdg = sbt(sA, "Bdg", [128, 8, 2], BF16)
        negd = sbt(sA, "negd", [128, 2], F32)
        sA0 = ExitStack()
        rb = sbt(sA0, "rb", [32, 8], F32)
        oht = sbt(sA0, "oht", [32, 387], F32)
        gfull = sbt(sA0, "gfull", [8, 3, 384], F32)
        Bt = Buf()
        Bbmd = Buf()
        BBm = Buf()
        S.dma(rb[:], rel_bias[:, :], writes=[Bt])
        S.dma(oht[:], c_oh[:, :], writes=[Bt])
        S.dma(negd[:], c_negdiag[:, :], writes=[Bc])
        MEMSET('dve', gfull[:], NEG, [Bt])
        MM(ps[1][0:8, 0:387], rb[:, :], oht[:, :], True, True, [Bt], [Bps[1]])
        CP('dve', gfull[:, :, 127:256], ps[1][0:8, 0:387].rearrange("p (b j) -> p b j", b=3), [Bps[1], Bt], [Bt])
        gfull_b = sbt(sA0, "gfull_b", [8, 3, 384], BF16)
        CP('dve', gfull_b[:], gfull[:], [Bt], [Bt])
        S.dma(bm_d[:, :, :], gfull_b[:], reads=[Bt], writes=[Bbmd])

        S.barrier()
        sA0.close()

        def bm_dma(c):
            S.dma(Bm[c:c + 1, :, :, :], bm_d[:, :, 127 - c:127 - c + 256].unsqueeze(0), reads=[Bbmd], writes=[BBm])

        with ExitStack() as sa:
            w_in_sb = sbt(sa, "w_in_sb", [128, 8, IN_DIM], BF16)
            Bwin = Buf("w_in")
            for c in range(8):
                S.dma(w_in_sb[:, c, :], w_in[c * 128:(c + 1) * 128, :], writes=[Bwin], eng='pool')
            triu = sbt(sa, "triu", [128, 128], F32)
            ones = sbt(sa, "ones", [128, 128], F32)
            negtri_f = sbt(sa, "negtri_f", [128, 128], F32)
            negtri8 = sbt(sa, "negtri8", [128, 8, 128], BF16)
            cwT = sbt(sa, "cwT", [128, 8, 4], F32)
            cbT = sbt(sa, "cbT", [128, 8], F32)
            dtb = sbt(sa, "dtb", [128, 8], F32)
            A_b = sbt(sa, "A_b", [128, 8], F32)
            dskip_b = sbt(sa, "dskip_b", [128, 8], F32)
            gssm_b = sbt(sa, "gssm_b", [128, 512], F32)
            S.dma(triu[:], c_triu[:, :], writes=[Bc])
            S.dma(negtri_f[:], c_negtri[:, :], writes=[Bc])
            for k in range(4):
                S.dma(cwT[:, :, k], conv_w[k, :].rearrange("(c p) -> p c", p=128), writes=[Bc], allow_slow_non_contiguous=True)
            S.dma(cbT[:], conv_b.rearrange("(c p) -> p c", p=128), writes=[Bc], allow_slow_non_contiguous=True)
            S.dma(dtb[:], dt_bias.partition_broadcast(128), writes=[Bc])
            S.dma(A_b[:], a_log.partition_broadcast(128), writes=[Bc])
            S.dma(dskip_b[:], d_skip.partition_broadcast(128), writes=[Bc])
            S.dma(gssm_b[:], g_ssm.partition_broadcast(128), writes=[Bc])
            MEMSET('pool', ones[:], 1.0, [Bc])
            CP('dve', negtri8[:], negtri_f[:].unsqueeze(1).to_broadcast([128, 8, 128]), [Bc], [Bc])
            TS('dve', cwT[:], cwT[:], 0.5, None, ALU.mult, None, [Bc], [Bc])
            TS('dve', cbT[:], cbT[:], 0.5, None, ALU.mult, None, [Bc], [Bc])
            ACT(A_b[:], A_b[:], AF.Exp, [Bc], [Bc])
            TS('dve', A_b[:], A_b[:], -1.0, None, ALU.mult, None, [Bc], [Bc])

            xt = [sbt(sa, f"xt{i}", [128, D], F32) for i in range(2)]
            Bxt = [Buf() for _ in range(2)]
            xn = [sbt(sa, f"xn{i}", [128, D], BF16) for i in range(2)]
            Bxn = [Buf() for _ in range(2)]
            rs = [sbt(sa, f"rs{i}", [128, 2], F32) for i in range(2)]
            Brs = [Buf() for _ in range(2)]
            xnT = sbt(sa, "xnT", [128, 8, 512], BF16)
            BxnT = [Buf() for _ in range(4)]
            stg = [sbt(sa, f"stg{i}", [128, 512], BF16) for i in range(3)]
            Bstg = [Buf() for _ in range(3)]
            cbuf = sbt(sa, "cbuf", [128, 8, 515], F32)
            Bcb = [Buf() for _ in range(8)]
            acc = [sbt(sa, f"acc{i}", [128, 512], F32) for i in range(2)]
            Bacc = [Buf() for _ in range(2)]
            th0 = sbt(sa, "th0", [128, 512], F32)
            th = [th0, th0]
            Bth0 = Buf()
            Bth = [Bth0, Bth0]
            xbcT = sbt(sa, "xbcT", [128, 8, 512], BF16)
            Bxbc = [Buf() for _ in range(8)]
            kvo = [sbt(sa, f"kvo{i}", [128, 512], F32) for i in range(2)]
            Bkvo = [Buf() for _ in range(2)]
            dtall = sbt(sa, "dtall", [128, 4, 8], F32)
            Bdt = Buf()
            xsB2 = [sbt(sa, f"xsB{i}", [128, 768], BF16) for i in range(2)]
            BxsB2 = [Buf() for _ in range(2)]
            sm = sbt(sa, "sm", [128, 4, 4, 8], F32)
            Bsm = Buf()
            avall = sbt(sa, "avall", [128, 4, 8], F32)
            Bav = Buf()
            Lm2 = [sbt(sa, f"Lm{i}", [128, 8, 128], F32) for i in range(2)]
            BLm2 = [Buf() for _ in range(2)]
            Mb2 = [sbt(sa, f"Mb{i}", [128, 8, 128], BF16) for i in range(2)]
            BMb2 = [Buf() for _ in range(2)]
            xdt2 = [sbt(sa, f"xdt{i}", [128, 8, 64], BF16) for i in range(2)]
            Bxdt2 = [Buf() for _ in range(2)]
            xde2 = [sbt(sa, f"xde{i}", [128, 8, 64], BF16) for i in range(2)]
            Bxde2 = [Buf() for _ in range(2)]
            xsk2 = [sbt(sa, f"xsk{i}", [128, 8, 64], BF16) for i in range(2)]
            Bxsk2 = [Buf() for _ in range(2)]
            stf = sbt(sa, "stf", [128, 8, 64], F32)
            Bstf = Buf()
            stb = sbt(sa, "stb", [128, 512], BF16)
            Bstb = Buf()
            ytl2 = [sbt(sa, f"ytl{i}", [128, 512], F32) for i in range(2)]
            Bytl2 = [Buf() for _ in range(2)]
            zs = sbt(sa, "zs", [128, 512], F32)
            Bzs = Buf()
            ss2 = sbt(sa, "ss2", [128, 4], F32)
            Bss2 = Buf()
            ynb = sbt(sa, "ynb", [128, 512], BF16)
            Bynb = Buf()
            sto = sbt(sa, "sto", [128, 4, 128], F32)
            Bsto = Buf()

            MEMSET('dve', cbuf[:, :, 0:3], 0.0, Bcb)
            MEMSET('dve', stf[:], 0.0, [Bstf])
            MEMSET('dve', stb[:], 0.0, [Bstb])

            tile_ctr = [0]
            ev_ctr = [0]

            def norm_transpose_tile(src_rows_ap, L, dst_cols, BdstT):
                s = tile_ctr[0] % 2
                tile_ctr[0] += 1
                S.dma(xt[s][:L, :], src_rows_ap, writes=[Bxt[s]])
                rms_rstd(xt[s][:L, :], L, xn[s][:L, :], rs[s][:L, 0:1], rs[s][:L, 1:2], [Bxt[s]], [Bxn[s], Brs[s]], 1.0 / D)
                ACT(xn[s][:L, :], xt[s][:L, :], AF.Copy, [Bxt[s], Brs[s]], [Bxn[s]], scale=rs[s][:L, 1:2])
                for c in range(8):
                    TR(psT[:, c * 128:c * 128 + L], xn[s][:L, c * 128:(c + 1) * 128], identb[:L, :L], [Bxn[s], Bc], [BpsT])
                pv3 = psT[:, :].rearrange("p (c t) -> p c t", c=8)[:, :, 0:L]
                TT('dve', xnT[:, :, dst_cols], pv3, gmixT[:].unsqueeze(2).to_broadcast([128, 8, L]), ALU.mult,
                   [BpsT, Bc], [BdstT])

            def ssd_prep(L, nt, dv_ap, main):
                TT('dve', avall[:L, 0:nt, :], dv_ap, A_b[:L, :].unsqueeze(1).to_broadcast([L, nt, 8]), ALU.mult, [Bdt, Bc], [Bav])
                rhs = avall[:L, 0:nt, :].rearrange("p i h -> p (i h)")
                MM(ps[5][:L, 32:32 + nt * 8], triu[:L, :L], rhs, True, True, [Bc, Bav], [Bps[5]])
                MM(ps[5][:, 64:64 + nt * 8], ones[:L, :], rhs, True, True, [Bc, Bav], [Bps[5]])
                pac = ps[5][:, 32:32 + nt * 8].rearrange("p (i h) -> p i h", h=8)
                pal = ps[5][:, 64:64 + nt * 8].rearrange("p (i h) -> p i h", h=8)
                TS('dve', sm[:L, 0, 0:nt, :], pac[:L], -1.0, None, ALU.mult, None, [Bps[5]], [Bsm])
                TT('dve', sm[:L, 2, 0:nt, :], pal[:L], sm[:L, 0, 0:nt, :], ALU.add, [Bps[5], Bsm], [Bsm])
                ACT(sm[:L, 2, 0:nt, :], sm[:L, 2, 0:nt, :], AF.Exp, [Bsm], [Bsm])
                TT('dve', sm[:L, 2, 0:nt, :], sm[:L, 2, 0:nt, :], dv_ap, ALU.mult, [Bsm, Bdt], [Bsm])
                ACT(sm[:, 3, 0:nt, :], pal, AF.Exp, [Bps[5]], [Bsm])
                if main:
                    ACT(sm[:L, 1, 0:nt, :], pac[:L], AF.Exp, [Bps[5]], [Bsm])

            def ssd_gen(L, cs, BxT, kind, dt_ap, mix_cols, ti, par, mixt=None, Bmix=None):
                main = (kind == 'main')
                xsB, BxsB = xsB2[par], BxsB2[par]
                Lm, BLm, Mb, BMb = Lm2[par], BLm2[par], Mb2[par], BMb2[par]
                xdt, Bxdt, xde, Bxde, xsk, Bxsk = xdt2[par], Bxdt2[par], xde2[par], Bxde2[par], xsk2[par], Bxsk2[par]
                ytl, Bytl = ytl2[par], Bytl2[par]
                for j in range(6):
                    TR(psT[:L, j * 128:(j + 1) * 128], xbcT[:, j, cs], identb[:, :], [Bxbc[j], Bc], [BpsT])
                CP('act', xsB[:L, :], psT[:L, 0:768], [BpsT], [BxsB])
                yield
                av = avall[:, ti, :]
                TT('dve', xde[:L], xsB[:L, 0:512].rearrange("p (h e) -> p h e", h=8),
                   sm[:L, 2, ti, :].unsqueeze(2).to_broadcast([L, 8, 64]), ALU.mult, [BxsB, Bsm], [Bxde])
                if main:
                    TT('dve', xdt[:L], xsB[:L, 0:512].rearrange("p (h e) -> p h e", h=8),
                       dt_ap.unsqueeze(2).to_broadcast([L, 8, 64]), ALU.mult, [BxsB, Bdt], [Bxdt])
                    TT('pool', xsk[:L], xsB[:L, 0:512].rearrange("p (h e) -> p h e", h=8),
                       dskip_b[:L, :].unsqueeze(2).to_broadcast([L, 8, 64]), ALU.mult, [BxsB, Bc], [Bxsk])
                    for g in range(2):
                        MM(ps[2][:L, g * 256:(g + 1) * 256], xbcT[:, 6 + g, cs], stb[:, g * 256:(g + 1) * 256], g == 0, g == 1,
                           [Bxbc[6 + g], Bstb], [Bps[2]])
                yield
                if main:
                    TT('dve', ytl[:L, :].rearrange("p (h e) -> p h e", h=8), ps[2][:L, :].rearrange("p (h e) -> p h e", h=8),
                       sm[:L, 1, ti, :].unsqueeze(2).to_broadcast([L, 8, 64]), ALU.mult, [Bps[2], Bsm], [Bytl])
                for g in range(2):
                    MM(ps[3][:, g * 256:(g + 1) * 256], xsB[:L, 512 + g * 128:512 + (g + 1) * 128],
                       xde[:L, g * 4:(g + 1) * 4, :].rearrange("p h e -> p (h e)"), g == 0, g == 1, [BxsB, Bxde], [Bps[3]])
                TT('dve', stf[:], stf[:], sm[:, 3, ti, :].unsqueeze(2).to_broadcast([128, 8, 64]), ALU.mult, [Bstf, Bsm], [Bstf])
                yield
                TT('dve', stf[:].rearrange("p h e -> p (h e)"), stf[:].rearrange("p h e -> p (h e)"), ps[3][:, :], ALU.add,
                   [Bstf, Bps[3]], [Bstf])
                CP('act', stb[:, :], stf[:].rearrange("p h e -> p (h e)"), [Bstf], [Bstb])
                yield 'S'
                if not main:
                    return
                for hb in range(2):
                    bank = ps[6 + hb]
                    o = bank[:, :].rearrange("p (h t) -> p h t", h=4)[:L, :, :L]
                    for h4 in range(4):
                        MM(o[:, h4, :], av[:L, hb * 4 + h4:hb * 4 + h4 + 1].to_broadcast([L, L]), triu[:L, :L], h4 == 0, False,
                           [Bc, Bav], [Bps[6 + hb]])
                    if L == 128:
                        MM(o, identb[:L, :L], negtri8[:L, hb * 4:(hb + 1) * 4, :L], False, True, [Bc], [Bps[6 + hb]])
                    else:
                        for h4 in range(4):
                            MM(o[:, h4, :], identb[:L, :L], negtri8[:L, hb * 4 + h4, :L], False, h4 == 3, [Bc], [Bps[6 + hb]])
                for g in range(2):
                    MM(ps[5][:L, 256 + g * 128:256 + g * 128 + L], xbcT[:, 4 + g, cs], xbcT[:, 6 + g, cs], True, True,
                       [Bxbc[4 + g], Bxbc[6 + g]], [Bps[5]])
                yield
                for h in range(8):
                    bank = ps[6 + h // 4]
                    o = bank[:, :].rearrange("p (h t) -> p h t", h=4)[:L, h % 4, :L]
                    ACT(Lm[:L, h, :L], o, AF.Exp, [Bps[6 + h // 4], Bsm], [BLm], bias=sm[:L, 0, ti, h:h + 1])
                    if h == 3:
                        yield
                yield
                cbv = ps[5][:, 256:512].rearrange("p (g t) -> p g t", g=2)[:L, :, :L]
                TT('dve', Mb[:L, :, :L].rearrange("p (g h) t -> p g h t", g=2),
                   Lm[:L, :, :L].rearrange("p (g h) t -> p g h t", g=2),
                   cbv.unsqueeze(2).to_broadcast([L, 2, 4, L]), ALU.mult, [BLm, Bps[5]], [BMb])
                yield
                for h in range(8):
                    MM(ps[1][:L, h * 64:(h + 1) * 64], Mb[:L, h, :L], xdt[:L, h, :], h == 0, False, [BMb, Bxdt], [Bps[1]])
                MM(ps[1][:L, :], identb[:L, :L], xsk[:L].rearrange("p h e -> p (h e)"), False, True, [Bc, Bxsk], [Bps[1]])
                for c in range(8):
                    MM(ps[4][:L, :], xnT[:, c, cs], w_in_sb[:, c, 1536:2048], c == 0, c == 7, [BxT, Bwin], [Bps[4]])
                yield
                TT('dve', ytl[:L, :], ytl[:L, :], ps[1][:L, :], ALU.add, [Bytl, Bps[1]], [Bytl])
                ACT(zs[:L, :], ps[4][:L, :], AF.Tanh, [Bps[4]], [Bzs], scale=0.5)
                yield
                STT(zs[:L, :], zs[:L, :], 1.0, ps[4][:L, :], ALU.add, ALU.mult, [Bzs, Bps[4]], [Bzs])
                yield
                STT(ytl[:L, :], zs[:L, :], 0.5, ytl[:L, :], ALU.mult, ALU.mult, [Bzs, Bytl], [Bytl])
                yield
                for g in range(2):
                    ACT(zs[:L, g * 256:(g + 1) * 256], ytl[:L, g * 256:(g + 1) * 256], AF.Square, [Bytl], [Bzs, Bss2],
                        accum_out=ss2[:L, g:g + 1])
                yield
                TS('dve', ss2[:L, 0:2], ss2[:L, 0:2], 1.0 / 256, EPS, ALU.mult, ALU.add, [Bss2], [Bss2])
                yield
                TT('pool', ss2[:L, 2:4], ss2[:L, 0:2], mhalf[:L, :].to_broadcast([L, 2]), ALU.pow, [Bss2, Bc], [Bss2])
                yield
                for g in range(2):
                    STT(ynb[:L, g * 256:(g + 1) * 256], ytl[:L, g * 256:(g + 1) * 256], ss2[:L, 2 + g:3 + g],
                        gssm_b[:L, g * 256:(g + 1) * 256], ALU.mult, ALU.mult, [Bytl, Bss2, Bc], [Bynb])
                yield
                for c in range(4):
                    TR(psT[:, c * 128:c * 128 + L], ynb[:L, c * 128:(c + 1) * 128], identb[:L, :L], [Bynb, Bc], [BpsT])
                mixt_ = mixT if mixt is None else mixt
                CP('act', mixt_[:, 4:8, mix_cols], psT[:, 0:512].rearrange("p (c t) -> p c t", c=4)[:, :, 0:L], [BpsT],
                   [BmixS if Bmix is None else Bmix])
                yield

            def run_chunks(gens, width=2):
                gens = list(gens)
                active = []
                nxt = [0]
                ready = [True]

                def start():
                    if nxt[0] < len(gens) and len(active) < width and ready[0]:
                        active.append(gens[nxt[0]])
                        nxt[0] += 1
                        ready[0] = False
                start()
                while active or nxt[0] < len(gens):
                    if not active:
                        ready[0] = True
                        start()
                    for g in list(active):
                        try:
                            r = next(g)
                        except StopIteration:
                            active.remove(g)
                            start()
                            continue
                        if r == 'S':
                            ready[0] = True
                            start()

            def emit_state(dst_ap):
                for c in range(4):
                    TR(ps[4][:, c * 128:(c + 1) * 128], stf[:, 2 * c:2 * c + 2, :].rearrange("p h e -> p (h e)"), identf[:, :],
                       [Bstf, Bc], [Bps[4]])
                CP('dve', sto[:].rearrange("p c n -> p (c n)"), ps[4][:, :], [Bps[4]], [Bsto])
                S.dma(dst_ap.rearrange("(c p) n -> p c n", p=128), sto[:], reads=[Bsto])

            def a1_supertile(t0, T, kind):
                nt = T // 128
                main = kind != 'prefix'
                for i in range(nt):
                    norm_transpose_tile(xl[t0 + i * 128:t0 + (i + 1) * 128, :], 128, slice(i * 128, (i + 1) * 128), BxnT[i])
                BxT_all = BxnT[:nt]
                nx = 8 if main else 6
                chunks = [('x', j, 2048 + j * 128) for j in range(8)]
                if main:
                    chunks += [('q', c, c * 128) for c in range(4)]
                chunks += [('k', c, 512 + c * 128) for c in range(4)]
                chunks += [('v', c, 1024 + c * 128) for c in range(4)]
                for ci, (knd, idx, col0) in enumerate(chunks):
                    bi = 1 + ci % 3
                    for c in range(8):
                        MM(ps[bi][:, 0:T], w_in_sb[:, c, col0:col0 + 128], xnT[:, c, 0:T], c == 0, c == 7, BxT_all + [Bwin], [Bps[bi]])
                    if knd == 'x':
                        eng = 'act' if (idx % 2 == 0) else 'dve'
                        CP(eng, cbuf[:, idx, 3:3 + T], ps[bi][:, 0:T], [Bps[bi]], [Bcb[idx]])
                    else:
                        si = ev_ctr[0] % 3
                        ev_ctr[0] += 1
                        if knd == 'q':
                            ACT(stg[si][:, 0:T], ps[bi][:, 0:T], AF.Copy, [Bps[bi]], [Bstg[si]], scale=0.125)
                            S.dma(qT_d[idx, :, t0 - NPRE:t0 - NPRE + T], stg[si][:, 0:T], reads=[Bstg[si]], writes=[Bqd])
                        else:
                            CP('dve' if knd == 'k' else 'act', stg[si][:, 0:T], ps[bi][:, 0:T], [Bps[bi]], [Bstg[si]])
                            dst = kT_d if knd == 'k' else vT_d
                            S.dma(dst[idx, :, t0:t0 + T], stg[si][:, 0:T], reads=[Bstg[si]], writes=[Bkd if knd == 'k' else Bvd])
                if kind == 'main':
                    for i in range(nt):
                        for wi, (col0, dst) in enumerate(((512, k_loc), (1024, v_loc))):
                            for c in range(8):
                                MM(ps[4][:, :], xnT[:, c, i * 128:(i + 1) * 128], w_in_sb[:, c, col0:col0 + 512], c == 0, c == 7,
                                   [BxnT[i], Bwin], [Bps[4]])
                            CP('dve' if wi == 0 else 'act', kvo[wi][:, :], ps[4][:, :], [Bps[4]], [Bkvo[wi]])
                            r0 = t0 - NPRE + i * 128
                            S.dma(dst[r0:r0 + 128, :], kvo[wi][:, :], reads=[Bkvo[wi]])
                for i in range(nt):
                    for c in range(8):
                        MM(ps[5][:, i * 8:(i + 1) * 8], xnT[:, c, i * 128:(i + 1) * 128], w_in_sb[:, c, 3072:3080], c == 0, c == 7,
                           [BxnT[i], Bwin], [Bps[5]])
                dv = dtall[:, 0:nt, :]
                TT('dve', dv, ps[5][:, 0:nt * 8].rearrange("p (i h) -> p i h", h=8), dtb[:].unsqueeze(1).to_broadcast([128, nt, 8]),
                   ALU.add, [Bps[5], Bc], [Bdt])
                ACT(dv, dv, AF.Exp, [Bdt], [Bdt])
                ACT(dv, dv, AF.Ln, [Bdt, Bc], [Bdt], bias=onec[:, :])
                if kind == 'prefix':
                    TS('dve', dv, dv, pvt[:, 0:1], None, ALU.mult, None, [Bdt, Bc], [Bdt])
                for j in range(nx):
                    a_ = j % 2
                    ACT(acc[a_][:, 0:T], cbuf[:, j, 3:3 + T], AF.Identity, [Bcb[j], Bc], [Bacc[a_]], scale=cwT[:, j, 3:4], bias=cbT[:, j:j + 1])
                    for k in range(3):
                        STT(acc[a_][:, 0:T], cbuf[:, j, k:k + T], cwT[:, j, k:k + 1], acc[a_][:, 0:T], ALU.mult, ALU.add,
                            [Bcb[j], Bc, Bacc[a_]], [Bacc[a_]])
                    ACT(th[a_][:, 0:T], acc[a_][:, 0:T], AF.Tanh, [Bacc[a_]], [Bth[a_]])
                    STT(xbcT[:, j, 0:T], th[a_][:, 0:T], 1.0, acc[a_][:, 0:T], ALU.add, ALU.mult, [Bth[a_], Bacc[a_]], [Bxbc[j]])
                if kind == 'main' and t0 + T == NPRE + NMAIN:
                    for hf in range(2):
                        for j in range(4):
                            TR(ps[4][0:3, j * 128:(j + 1) * 128], cbuf[:, hf * 4 + j, T:T + 3], identf[:, :], [Bcb[hf * 4 + j], Bc], [Bps[4]])
                        CP('dve', kvo[hf][0:3, :], ps[4][0:3, :], [Bps[4]], [Bkvo[hf]])
                        S.dma(conv_loc[:, hf * 512:(hf + 1) * 512], kvo[hf][0:3, :], reads=[Bkvo[hf]])
                for j in range(8):
                    CP('pool', cbuf[:, j, 0:3], cbuf[:, j, T:T + 3], [Bcb[j]], [Bcb[j]])
                ssd_prep(128, nt, dtall[:, 0:nt, :], main)
                run_chunks([ssd_gen(128, slice(i * 128, (i + 1) * 128), BxnT[i], 'main' if main else 'prefix', dtall[:, i, :],
                                    slice(t0 - NPRE + i * 128, t0 - NPRE + (i + 1) * 128), i, i % 2) for i in range(nt)])
                if kind == 'main' and t0 + T == NPRE + NMAIN:
                    emit_state(ssm_loc)

            Bqd, Bkd, Bvd = Buf("qd"), Buf("kd"), Buf("vd")
            st_list = [(s * 512, 512, 'prefix') for s in range(4)] + [(NPRE + s * 512, 512, 'main') for s in range(4)] + \
                      [(NPRE + NMAIN, 128, 'halo')]
            if skip_p:
                st_list = []
                for c_ in range(128):
                    bm_dma(c_)
            for si_, (t0, T, kind) in enumerate(st_list):
                a1_supertile(t0, T, kind)
                for c_ in range(si_ * 16, min(128, si_ * 16 + 16)):
                    bm_dma(c_)


            if not skip_a1:
                scb = sbt(sa, "scb", [128, 8, 4, 11], F32)
                Bscb = [Buf() for _ in range(8)]
                sc3 = sbt(sa, "sc3", [128, 8, 12], F32)
                Bsc3 = Buf()
                hs12 = xt[0]
                Bhs12 = Bxt[0]
                norm_transpose_tile(xs_d[:, :], 32, slice(0, 32), BxnT[0])
                if ks1 == 0.1:
                    S.barrier(); S.finish(); return nc
                schunks = [('q', c, c * 128) for c in range(4)] + [('k', c, 512 + c * 128) for c in range(4)] + \
                          [('x', j, 2048 + j * 128) for j in range(8)]
                for ci, (knd, idx, col0) in enumerate(schunks):
                    bi = 1 + ci % 3
                    for c in range(8):
                        MM(ps[bi][:, 0:32], w_in_sb[:, c, col0:col0 + 128], xnT[:, c, 0:32], c == 0, c == 7, [BxnT[0], Bwin], [Bps[bi]])
                    if knd == 'q':
                        ACT(sQT[0:64, idx, 0, :], ps[bi][0:64, 0:32], AF.Copy, [Bps[bi]], [Bsq], scale=0.125)
                        ACT(sQT[64:128, idx, 1, :], ps[bi][64:128, 0:32], AF.Copy, [Bps[bi]], [Bsq], scale=0.125)
                    elif knd == 'k':
                        CP('dve', sKT[:, idx, :], ps[bi][:, 0:32], [Bps[bi]], [Bsq])
                    else:
                        CP('act' if idx % 2 == 0 else 'dve', scb[:, idx, :, 3:11], ps[bi][:, 0:32].rearrange("p (b t) -> p b t", b=4),
                           [Bps[bi]], [Bscb[idx]])
                if ks1 == 0.2:
                    S.barrier(); S.finish(); return nc
                kvm = os.environ.get('KV', 'all')
                for wi, (col0, dst) in enumerate(((512, ks_d), (1024, vs_d))):
                    for c in range(8):
                        MM(ps[4][0:32, :], xnT[:, c, 0:32], w_in_sb[:, c, col0:col0 + 512], c == 0, c == 7, [BxnT[0], Bwin], [Bps[4]])
                    if kvm in ('all', 'cp', 'cpdma', 'cpsvn'):
                        CP('dve', kvo[wi][0:32, :], ps[4][0:32, :], [Bps[4]], [Bkvo[wi]])
                    if wi == 1 and kvm in ('all', 'cpsvn'):
                        CP('act', sVn[0:32, :, 0:64], ps[4][0:32, :].rearrange("p (h e) -> p h e", h=8), [Bps[4]], [Bsq])
                    if kvm in ('all', 'cpdma'):
                        S.dma(dst[:, :], kvo[wi][0:32, :], reads=[Bkvo[wi]])
                if ks1 == 1:
                    S.barrier(); S.finish(); return nc
                S.dma(hs12[0:12, :], sconv_d[:, :], writes=[Bhs12])
                for hf in range(2):
                    for j in range(4):
                        TR(ps[4][:, j * 12:(j + 1) * 12], hs12[0:12, (hf * 4 + j) * 128:(hf * 4 + j + 1) * 128], identf[0:12, 0:12],
                           [Bhs12, Bc], [Bps[4]])
                    CP('dve', scb[:, hf * 4:(hf + 1) * 4, :, 0:3], ps[4][:, 0:48].rearrange("p (j b t) -> p j b t", j=4, b=4),
                       [Bps[4]], Bscb[hf * 4:(hf + 1) * 4])
                for j in range(8):
                    a_ = j % 2
                    accv = acc[a_][:, 0:32].rearrange("p (b t) -> p b t", b=4)
                    ACT(accv, scb[:, j, :, 3:11], AF.Identity, [Bscb[j], Bc], [Bacc[a_]], scale=cwT[:, j, 3:4], bias=cbT[:, j:j + 1])
                    for k in range(3):
                        STT(accv, scb[:, j, :, k:k + 8], cwT[:, j, k:k + 1], accv, ALU.mult, ALU.add, [Bscb[j], Bc, Bacc[a_]], [Bacc[a_]])
                    ACT(th[a_][:, 0:32], acc[a_][:, 0:32], AF.Tanh, [Bacc[a_]], [Bth[a_]])
                    STT(xbcT[:, j, 0:32], th[a_][:, 0:32], 1.0, acc[a_][:, 0:32], ALU.add, ALU.mult, [Bth[a_], Bacc[a_]], [Bxbc[j]])
                CP('pool', sc3[:].rearrange("p j (b t) -> p j b t", b=4), scb[:, :, :, 8:11], Bscb, [Bsc3])
                for hf in range(2):
                    for j in range(4):
                        TR(ps[4][0:12, j * 128:(j + 1) * 128], sc3[:, hf * 4 + j, :], identf[:, :], [Bsc3, Bc], [Bps[4]])
                    CP('dve', hs12[0:12, hf * 512:(hf + 1) * 512], ps[4][0:12, :], [Bps[4]], [Bhs12])
                S.dma(conv_s_d[:, :], hs12[0:12, :], reads=[Bhs12])
                if ks1 == 2:
                    S.barrier(); S.finish(); return nc
                for b in range(4):
                    for c in range(8):
                        MM(ps[5][0:8, b * 8:(b + 1) * 8], xnT[:, c, b * 8:(b + 1) * 8], w_in_sb[:, c, 3072:3080], c == 0, c == 7,
                           [BxnT[0], Bwin], [Bps[5]])
                dvs = dtall[0:8, 0:4, :]
                TT('dve', dvs, ps[5][0:8, 0:32].rearrange("p (i h) -> p i h", h=8), dtb[0:8, :].unsqueeze(1).to_broadcast([8, 4, 8]),
                   ALU.add, [Bps[5], Bc], [Bdt])
                ACT(dvs, dvs, AF.Exp, [Bdt], [Bdt])
                ACT(dvs, dvs, AF.Ln, [Bdt, Bc], [Bdt], bias=onec[0:8, :])
                if ks1 == 3:
                    S.barrier(); S.finish(); return nc
                ssd_prep(8, 4, dtall[0:8, 0:4, :], True)
                for b in range(4):
                    S.dma(sto[:], sssm_d[b].rearrange("(c p) n -> p c n", p=128), writes=[Bsto])
                    for c in range(4):
                        TR(ps[4][:, c * 128:(c + 1) * 128], sto[:, c, :], identf[:, :], [Bsto, Bc], [Bps[4]])
                    CP('dve', stf[:].rearrange("p h e -> p (h e)"), ps[4][:, :], [Bps[4]], [Bstf])
                    CP('act', stb[:, :], stf[:].rearrange("p h e -> p (h e)"), [Bstf], [Bstb])
                    for _ in ssd_gen(8, slice(b * 8, (b + 1) * 8), BxnT[0], 'main', dtall[0:8, b, :], slice(b * 8, (b + 1) * 8), b, b % 2,
                                     mixt=smixT, Bmix=BsmixS):
                        pass
                    emit_state(ssm_s_d[b])

        S.barrier()
        if stage <= 1:
            S.finish()
            return nc

        with ExitStack() as s2:
            sel = sbt(s2, "sel", [128, 64], F32)
            QT = [sbt(s2, f"QT{i}", [128, 2, NQ], BF16) for i in range(2)]
            KT = [sbt(s2, f"KT{i}", [128, NLOC], BF16) for i in range(2)]
            VT = [sbt(s2, f"VT{i}", [128, NLOC], BF16) for i in range(2)]
            Bqkv = [Buf() for _ in range(2)]
            NVB = 70
            Vb = sbt(s2, "Vb", [128, NVB, 2, 66], BF16)
            BVb = Buf()
            accA = sbt(s2, "accA", [128, 2, NQ], F32)
            BaccA = Buf()
            PT = [sbt(s2, f"PT{i}", [128, 512], BF16) for i in range(4)]
            BPT = [Buf() for _ in range(4)]
            rc = [sbt(s2, f"rc{i}", [64, 512], F32) for i in range(2)]
            Brc = [Buf() for _ in range(2)]
            TT('dve', Bdg[:], Bm[:, :, 0, 0:2], negd[:].unsqueeze(1).to_broadcast([128, 8, 2]), ALU.add, [BBm, Bc], [Bc])
            if stage == 1.2:
                S.barrier(); S.finish(); return nc
            wtmp = [sbt(s2, f"wtmp{i}", [128, 8, 256], BF16) for i in range(2)]
            Bwtmp = [Buf(f"wtmp{i}") for i in range(2)]
            Bwup_d = Buf("wup_d")
            w_up_v = w_up.rearrange("(c p) n -> p c n", p=128)

            def prep_wup(j):
                s = j % 2
                S.dma(wtmp[s][:, :, 0:128], w_up_v[:, :, j * 128:(j + 1) * 128], writes=[Bwtmp[s]], eng='pool')
                S.dma(wtmp[s][:, :, 128:256], w_up_v[:, :, DFF + j * 128:DFF + (j + 1) * 128], writes=[Bwtmp[s]], eng='pool')
                S.dma(wup_d[j, :, :], wtmp[s][:].rearrange("p c n -> p (c n)"), reads=[Bwtmp[s]], writes=[Bwup_d])


            wtb = [sbt(s2, f"wtb{i}", [128, D], BF16) for i in range(2)]
            Bwtb = [Buf() for _ in range(2)]
            BwB_d = Buf("wB_d")
            wB_src = [w_out[c * 128:(c + 1) * 128, :] for c in range(8)] + [w_down[j * 128:(j + 1) * 128, :] for j in range(NG)] + \
                     [w_ple_proj[c * 128:(c + 1) * 128, :] for c in range(2)] + [w_ple_gate[c * 128:(c + 1) * 128, :] for c in range(8)]

            fcwT = sbt(s2, "fcwT", [128, 2 * NG, 3], F32)
            Bfcw = Buf()
            for k in range(3):
                for hf in range(2):
                    S.dma(fcwT[:, hf * NG:(hf + 1) * NG, k], ffn_conv_w[k, hf * DFF:(hf + 1) * DFF].rearrange("(c p) -> p c", p=128),
                          writes=[Bfcw], allow_slow_non_contiguous=True)
            dgs = [sbt(s2, f"dgs{i}", [128, 2, 3, 128], BF16) for i in range(2)]
            Bdgs = [Buf() for _ in range(2)]
            Bdg_d = Buf("dg_d")

            def prep_dg(j):
                sl = j % 2
                for k in range(3):
                    ACT(dgs[sl][:, 0, k, :], identb[:, :], AF.Copy, [Bc, Bfcw], [Bdgs[sl]], scale=fcwT[:, j, k:k + 1])
                    TS('dve', dgs[sl][:, 1, k, :], identb[:, :], fcwT[:, NG + j, k:k + 1], None, ALU.mult, None, [Bc, Bfcw], [Bdgs[sl]])
                S.dma(dg_d[j, :, :], dgs[sl][:].rearrange("p a k n -> p (a k n)"), reads=[Bdgs[sl]], writes=[Bdg_d])

            def prep_wB(i):
                sl = i % 2
                S.dma(wtb[sl][:, :], wB_src[i], writes=[Bwtb[sl]], eng='pool')
                S.dma(wB_d[i, :, :], wtb[sl][:, :], reads=[Bwtb[sl]], writes=[BwB_d])

            sKc = sbt(s2, "sKc", [128, 4, 13 * 128], BF16)
            BsKc = Buf()
            sVc = sbt(s2, "sVc", [128, 13, 8, 66], BF16)
            BsVc = Buf()
            kst = [sbt(s2, f"kst{i}", [128, 512], F32) for i in range(2)]
            Bkst = [Buf() for _ in range(2)]
            vst = [sbt(s2, f"vst{i}", [128, 512], F32) for i in range(2)]
            Bvst = [Buf() for _ in range(2)]
            accS = sbt(s2, "accS", [128, 4, 2, 32], F32)
            BaccS = [Buf() for _ in range(4)]
            cmt = sbt(s2, "cmt", [32, 3, 32], F32)
            Bown = sbt(s2, "Bown", [32, 8, 3, 32], BF16)
            S.dma(cmt[:], c_cm[:, :, :], writes=[Bc])
            TT('dve', Bown[:], Bm[0:32, :, 0, 0:32].unsqueeze(2).to_broadcast([32, 8, 3, 32]),
               cmt[:].unsqueeze(1).to_broadcast([32, 8, 3, 32]), ALU.add, [Bc], [Bc])
            MEMSET('pool', sVc[:].rearrange("p t h e -> p (t h e)"), 1.0, [BsVc])
            MEMSET('dve', sel[:], 0.0, [Bc])
            MEMSET('dve', sel[64:65, :], 1.0, [Bc])
            MEMSET('dve', accA[:], 1.0, [BaccA])
            for i_ in range(2):
                MEMSET('pool', QT[i_][:].rearrange("p h q -> p (h q)"), 0.0, [Bqkv[i_]])
            MEMSET('pool', Vb[:].rearrange("p b h e -> p (b h e)"), 1.0, [BVb])
            for (a0, a1) in ((0, 1), (18, 22), (38, 54)):
                TS('dve', Vb[:, a0:a1, :, 64:65], Vb[:, a0:a1, :, 64:65], pvt[:, 0:1], None, ALU.mult, None, [BVb, Bc], [BVb])

            vblocks = []
            for tau in range(15, 33):
                vblocks.append((tau - 15, 128 * tau, 1))
            for sg in range(3, 8):
                for r in range(4):
                    vblocks.append((18 + (sg - 3) * 4 + r, 512 * sg + r, 4))
            for z_ in range(2):
                for r in range(16):
                    vblocks.append((38 + 16 * z_ + r, 2048 * z_ + r, 16))
            assert len(vblocks) == NVB

            cnt = {'s': 0, 'o': 0, 'pt': 0, 'n': 0, 'ev': 0}

            def cols(c0, step, n):
                return slice(c0, c0 + step * (n - 1) + 1, step) if step > 1 else slice(c0, c0 + n)

            def attn_core(N, nkeys, k_aps, q_fn, v_fn, bm_ap, acc_ap, mode, pv_rep, rK, rQ, rV, wAcc):
                nk = len(k_aps)
                bi = 1 + cnt['s'] % 3
                cnt['s'] += 1
                psv = ps[bi][:, :].rearrange("p (h k q) -> p h k q", h=2, k=2)
                first = True
                for h2 in range(2):
                    for k in range(nk):
                        MM(psv[:nkeys, h2, k, 0:N], k_aps[k], q_fn(h2), first, False, rK + rQ, [Bps[bi]])
                        first = False
                if N == 128 and nk == 2:
                    MM(psv[:nkeys, :, 0:nk, 0:N], identb[:nkeys, :nkeys], bm_ap, False, True, [Bc], [Bps[bi]])
                else:
                    for h2 in range(2):
                        for k in range(nk):
                            MM(psv[:nkeys, h2, k, 0:N], identb[:nkeys, :nkeys], bm_ap[:, h2, k, :], False, (h2 == 1 and k == nk - 1),
                               [Bc], [Bps[bi]])
                pi = cnt['pt'] % 4
                cnt['pt'] += 1
                ptv = PT[pi][:, :].rearrange("p (h k q) -> p h k q", h=2, k=2)
                ACT(ptv[:nkeys, :, 0:nk, 0:N], psv[:nkeys, :, 0:nk, 0:N], AF.Exp, [Bps[bi]], [BPT[pi]])
                def stage2():
                    oi = 4 + cnt['o'] % 2
                    cnt['o'] += 1
                    pso = ps[oi][0:65, 0:256].rearrange("p (h q) -> p h q", h=2)
                    first2 = True
                    for h2 in range(2):
                        tot = nk * pv_rep
                        ii = 0
                        for k in range(nk):
                            for _rep in range(pv_rep):
                                ii += 1
                                MM(pso[:, h2, 0:N], v_fn(k, h2), ptv[:nkeys, h2, k, 0:N], first2, ii == tot, rV + [BPT[pi]], [Bps[oi]])
                                first2 = False
                    if mode == 'copy':
                        CP('act', acc_ap, pso[:, :, 0:N], [Bps[oi]], wAcc)
                    else:
                        TT('dve', acc_ap, acc_ap, pso[:, :, 0:N], ALU.add, [Bps[oi]] + wAcc, wAcc)

                if pending:
                    pending.pop(0)()
                pending.append(stage2)

            pending = []

            def flush_pending():
                while pending:
                    pending.pop(0)()

            def attn_unit(p, s, N, qc, kbs, bm_ap, accc, mode, pv_rep=1):
                attn_core(N, 128, [KT[s][:, cols(kc0, kst, 128)] for (kc0, kst, _) in kbs],
                          lambda h2: QT[s][:, h2, cols(qc[0], qc[1], N)],
                          lambda k, h2: Vb[:, kbs[k][2], h2, 0:65], bm_ap,
                          accA[0:65, :, cols(accc[0], accc[1], N)], mode, pv_rep, [Bqkv[s]], [], [BVb], [BaccA])

            def bm_std(p, br, N):
                return Bm[:, 2 * p:2 * p + 2, br, :].rearrange("p h (k q) -> p h k q", k=2)[:, :, :, 0:N]

            def bm_prev(p, br, N):
                return Bm[:, 2 * p:2 * p + 2, br, 128:128 + N].unsqueeze(2)

            for p in range(4):
                s = p % 2
                for j in range(p * 6, min(NG, p * 6 + 6)):
                    prep_wup(j)
                for j in range(p * 10, p * 10 + 10):
                    prep_wB(j)
                for j in range(p * 6, min(NG, p * 6 + 6)):
                    prep_dg(j)
                S.dma(QT[s][0:64, 0, :], qT_d[p, 0:64, :], reads=[Bqd], writes=[Bqkv[s]])
                S.dma(QT[s][64:128, 1, :], qT_d[p, 64:128, :], reads=[Bqd], writes=[Bqkv[s]])
                S.dma(KT[s][:, :], kT_d[p, :, :], reads=[Bkd], writes=[Bqkv[s]])
                S.dma(VT[s][:, :], vT_d[p, :, :], reads=[Bvd], writes=[Bqkv[s]])
                for g0 in range(0, NVB, 8):
                    grp = vblocks[g0:g0 + 8]
                    for sl, (vbi, c0, st_) in enumerate(grp):
                        TR(psT[:, sl * 128:(sl + 1) * 128], VT[s][:, cols(c0, st_, 128)], identb[:, :], [Bqkv[s], Bc], [BpsT])
                    n = len(grp)
                    eng = 'act' if (cnt['ev'] % 2 == 0) else 'dve'
                    cnt['ev'] += 1
                    CP(eng, Vb[:, g0:g0 + n, :, 0:64], psT[:, 0:n * 128].rearrange("p (b h e) -> p b h e", b=n, h=2), [BpsT], [BVb])
                if stage == 1.4:
                    S.barrier(); S.finish(); return nc
                for n in range(16):
                    attn_unit(p, s, 128, (128 * n, 1), [(NPRE + 128 * n, 1, n + 1), (NPRE + 128 * (n - 1), 1, n)],
                              bm_std(p, 0, 128), (128 * n, 1), 'copy')
                attn_unit(p, s, 2, (2048, 1), [(NPRE + 2048, 1, 17), (NPRE + 1920, 1, 16)], bm_std(p, 0, 2), (2048, 1), 'copy')
                if stage == 1.6:
                    S.barrier(); S.finish(); return nc
                for sg in range(4):
                    for r in range(4):
                        attn_unit(p, s, 128, (512 * sg + r, 4),
                                  [(NPRE + 512 * sg + r, 4, 18 + (sg + 1) * 4 + r), (NPRE + 512 * (sg - 1) + r, 4, 18 + sg * 4 + r)],
                                  bm_std(p, 1, 128), (512 * sg + r, 4), 'add')
                for r in range(16):
                    attn_unit(p, s, 128, (r, 16), [(NPRE + r, 16, 38 + 16 + r), (r, 16, 38 + r)], bm_std(p, 2, 128), (r, 16), 'add')
                for qi in range(2):
                    attn_unit(p, s, 1, (2048 + qi, 1), [(NPRE + 1536 + qi, 4, 18 + 16 + qi)], bm_prev(p, 1, 1), (2048 + qi, 1), 'add')
                    attn_unit(p, s, 1, (2048 + qi, 1), [(NPRE + qi, 16, 38 + 16 + qi)], bm_prev(p, 2, 1), (2048 + qi, 1), 'add')
                attn_unit(p, s, 2, (2048, 1), [(NPRE + 2048, 1, 17)], Bdg[:, 2 * p:2 * p + 2, 0:2].unsqueeze(2), (2048, 1), 'add', pv_rep=2)
                flush_pending()
                for h2 in range(2):
                    for c0 in range(0, NQ, 512):
                        n = min(512, NQ - c0)
                        bi = 6 + cnt['n'] % 2
                        ri = cnt['n'] % 2
                        cnt['n'] += 1
                        MM(ps[bi][0:64, 0:n], sel[0:65, 0:64], accA[0:65, h2, c0:c0 + n], True, True, [Bc, BaccA], [Bps[bi]])
                        S.op('dve', lambda e, o=rc[ri][0:64, 0:n], i_=ps[bi][0:64, 0:n]: e.reciprocal(out=o, in_=i_), [Bps[bi]], [Brc[ri]])
                        TT('dve', mixT[h2 * 64:(h2 + 1) * 64, p, c0:c0 + n], accA[0:64, h2, c0:c0 + n], rc[ri][0:64, 0:n], ALU.mult,
                           [BaccA, Brc[ri]], [BmixA[p]])

            if not skip_a1:
                for p in range(4):
                    for br in range(3):
                        attn_core(32, 32, [sKT[:, p, 0:32]], lambda h2, p=p: sQT[:, p, h2, 0:32],
                                  lambda k, h2, p=p: sVn[0:32, 2 * p + h2, 0:65], Bown[0:32, 2 * p:2 * p + 2, br, :].unsqueeze(2),
                                  accS[0:65, p, :, :], 'copy' if br == 0 else 'add', 1, [Bsq], [], [Bsq], [BaccS[p]])
                flush_pending()
                tix = [0]
                for b in range(4):
                    tiles = [(1920, 1)] + [(1536 + r, 4) for r in range(4)] + [(r, 16) for r in range(8)]
                    for ti, (r0, st_) in enumerate(tiles):
                        sl = tix[0] % 2
                        tix[0] += 1
                        rows = slice(r0, r0 + 127 * st_ + 1, st_) if st_ > 1 else slice(r0, r0 + 128)
                        S.dma(kst[sl][:, :], ck_d[b, rows, :], writes=[Bkst[sl]])
                        S.dma(vst[sl][:, :], cv_d[b, rows, :], writes=[Bvst[sl]])
                        for c in range(4):
                            TR(ps[7][:, c * 128:(c + 1) * 128], kst[sl][:, c * 128:(c + 1) * 128], identf[:, :], [Bkst[sl], Bc], [Bps[7]])
                        CP('act' if ti % 2 == 0 else 'dve', sKc[:, :, ti * 128:(ti + 1) * 128],
                           ps[7][:, :].rearrange("p (c k) -> p c k", c=4), [Bps[7]], [BsKc])
                        CP('dve' if ti % 2 == 0 else 'act', sVc[:, ti, :, 0:64], vst[sl][:, :].rearrange("p (h e) -> p h e", h=8), [Bvst[sl]], [BsVc])
                    for p in range(4):
                        def unit(ti, N, qcol0, qstep, br, p=p, b=b):
                            attn_core(N, 128, [sKc[:, p, ti * 128:(ti + 1) * 128]],
                                      lambda h2: sQT[:, p, h2, cols(b * 8 + qcol0, qstep, N)],
                                      lambda k, h2: sVc[:, ti, 2 * p + h2, 0:65], bm_prev(p, br, N),
                                      accS[0:65, p, :, cols(b * 8 + qcol0, qstep, N)], 'add', 1, [BsKc], [Bsq], [BsVc], [BaccS[p]])
                        unit(0, 8, 0, 1, 0)
                        for r in range(4):
                            unit(1 + r, 2, r, 4, 1)
                        for r in range(8):
                            unit(5 + r, 1, r, 1, 2)
                    flush_pending()
                for p in range(4):
                    for h2 in range(2):
                        bi = 6 + cnt['n'] % 2
                        ri = cnt['n'] % 2
                        cnt['n'] += 1
                        MM(ps[bi][0:64, 0:32], sel[0:65, 0:64], accS[0:65, p, h2, :], True, True, [Bc, BaccS[p]], [Bps[bi]])
                        S.op('dve', lambda e, o=rc[ri][0:64, 0:32], i_=ps[bi][0:64, 0:32]: e.reciprocal(out=o, in_=i_), [Bps[bi]], [Brc[ri]])
                        TT('dve', smixT[h2 * 64:(h2 + 1) * 64, p, :], accS[0:64, p, h2, :], rc[ri][0:64, 0:32], ALU.mult,
                           [BaccS[p], Brc[ri]], [BsmixA[p]])
        sA.close()
        S.barrier()
        if stage <= 2:
            S.finish()
            return nc

        with ExitStack() as s3:
            w_out_sb = sbt(s3, "w_out_sb", [128, 8, D], BF16)
            w_dn_sb = sbt(s3, "w_dn_sb", [128, NG, D], BF16)
            w_pp_sb = sbt(s3, "w_pp_sb", [128, 2, D], BF16)
            w_pg_sb = sbt(s3, "w_pg_sb", [128, 8, D], BF16)
            BwB = Buf()
            S.dma(w_out_sb[:], wB_d[0:8, :, :].rearrange("c p n -> p c n"), reads=[BwB_d], writes=[BwB])
            S.dma(w_dn_sb[:], wB_d[8:30, :, :].rearrange("c p n -> p c n"), reads=[BwB_d], writes=[BwB])
            S.dma(w_pp_sb[:], wB_d[30:32, :, :].rearrange("c p n -> p c n"), reads=[BwB_d], writes=[BwB])
            S.dma(w_pg_sb[:], wB_d[32:40, :, :].rearrange("c p n -> p c n"), reads=[BwB_d], writes=[BwB])
            fcbT = sbt(s3, "fcbT", [128, 2 * NG], F32)
            gple_b = sbt(s3, "gple_b", [128, D], F32)
            gfin_b = sbt(s3, "gfin_b", [128, D], F32)
            for hf in range(2):
                S.dma(fcbT[:, hf * NG:(hf + 1) * NG], ffn_conv_b[hf * DFF:(hf + 1) * DFF].rearrange("(c p) -> p c", p=128),
                      writes=[Bc], allow_slow_non_contiguous=True)
            S.dma(gple_b[:], g_ple.partition_broadcast(128), writes=[Bc])
            S.dma(gfin_b[:], g_final.partition_broadcast(128), writes=[Bc])

            TB = 256
            xtb = [sbt(s3, f"xtb{i}", [128, D], F32) for i in range(2)]
            Bxtb = [Buf() for _ in range(2)]
            ptb = [sbt(s3, f"ptb{i}", [128, 256], F32) for i in range(2)]
            Bptb = [Buf() for _ in range(2)]
            hh = sbt(s3, "hh", [128, 2, D], F32)
            Bhh = [Buf() for _ in range(2)]
            hn2 = [sbt(s3, f"hn{i}", [128, D], BF16) for i in range(2)]
            Bhn2 = [Buf() for _ in range(2)]
            rsb2 = [sbt(s3, f"rsb{i}", [128, 8], F32) for i in range(2)]
            Brsb2 = [Buf() for _ in range(2)]

            def lockstep(gens):
                gens = list(gens)
                while gens:
                    for g in list(gens):
                        try:
                            next(g)
                        except StopIteration:
                            gens.remove(g)

            hnT = sbt(s3, "hnT", [128, 8, TB], BF16)
            BhnT = [Buf() for _ in range(2)]
            wg = [sbt(s3, f"wg{i}", [128, 8, 256], BF16) for i in range(3)]
            Bwg = [Buf() for _ in range(3)]
            ub = [sbt(s3, f"ub{i}", [128, 2, TB + 2], BF16) for i in range(2)]
            Bub = [Buf() for _ in range(2)]
            hist = sbt(s3, "hist", [128, NG, 2, 2], BF16)
            Bhist = [Buf() for _ in range(NG)]
            ffo = sbt(s3, "ffo", [128, 2, NG, 2], F32)
            Bffo = Buf()
            ffs = sbt(s3, "ffs", [8, 512], F32)
            ffo_s = sbt(s3, "ffo_s", [128, 2, NG, 4, 2], F32)
            hist_s = sbt(s3, "hist_s", [128, 2, NG, 4, 2], BF16)
            Bhist_s = Buf()
            hs8 = sbt(s3, "hs8", [8, 512], F32)
            Bhs8 = Buf()
            Bffs = Buf()
            dg = [sbt(s3, f"dg{i}", [128, 2, 3, 128], BF16) for i in range(4)]
            Bdgm = [Buf() for _ in range(4)]
            sa_ = [sbt(s3, f"sa{i}", [128, TB], F32) for i in range(2)]
            Bsa = [Buf() for _ in range(2)]
            gT = sbt(s3, "gT", [128, NG, TB], BF16)
            BgT = Buf()
            ppb2 = [sbt(s3, f"ppb{i}", [128, 256], BF16) for i in range(2)]
            Bppb2 = [Buf() for _ in range(2)]
            ppT2 = [sbt(s3, f"ppT{i}", [128, 2, 128], BF16) for i in range(2)]
            BppT2 = [Buf() for _ in range(2)]
            t12 = xtb
            Bt12 = Bxtb
            h2T2 = [sbt(s3, f"h2T{i}", [128, 8, 128], BF16) for i in range(2)]
            Bh2T2 = [Buf() for _ in range(2)]
            gt2 = [sbt(s3, f"gt{i}", [128, D], F32) for i in range(2)]
            Bgt2 = [Buf() for _ in range(2)]
            MEMSET('dve', hist[:].rearrange("p j a t -> p (j a t)"), 0.0, Bhist)

            tcb = [0]

            def b_supertile(t0, T, samp=False):
                L = 32 if samp else 128
                nt = 1 if samp else T // 128
                mixsrc = smixT if samp else mixT
                Bmixsrc = ([BsmixS] + BsmixA) if samp else ([BmixS] + BmixA)
                def head_gen(i):
                    s = i % 2
                    hn, Bhn, rsb, Brsb = hn2[i], Bhn2[i], rsb2[i], Brsb2[i]
                    r0 = t0 + i * 128
                    xsrc = xs_d[0:32, :] if samp else xl[NPRE + r0:NPRE + r0 + 128, :]
                    S.dma(xtb[s][:L, :], xsrc, writes=[Bxtb[s]])
                    for hf in range(2):
                        bk = 1 + 2 * (i % 2) + hf
                        for c in range(8):
                            MM(ps[bk][:L, :], mixsrc[:, c, r0:r0 + L], w_out_sb[:, c, hf * 512:(hf + 1) * 512], c == 0, c == 7,
                               [BwB] + Bmixsrc, [Bps[bk]])
                        TT('dve', hh[:L, i, hf * 512:(hf + 1) * 512], xtb[s][:L, hf * 512:(hf + 1) * 512], ps[bk][:L, :], ALU.add,
                           [Bxtb[s], Bps[bk]], [Bhh[i]])
                        yield
                    rms_rstd(hh[:L, i, :], L, hn[:L, :], rsb[:L, 0:1], rsb[:L, 1:2], [Bhh[i]], [Bhn, Brsb], 1.0 / D)
                    yield
                    ACT(hn[:L, :], hh[:L, i, :], AF.Copy, [Bhh[i], Brsb], [Bhn], scale=rsb[:L, 1:2])
                    yield
                    for c in range(8):
                        TR(psT[:, c * 128:c * 128 + L], hn[:L, c * 128:(c + 1) * 128], identb[:L, :L], [Bhn, Bc], [BpsT])
                    TT('dve', hnT[:, :, i * 128:i * 128 + L], psT[:, :].rearrange("p (c t) -> p c t", c=8)[:, :, 0:L],
                       gffnT[:].unsqueeze(2).to_broadcast([128, 8, L]), ALU.mult, [BpsT, Bc], [BhnT[i]])
                    yield

                lockstep([head_gen(i) for i in range(nt)])
                last_main = (t0 + T == NMAIN) or samp

                def up_stage(j):
                    ws = j % 3
                    S.dma(wg[ws][:].rearrange("p c n -> p (c n)"), wup_d[j, :, :], reads=[Bwup_d], writes=[Bwg[ws]])
                    us = j % 2
                    if samp:
                        ubv = ub[us][:, :, 0:40].rearrange("p a (b t) -> p a b t", b=4)
                        CP('pool', ubv[:, :, :, 0:2], hist_s[:, :, j, :, :], [Bhist_s], [Bub[us]])
                    else:
                        CP('pool', ub[us][:, :, 0:2], hist[:, j, :, :], [Bhist[j]], [Bub[us]])
                    for ab in range(2):
                        ubk = (3 + ab) if j % 2 == 0 else (1 + ab)
                        for c in range(8):
                            MM(ps[ubk][:, 0:T], wg[ws][:, c, ab * 128:(ab + 1) * 128], hnT[:, c, 0:T], c == 0, c == 7,
                               [Bwg[ws]] + BhnT[:nt], [Bps[ubk]])
                        if samp:
                            pv4 = ps[ubk][:, 0:32].rearrange("p (b t) -> p b t", b=4)
                            CP('act' if ab == 0 else 'dve', ubv[:, ab, :, 2:10], pv4, [Bps[ubk]], [Bub[us]])
                            CP('dve', ffo_s[:, ab, j, :, :], pv4[:, :, 6:8], [Bps[ubk]], [Bffo])
                        else:
                            CP('act' if ab == 0 else 'dve', ub[us][:, ab, 2:2 + T], ps[ubk][:, 0:T], [Bps[ubk]], [Bub[us]])
                            if last_main:
                                CP('dve', ffo[:, ab, j, :], ps[ubk][:, T - 2:T], [Bps[ubk]], [Bffo])
                    if not samp:
                        CP('pool', hist[:, j, :, :], ub[us][:, :, T:T + 2], [Bub[us]], [Bhist[j]])
                    S.dma(dg[j % 4][:].rearrange("p a k n -> p (a k n)"), dg_d[j, :, :], reads=[Bdg_d], writes=[Bdgm[j % 4]])

                def conv_stage(j):
                    us = j % 2
                    for ab in range(2):
                        for k in range(3):
                            if samp:
                                ubv = ub[us][:, :, 0:40].rearrange("p a (b t) -> p a b t", b=4)
                                rhs = ubv[:, ab, :, k:k + 8]
                                o_ = ps[5 + ab][:, 0:32].rearrange("p (b t) -> p b t", b=4)
                            else:
                                rhs = ub[us][:, ab, k:k + T]
                                o_ = ps[5 + ab][:, 0:T]
                            MM(o_, dg[j % 4][:, ab, k, :], rhs, k == 0, k == 2, [Bdgm[j % 4], Bub[us]], [Bps[5 + ab]])
                    ACT(sa_[us][:, 0:T], ps[5][:, 0:T], AF.Silu, [Bps[5], Bc], [Bsa[us]], bias=fcbT[:, j:j + 1])
                    STT(gT[:, j, 0:T], ps[6][:, 0:T], fcbT[:, NG + j:NG + j + 1], sa_[us][:, 0:T], ALU.add, ALU.mult,
                        [Bps[6], Bc, Bsa[us]], [BgT])

                for j in range(NG):
                    up_stage(j)
                    if j > 0:
                        conv_stage(j - 1)
                conv_stage(NG - 1)

                def tail_gen(i):
                    s = i % 2
                    hn, Bhn, rsb, Brsb = hn2[i], Bhn2[i], rsb2[i], Brsb2[i]
                    ppb, Bppb, ppT, BppT = ppb2[i], Bppb2[i], ppT2[i], BppT2[i]
                    t1, Bt1, h2T, Bh2T, gt, Bgt = t12[i], Bt12[i], h2T2[i], Bh2T2[i], gt2[i], Bgt2[i]
                    b1 = [1 + 2 * (i % 2), 2 + 2 * (i % 2)]
                    b2 = [5, 6] if i % 2 == 0 else [7, 6]
                    r0 = t0 + i * 128
                    psrc = psm_d[0:32, :] if samp else pl[r0:r0 + 128, :]
                    S.dma(ptb[s][:L, :], psrc, writes=[Bptb[s]])
                    for hf in range(2):
                        for j in range(NG):
                            MM(ps[b1[hf]][:L, :], gT[:, j, i * 128:i * 128 + L], w_dn_sb[:, j, hf * 512:(hf + 1) * 512], j == 0, j == NG - 1,
                               [BgT, BwB], [Bps[b1[hf]]])
                        TT('dve', hh[:L, i, hf * 512:(hf + 1) * 512], hh[:L, i, hf * 512:(hf + 1) * 512], ps[b1[hf]][:L, :], ALU.add,
                           [Bhh[i], Bps[b1[hf]]], [Bhh[i]])
                        yield
                    CP('act', ppb[:L, :], ptb[s][:L, :], [Bptb[s]], [Bppb])
                    yield
                    for c in range(2):
                        TR(psT[:, c * 128:c * 128 + L], ppb[:L, c * 128:(c + 1) * 128], identb[:L, :L], [Bppb, Bc], [BpsT])
                    CP('dve', ppT[:, :, 0:L], psT[:, 0:256].rearrange("p (c t) -> p c t", c=2)[:, :, 0:L], [BpsT], [BppT])
                    yield
                    for hf in range(2):
                        for c in range(2):
                            MM(ps[b1[hf]][:L, :], ppT[:, c, 0:L], w_pp_sb[:, c, hf * 512:(hf + 1) * 512], c == 0, c == 1, [BppT, BwB], [Bps[b1[hf]]])
                        ACT(t1[:L, hf * 512:(hf + 1) * 512], ps[b1[hf]][:L, :], AF.Square, [Bps[b1[hf]]], [Bt1, Brsb], accum_out=rsb[:L, 2 + hf:3 + hf])
                        yield
                    TT('dve', rsb[:L, 4:5], rsb[:L, 2:3], rsb[:L, 3:4], ALU.add, [Brsb], [Brsb])
                    TS('dve', rsb[:L, 4:5], rsb[:L, 4:5], 1.0 / D, EPS, ALU.mult, ALU.add, [Brsb], [Brsb])
                    yield
                    TT('pool', rsb[:L, 5:6], rsb[:L, 4:5], mhalf[:L, :], ALU.pow, [Brsb, Bc], [Brsb])
                    yield
                    for hf in range(2):
                        STT(t1[:L, hf * 512:(hf + 1) * 512], ps[b1[hf]][:L, :], rsb[:L, 5:6], gple_b[:L, hf * 512:(hf + 1) * 512], ALU.mult, ALU.mult,
                            [Bps[b1[hf]], Brsb, Bc], [Bt1])
                    yield
                    CP('act', hn[:L, :], hh[:L, i, :], [Bhh[i]], [Bhn])
                    yield
                    for c in range(8):
                        TR(psT[:, c * 128:c * 128 + L], hn[:L, c * 128:(c + 1) * 128], identb[:L, :L], [Bhn, Bc], [BpsT])
                    CP('dve', h2T[:, :, 0:L], psT[:, :].rearrange("p (c t) -> p c t", c=8)[:, :, 0:L], [BpsT], [Bh2T])
                    yield
                    for hf in range(2):
                        for c in range(8):
                            MM(ps[b2[hf]][:L, :], h2T[:, c, 0:L], w_pg_sb[:, c, hf * 512:(hf + 1) * 512], c == 0, c == 7, [Bh2T, BwB], [Bps[b2[hf]]])
                        ACT(gt[:L, hf * 512:(hf + 1) * 512], ps[b2[hf]][:L, :], AF.Tanh, [Bps[b2[hf]]], [Bgt], scale=0.5)
                        yield
                    STT(gt[:L, :], gt[:L, :], 1.0, t1[:L, :], ALU.add, ALU.mult, [Bgt, Bt1], [Bgt])
                    yield
                    STT(hh[:L, i, :], gt[:L, :], 0.5, hh[:L, i, :], ALU.mult, ALU.add, [Bgt, Bhh[i]], [Bhh[i]])
                    yield
                    rms_rstd(hh[:L, i, :], L, hn[:L, :], rsb[:L, 6:7], rsb[:L, 7:8], [Bhh[i]], [Bhn, Brsb], 1.0 / D)
                    yield
                    STT(gt[:L, :], hh[:L, i, :], rsb[:L, 7:8], gfin_b[:L, :], ALU.mult, ALU.mult, [Bhh[i], Brsb, Bc], [Bgt])
                    ydst = ys_d[0:32, :] if samp else y_loc[r0:r0 + 128, :]
                    S.dma(ydst, gt[:L, :], reads=[Bgt])
                    yield

                lockstep([tail_gen(i) for i in range(nt)])
                if last_main:
                    nr = 8 if samp else 2
                    for rd in range(11):
                        for q4 in range(4):
                            ch = rd * 4 + q4
                            src = ffo_s[:, ch // NG, ch % NG, :, :].rearrange("p b t -> p (b t)") if samp else ffo[:, ch // NG, ch % NG, :]
                            TR(ps[7][0:nr, q4 * 128:(q4 + 1) * 128], src, identf[:, :], [Bffo, Bc], [Bps[7]])
                        CP('dve', ffs[0:nr, :], ps[7][0:nr, :], [Bps[7]], [Bffs])
                        fdst = ffn_s_d if samp else ffn_loc
                        S.dma(fdst[:, rd * 512:(rd + 1) * 512], ffs[0:nr, :], reads=[Bffs])

            for t0 in range(0, NMAIN, TB):
                b_supertile(t0, TB)
            b_supertile(NMAIN, NHALO)
            if not skip_a1:
                for rd in range(11):
                    S.dma(hs8[:, :], sffn_d[:, rd * 512:(rd + 1) * 512], writes=[Bhs8])
                    for q4 in range(4):
                        TR(ps[7][:, q4 * 8:(q4 + 1) * 8], hs8[0:8, q4 * 128:(q4 + 1) * 128], identf[0:8, 0:8], [Bhs8, Bc], [Bps[7]])
                    for q4 in range(4):
                        ch = rd * 4 + q4
                        CP('dve', hist_s[:, ch // NG, ch % NG, :, :], ps[7][:, q4 * 8:(q4 + 1) * 8].rearrange("p (b t) -> p b t", b=4),
                           [Bps[7]], [Bhist_s])
                b_supertile(0, 32, samp=True)

        S.finish()
    return nc


_NC_CACHE = {}


def _get_nc():
    if 'nc' not in _NC_CACHE:
        import os
        _NC_CACHE['nc'] = build_nc(float(os.environ.get('KSTAGE', '99')))
    return _NC_CACHE['nc']


def _prep_inputs(inputs):
    f = lambda a: np.ascontiguousarray(np.asarray(a, dtype=np.float32))
    x = f(inputs["x_prompt"]); p = f(inputs["p_prompt"])[0]
    consts = host_consts()
    shared = dict(consts)
    for k in ["rel_bias"]:
        shared[k] = f(inputs[k])
    for k in ["g_mix", "w_in", "conv_w", "conv_b", "dt_bias", "a_log", "d_skip", "g_ssm", "w_out", "g_ffn", "w_up",
              "ffn_conv_w", "ffn_conv_b", "w_down", "w_ple_proj", "g_ple", "w_ple_gate"]:
        shared[k] = f(inputs[k])[0]
    shared["g_final"] = f(inputs["g_final"])
    xs = f(inputs["x_sample"]); psm = f(inputs["p_sample"])[0]
    ck = f(inputs["cache_k"])[0]; cv = f(inputs["cache_v"])[0]
    sssm = f(inputs["state_ssm"])[0]; sconv = f(inputs["state_conv"])[0]; sffn = f(inputs["state_ffn_conv"])[0]
    in_maps = []
    for core in range(8):
        b, half = core // 2, core % 2
        if half == 0:
            xl = np.concatenate([np.zeros((NPRE, D), np.float32), x[b, 0:NMAIN + NHALO]], axis=0)
            pl = p[b, 0:NQ]
        else:
            xl = np.concatenate([x[b], np.zeros((NHALO, D), np.float32)], axis=0)
            pl = np.concatenate([p[b, NPRE:], np.zeros((NHALO, 256), np.float32)], axis=0)
        m = dict(shared)
        m["xl"] = np.ascontiguousarray(xl)
        m["pl"] = np.ascontiguousarray(pl)
        m["pv"] = np.full((128, 1), float(half), np.float32)
        sq = slice(4 * core, 4 * core + 4)
        m["xs"] = np.ascontiguousarray(xs[sq].reshape(32, D))
        m["psm"] = np.ascontiguousarray(psm[sq].reshape(32, 256))
        m["ck"] = np.ascontiguousarray(ck[sq].reshape(4, 2048, 512))
        m["cv"] = np.ascontiguousarray(cv[sq].reshape(4, 2048, 512))
        m["sssm"] = np.ascontiguousarray(sssm[sq].reshape(4, 512, 128))
        m["sconv"] = np.ascontiguousarray(sconv[sq].reshape(12, 1024))
        m["sffn"] = np.ascontiguousarray(sffn[sq].reshape(8, 2 * DFF))
        in_maps.append(m)
    return in_maps


def kernel(**inputs):
    in_maps = _prep_inputs(inputs)
    nc = _get_nc()
    res = run_bass_kernel_spmd(nc, in_maps, core_ids=list(range(8)))
    R = res.results
    y_prompt = np.zeros((4, 4096, D), np.float32)
    k_prompt = np.zeros((1, 4, 2048, 8, 64), np.float32)
    v_prompt = np.zeros((1, 4, 2048, 8, 64), np.float32)
    ssm_prompt = np.zeros((1, 4, 8, 64, 128), np.float32)
    conv_prompt = np.zeros((1, 4, 3, 1024), np.float32)
    ffn_prompt = np.zeros((1, 4, 2, 2 * DFF), np.float32)
    for b in range(4):
        A, Bc = R[2 * b], R[2 * b + 1]
        y_prompt[b, 0:NMAIN + 2] = A["y_loc"][0:NMAIN + 2]
        y_prompt[b, NMAIN + 2:] = Bc["y_loc"][2:NMAIN]
        k_prompt[0, b] = Bc["k_loc"].reshape(2048, 8, 64)
        v_prompt[0, b] = Bc["v_loc"].reshape(2048, 8, 64)
        ssm_prompt[0, b] = Bc["ssm_loc"].reshape(8, 64, 128)
        conv_prompt[0, b] = Bc["conv_loc"]
        ffn_prompt[0, b] = Bc["ffn_loc"]
    y_sample = np.zeros((32, 8, D), np.float32)
    k_sample = np.zeros((1, 32, 8, 8, 64), np.float32)
    v_sample = np.zeros((1, 32, 8, 8, 64), np.float32)
    ssm_sample = np.zeros((1, 32, 8, 64, 128), np.float32)
    conv_sample = np.zeros((1, 32, 3, 1024), np.float32)
    ffn_sample = np.zeros((1, 32, 2, 2 * DFF), np.float32)
    for core in range(8):
        sq = slice(4 * core, 4 * core + 4)
        r = R[core]
        y_sample[sq] = r["ys"].reshape(4, 8, D)
        k_sample[0, sq] = r["ks"].reshape(4, 8, 8, 64)
        v_sample[0, sq] = r["vs"].reshape(4, 8, 8, 64)
        ssm_sample[0, sq] = r["ssm_s"].reshape(4, 8, 64, 128)
        conv_sample[0, sq] = r["conv_s"].reshape(4, 3, 1024)
        ffn_sample[0, sq] = r["ffn_s"].reshape(4, 2, 2 * DFF)
    return (y_prompt, y_sample, k_prompt, v_prompt, k_sample, v_sample, ssm_prompt, ssm_sample,
            conv_prompt, conv_sample, ffn_prompt, ffn_sample)
```

```python
import numpy as np
from contextlib import ExitStack
import concourse.bass as bass
import concourse.mybir as mybir
from concourse.bass_utils import run_bass_kernel_spmd

F32 = mybir.dt.float32
BF16 = mybir.dt.bfloat16
AF = mybir.ActivationFunctionType
ALU = mybir.AluOpType

NEG = -30000.0
D = 1024
NPRE = 2048
NMAIN = 2048
NHALO = 128
NLOC = NPRE + NMAIN + NHALO
NQ = NMAIN + NHALO
IN_DIM = 3080
DFF = 2816
NG = 22
EPS = 1e-6
BRANCH_D = (1, 4, 16)


class Buf:
    def __init__(self, name="b", excl=False):
        self.name = name
        self.w = None
        self.r = {}
        self.excl = excl


class Sched:
    NDQ = 28
    NSP = 20

    def __init__(self, nc, stack):
        self.nc = nc
        self.h = {'pe': nc.tensor, 'act': nc.scalar, 'dve': nc.vector, 'pool': nc.gpsimd, 'sp': nc.sync}
        self.sem = {k: stack.enter_context(nc.semaphore(k + "_sem")) for k in self.h}
        self.cnt = {k: 0 for k in self.h}
        self.seen = {k: {} for k in self.h}
        for i in range(self.NDQ):
            self.sem[('dq', i)] = stack.enter_context(nc.semaphore(f"dq{i}"))
        self.dcnt = [0] * self.NDQ
        self.rr = 0
        self.rr_pool = 0
        self.nwait = 0
        self.ndma = 0

    def _deps(self, reads, writes):
        deps = {}
        for b in reads:
            if b.w is not None:
                k, v = b.w
                if deps.get(k, 0) < v:
                    deps[k] = v
            if b.excl:
                for k, v in b.r.items():
                    if deps.get(k, 0) < v:
                        deps[k] = v
        for b in writes:
            if b.w is not None:
                k, v = b.w
                if deps.get(k, 0) < v:
                    deps[k] = v
            for k, v in b.r.items():
                if deps.get(k, 0) < v:
                    deps[k] = v
        return deps

    def _wait(self, eng, deps):
        h = self.h[eng]
        seen = self.seen[eng]
        for k, v in deps.items():
            if k == eng and eng in ('pe', 'sp'):
                continue
            if seen.get(k, 0) >= v:
                continue
            h.wait_ge(self.sem[k], v)
            self.nwait += 1
            seen[k] = v

    def _mark(self, ev, reads, writes):
        k, v = ev
        for b in reads:
            if b.r.get(k, 0) < v:
                b.r[k] = v
        for b in writes:
            b.w = ev
            b.r = {}

    def op(self, eng, fn, reads=(), writes=()):
        self._wait(eng, self._deps(reads, writes))
        ins = fn(self.h[eng])
        self.cnt[eng] += 1
        ins.then_inc(self.sem[eng], 1)
        self._mark((eng, self.cnt[eng]), reads, writes)

    def dma(self, out, in_, reads=(), writes=(), eng='sp', **kw):
        if eng == 'pool':
            i = self.NSP + self.rr_pool
            self.rr_pool = (self.rr_pool + 1) % (self.NDQ - self.NSP)
        else:
            i = self.rr
            self.rr = (i + 1) % self.NSP
        deps = self._deps(reads, writes)
        if self.dcnt[i] > 0:
            k = ('dq', i)
            deps[k] = max(deps.get(k, 0), 16 * self.dcnt[i])
        self._wait(eng, deps)
        ins = self.h[eng].dma_start(out=out, in_=in_, **kw)
        self.dcnt[i] += 1
        self.ndma += 1
        ins.then_inc(self.sem[('dq', i)], 16)
        self._mark((('dq', i), 16 * self.dcnt[i]), reads, writes)

    def pe_mode(self, mode):
        if getattr(self, 'cur_mode', None) is not None and self.cur_mode != mode and self.cnt['pe'] > 0:
            self.h['pe'].wait_ge(self.sem['pe'], self.cnt['pe'])
            self.h['pe'].drain()
            self.nwait += 1
            self.ndrain = getattr(self, 'ndrain', 0) + 1
        self.cur_mode = mode

    def barrier(self):
        for eng in self.h:
            deps = {k: self.cnt[k] for k in self.h if k != eng and self.cnt[k] > 0}
            for i in range(self.NDQ):
                if self.dcnt[i] > 0:
                    deps[('dq', i)] = 16 * self.dcnt[i]
            self._wait(eng, deps)

    def finish(self):
        deps = {('dq', i): 16 * self.dcnt[i] for i in range(self.NDQ) if self.dcnt[i] > 0}
        for k in self.h:
            if k != 'sp' and self.cnt[k] > 0:
                deps[k] = self.cnt[k]
        self._wait('sp', deps)


def rel_bucket_np(dist):
    dist = np.asarray(dist, np.int64)
    d = np.maximum(dist, 1).astype(np.float32)
    large = 16 + (np.log(d / np.float32(16)) / np.float32(np.log(2048 / 16)) * np.float32(16)).astype(np.int32)
    large = np.minimum(large, 31)
    return np.where(dist < 16, dist, large).astype(np.int64)


def host_consts():
    ident = np.eye(128, dtype=np.float32)
    triu = np.triu(np.ones((128, 128), np.float32))
    negtri = np.where(triu > 0, 0.0, NEG).astype(np.float32)
    oh = np.zeros((32, 3 * 129), np.float32)
    for bi, d in enumerate(BRANCH_D):
        bk = rel_bucket_np(np.arange(129) * d)
        for j in range(129):
            oh[bk[j], bi * 129 + j] = 1.0
    negdiag = np.full((128, 2), NEG, np.float32)
    negdiag[0, 0] = 0.0
    negdiag[1, 1] = 0.0
    cm = np.full((32, 3, 32), NEG, np.float32)
    for kg in range(32):
        for qg in range(32):
            if kg // 8 != qg // 8:
                continue
            dist = qg % 8 - kg % 8
            if dist < 0:
                continue
            cm[kg, 0, qg] = 0.0
            if dist in (0, 4):
                cm[kg, 1, qg] = 0.0
            if dist == 0:
                cm[kg, 2, qg] = 0.0
    return dict(ident=ident, triu=triu, negtri=negtri, oh=oh, negdiag=negdiag, cm=cm)


def build_nc(stage=99):
    global _LAST_S
    import os
    skip_p = os.environ.get('KSKIPP', '0') == '1'
    skip_a1 = os.environ.get('KNOSAMP', '0') == '1'
    ks1 = float(os.environ.get('KS1', '99'))
    nc = bass.Bass("TRN2", target_bir_lowering=False)

    def din(name, shape):
        return nc.dram_tensor(name, list(shape), F32, kind="ExternalInput").ap()

    def dout(name, shape):
        return nc.dram_tensor(name, list(shape), F32, kind="ExternalOutput").ap()

    xl = din("xl", [NLOC, D])
    pl = din("pl", [NQ, 256])
    pv = din("pv", [128, 1])
    rel_bias = din("rel_bias", [32, 8])
    g_mix = din("g_mix", [D])
    w_in = din("w_in", [D, IN_DIM])
    conv_w = din("conv_w", [4, 1024])
    conv_b = din("conv_b", [1024])
    dt_bias = din("dt_bias", [8])
    a_log = din("a_log", [8])
    d_skip = din("d_skip", [8])
    g_ssm = din("g_ssm", [512])
    w_out = din("w_out", [D, D])
    g_ffn = din("g_ffn", [D])
    w_up = din("w_up", [D, 2 * DFF])
    ffn_conv_w = din("ffn_conv_w", [3, 2 * DFF])
    ffn_conv_b = din("ffn_conv_b", [2 * DFF])
    w_down = din("w_down", [DFF, D])
    w_ple_proj = din("w_ple_proj", [256, D])
    g_ple = din("g_ple", [D])
    w_ple_gate = din("w_ple_gate", [D, D])
    g_final = din("g_final", [D])
    c_ident = din("ident", [128, 128])
    c_triu = din("triu", [128, 128])
    c_negtri = din("negtri", [128, 128])
    c_oh = din("oh", [32, 387])
    c_negdiag = din("negdiag", [128, 2])

    xs_d = din("xs", [32, D])
    psm_d = din("psm", [32, 256])
    ck_d = din("ck", [4, 2048, 512])
    cv_d = din("cv", [4, 2048, 512])
    sssm_d = din("sssm", [4, 512, 128])
    sconv_d = din("sconv", [12, 1024])
    sffn_d = din("sffn", [8, 2 * DFF])
    c_cm = din("cm", [32, 3, 32])
    ys_d = dout("ys", [32, D])
    ks_d = dout("ks", [32, 512])
    vs_d = dout("vs", [32, 512])
    ssm_s_d = dout("ssm_s", [4, 512, 128])
    conv_s_d = dout("conv_s", [12, 1024])
    ffn_s_d = dout("ffn_s", [8, 2 * DFF])

    y_loc = dout("y_loc", [NQ, D])
    k_loc = dout("k_loc", [NMAIN, 512])
    v_loc = dout("v_loc", [NMAIN, 512])
    ssm_loc = dout("ssm_loc", [512, 128])
    conv_loc = dout("conv_loc", [3, 1024])
    ffn_loc = dout("ffn_loc", [2, 2 * DFF])

    qT_d = nc.dram_tensor("qT_d", [4, 128, NQ], BF16, kind="Internal").ap()
    kT_d = nc.dram_tensor("kT_d", [4, 128, NLOC], BF16, kind="Internal").ap()
    vT_d = nc.dram_tensor("vT_d", [4, 128, NLOC], BF16, kind="Internal").ap()
    bm_d = nc.dram_tensor("bm_d", [8, 3, 384], BF16, kind="Internal").ap()
    wup_d = nc.dram_tensor("wup_d", [NG, 128, 8 * 256], BF16, kind="Internal").ap()
    wB_d = nc.dram_tensor("wB_d", [40, 128, D], BF16, kind="Internal").ap()
    dg_d = nc.dram_tensor("dg_d", [NG, 128, 768], BF16, kind="Internal").ap()

    with ExitStack() as top:
        S = Sched(nc, top)
        _LAST_S = S

        def sbt(stack, name, shape, dt):
            return stack.enter_context(nc.sbuf_tensor("s_" + name, list(shape), dt))

        def _ru(x):
            return 32 if x <= 32 else (64 if x <= 64 else 128)

        def _mode(lhsT):
            shp = lhsT.shape
            m = 1
            for d_ in shp[1:]:
                m *= int(d_)
            return (_ru(int(shp[0])), _ru(m), lhsT.dtype == F32)

        def MM(out, lhsT, rhs, start, stop, r, w):
            md = _mode(lhsT)
            S.pe_mode(md if md[:2] != (128, 128) else (128, 128))
            S.op('pe', lambda e: e.matmul(out, lhsT=lhsT, rhs=rhs, start=start, stop=stop, skip_group_check=True), r, w)

        def TR(out, in_, ident, r, w):
            md = _mode(in_)
            S.pe_mode((md + ('T',)) if md[:2] != (128, 128) else (128, 128))
            S.op('pe', lambda e: e.transpose(out=out, in_=in_, identity=ident), r, w)

        def ACT(out, in_, func, r, w, **kw):
            S.op('act', lambda e: e.activation(out=out, in_=in_, func=func, **kw), r, w)

        def CP(eng, out, in_, r, w):
            if eng == 'act':
                S.op('act', lambda e: e.activation(out=out, in_=in_, func=AF.Copy), r, w)
            else:
                S.op(eng, lambda e: e.tensor_copy(out=out, in_=in_), r, w)

        def TT(eng, out, in0, in1, op, r, w):
            S.op(eng, lambda e: e.tensor_tensor(out=out, in0=in0, in1=in1, op=op), r, w)

        def TS(eng, out, in0, s1, s2, op0, op1, r, w):
            if op1 is None:
                S.op(eng, lambda e: e.tensor_scalar(out=out, in0=in0, scalar1=s1, scalar2=None, op0=op0), r, w)
            else:
                S.op(eng, lambda e: e.tensor_scalar(out=out, in0=in0, scalar1=s1, scalar2=s2, op0=op0, op1=op1), r, w)

        def STT(out, in0, scalar, in1, op0, op1, r, w):
            S.op('dve', lambda e: e.scalar_tensor_tensor(out=out, in0=in0, scalar=scalar, in1=in1, op0=op0, op1=op1), r, w)

        def MEMSET(eng, ap, val, w):
            S.op(eng, lambda e: e.memset(ap, val), (), w)

        psT = top.enter_context(nc.psum_tensor("psT", [128, 1024], BF16))
        BpsT = Buf("psT", excl=True)
        ps = [None] + [top.enter_context(nc.psum_tensor(f"ps{i}", [128, 512], F32)) for i in range(1, 8)]
        Bps = [None] + [Buf(f"ps{i}", excl=True) for i in range(1, 8)]

        identf = sbt(top, "identf", [128, 128], F32)
        identb = sbt(top, "identb", [128, 128], BF16)
        onec = sbt(top, "onec", [128, 1], F32)
        epsc = sbt(top, "epsc", [128, 1], F32)
        mhalf = sbt(top, "mhalf", [128, 1], F32)
        pvt = sbt(top, "pvt", [128, 1], F32)
        gmixT = sbt(top, "gmixT", [128, 8], F32)
        gffnT = sbt(top, "gffnT", [128, 8], F32)
        Bc = Buf("consts")

        top.enter_context(nc.Block())

        S.dma(identf[:], c_ident[:, :], writes=[Bc])
        S.dma(pvt[:], pv[:, :], writes=[Bc])
        S.dma(gmixT[:], g_mix.rearrange("(c p) -> p c", p=128), writes=[Bc], allow_slow_non_contiguous=True)
        S.dma(gffnT[:], g_ffn.rearrange("(c p) -> p c", p=128), writes=[Bc], allow_slow_non_contiguous=True)
        MEMSET('dve', onec[:], 1.0, [Bc])
        MEMSET('dve', epsc[:], EPS, [Bc])
        MEMSET('dve', mhalf[:], -0.5, [Bc])
        CP('dve', identb[:], identf[:], [Bc], [Bc])

        def rms_rstd(src_ap, L, junk_ap, ss_ap, rstd_ap, r, w, inv_n):
            ACT(junk_ap, src_ap, AF.Square, r, w, accum_out=ss_ap)
            TS('dve', ss_ap, ss_ap, inv_n, EPS, ALU.mult, ALU.add, w, w)
            TT('pool', rstd_ap, ss_ap, mhalf[:L, :], ALU.pow, w + [Bc], w)

        mixT = sbt(top, "mixT", [128, 8, NQ], BF16)
        BmixA = [Buf(f"mixA{p}") for p in range(4)]
        BmixS = Buf("mixS")
        smixT = sbt(top, "smixT", [128, 8, 32], BF16)
        BsmixA = [Buf() for _ in range(4)]
        BsmixS = Buf()
        sQT = sbt(top, "sQT", [128, 4, 2, 32], BF16)
        sKT = sbt(top, "sKT", [128, 4, 32], BF16)
        sVn = sbt(top, "sVn", [32, 8, 66], BF16)
        Bsq = Buf()
        MEMSET('pool', sQT[:].rearrange("p a h q -> p (a h q)"), 0.0, [Bsq])
        MEMSET('pool', sVn[:].rearrange("p h e -> p (h e)"), 1.0, [Bsq])

        sA = ExitStack()
        Bm = sbt(sA, "Bm", [128, 8, 3, 256], BF16)
        Bdg = sbt(sA, "Bdg", [128, 8, 2], BF16)
        negd = sbt(sA, "negd", [128, 2], F32)
        sA0 = ExitStack()
        rb = sbt(sA0, "rb", [32, 8], F32)
        oht = sbt(sA0, "oht", [32, 387], F32)
        gfull = sbt(sA0, "gfull", [8, 3, 384], F32)
        Bt = Buf()
        Bbmd = Buf()
        BBm = Buf()
        S.dma(rb[:], rel_bias[:, :], writes=[Bt])
        S.dma(oht[:], c_oh[:, :], writes=[Bt])
        S.dma(negd[:], c_negdiag[:, :], writes=[Bc])
        MEMSET('dve', gfull[:], NEG, [Bt])
        MM(ps[1][0:8, 0:387], rb[:, :], oht[:, :], True, True, [Bt], [Bps[1]])
        CP('dve', gfull[:, :, 127:256], ps[1][0:8, 0:387].rearrange("p (b j) -> p b j", b=3), [Bps[1], Bt], [Bt])
        gfull_b = sbt(sA0, "gfull_b", [8, 3, 384], BF16)
        CP('dve', gfull_b[:], gfull[:], [Bt], [Bt])
        S.dma(bm_d[:, :, :], gfull_b[:], reads=[Bt], writes=[Bbmd])

        S.barrier()
        sA0.close()

        def bm_dma(c):
            S.dma(Bm[c:c + 1, :, :, :], bm_d[:, :, 127 - c:127 - c + 256].unsqueeze(0), reads=[Bbmd], writes=[BBm])

        with ExitStack() as sa:
            w_in_sb = sbt(sa, "w_in_sb", [128, 8, IN_DIM], BF16)
            Bwin = Buf("w_in")
            for c in range(8):
                S.dma(w_in_sb[:, c, :], w_in[c * 128:(c + 1) * 128, :], writes=[Bwin], eng='pool')
            triu = sbt(sa, "triu", [128, 128], F32)
            ones = sbt(sa, "ones", [128, 128], F32)
            negtri_f = sbt(sa, "negtri_f", [128, 128], F32)
            negtri8 = sbt(sa, "negtri8", [128, 8, 128], BF16)
            cwT = sbt(sa, "cwT", [128, 8, 4], F32)
            cbT = sbt(sa, "cbT", [128, 8], F32)
            dtb = sbt(sa, "dtb", [128, 8], F32)
            A_b = sbt(sa, "A_b", [128, 8], F32)
            dskip_b = sbt(sa, "dskip_b", [128, 8], F32)
            gssm_b = sbt(sa, "gssm_b", [128, 512], F32)
            S.dma(triu[:], c_triu[:, :], writes=[Bc])
            S.dma(negtri_f[:], c_negtri[:, :], writes=[Bc])
            for k in range(4):
                S.dma(cwT[:, :, k], conv_w[k, :].rearrange("(c p) -> p c", p=128), writes=[Bc], allow_slow_non_contiguous=True)
            S.dma(cbT[:], conv_b.rearrange("(c p) -> p c", p=128), writes=[Bc], allow_slow_non_contiguous=True)
            S.dma(dtb[:], dt_bias.partition_broadcast(128), writes=[Bc])
            S.dma(A_b[:], a_log.partition_broadcast(128), writes=[Bc])
            S.dma(dskip_b[:], d_skip.partition_broadcast(128), writes=[Bc])
            S.dma(gssm_b[:], g_ssm.partition_broadcast(128), writes=[Bc])
            MEMSET('pool', ones[:], 1.0, [Bc])
            CP('dve', negtri8[:], negtri_f[:].unsqueeze(1).to_broadcast([128, 8, 128]), [Bc], [Bc])
            TS('dve', cwT[:], cwT[:], 0.5, None, ALU.mult, None, [Bc], [Bc])
            TS('dve', cbT[:], cbT[:], 0.5, None, ALU.mult, None, [Bc], [Bc])
            ACT(A_b[:], A_b[:], AF.Exp, [Bc], [Bc])
            TS('dve', A_b[:], A_b[:], -1.0, None, ALU.mult, None, [Bc], [Bc])

            xt = [sbt(sa, f"xt{i}", [128, D], F32) for i in range(2)]
            Bxt = [Buf() for _ in range(2)]
            xn = [sbt(sa, f"xn{i}", [128, D], BF16) for i in range(2)]
            Bxn = [Buf() for _ in range(2)]
            rs = [sbt(sa, f"rs{i}", [128, 2], F32) for i in range(2)]
            Brs = [Buf() for _ in range(2)]
            xnT2 = [sbt(sa, f"xnT{i}", [128, 8, 512], BF16) for i in range(2)]
            BxnT2 = [[Buf() for _ in range(4)] for _ in range(2)]
            xnT = xnT2[0]
            BxnT = BxnT2[0]
            stg = [sbt(sa, f"stg{i}", [128, 512], BF16) for i in range(3)]
            Bstg = [Buf() for _ in range(3)]
            cbuf = sbt(sa, "cbuf", [128, 8, 515], F32)
            Bcb = [Buf() for _ in range(8)]
            acc = [sbt(sa, f"acc{i}", [128, 512], F32) for i in range(2)]
            Bacc = [Buf() for _ in range(2)]
            th0 = sbt(sa, "th0", [128, 512], F32)
            th = [th0, th0]
            Bth0 = Buf()
            Bth = [Bth0, Bth0]
            xbcT = sbt(sa, "xbcT", [128, 8, 512], BF16)
            Bxbc = [Buf() for _ in range(8)]
            kvo = [sbt(sa, f"kvo{i}", [128, 512], F32) for i in range(2)]
            Bkvo = [Buf() for _ in range(2)]
            dtall = sbt(sa, "dtall", [128, 4, 8], F32)
            Bdt = Buf()
            xsB2 = [sbt(sa, f"xsB{i}", [128, 768], BF16) for i in range(2)]
            BxsB2 = [Buf() for _ in range(2)]
            sm = sbt(sa, "sm", [128, 4, 4, 8], F32)
            Bsm = Buf()
            avall = sbt(sa, "avall", [128, 4, 8], F32)
            Bav = Buf()
            Lm2 = [sbt(sa, f"Lm{i}", [128, 8, 128], F32) for i in range(2)]
            BLm2 = [Buf() for _ in range(2)]
            Mb2 = [sbt(sa, f"Mb{i}", [128, 8, 128], BF16) for i in range(2)]
            BMb2 = [Buf() for _ in range(2)]
            xdt2 = [sbt(sa, f"xdt{i}", [128, 8, 64], BF16) for i in range(2)]
            Bxdt2 = [Buf() for _ in range(2)]
            xde2 = [sbt(sa, f"xde{i}", [128, 8, 64], BF16) for i in range(2)]
            Bxde2 = [Buf() for _ in range(2)]
            xsk2 = [sbt(sa, f"xsk{i}", [128, 8, 64], BF16) for i in range(2)]
            Bxsk2 = [Buf() for _ in range(2)]
            stf = sbt(sa, "stf", [128, 8, 64], F32)
            Bstf = Buf()
            stb = sbt(sa, "stb", [128, 512], BF16)
            Bstb = Buf()
            ytl2 = [sbt(sa, f"ytl{i}", [128, 512], F32) for i in range(2)]
            Bytl2 = [Buf() for _ in range(2)]
            zs = sbt(sa, "zs", [128, 512], F32)
            Bzs = Buf()
            ss2 = sbt(sa, "ss2", [128, 4], F32)
            Bss2 = Buf()
            ynb = sbt(sa, "ynb", [128, 512], BF16)
            Bynb = Buf()
            sto = sbt(sa, "sto", [128, 4, 128], F32)
            Bsto = Buf()

            MEMSET('dve', cbuf[:, :, 0:3], 0.0, Bcb)
            MEMSET('dve', stf[:], 0.0, [Bstf])
            MEMSET('dve', stb[:], 0.0, [Bstb])

            tile_ctr = [0]
            ev_ctr = [0]

            def norm_transpose_tile(src_rows_ap, L, dst_cols, BdstT, xT=None):
                s = tile_ctr[0] % 2
                tile_ctr[0] += 1
                S.dma(xt[s][:L, :], src_rows_ap, writes=[Bxt[s]])
                rms_rstd(xt[s][:L, :], L, xn[s][:L, :], rs[s][:L, 0:1], rs[s][:L, 1:2], [Bxt[s]], [Bxn[s], Brs[s]], 1.0 / D)
                ACT(xn[s][:L, :], xt[s][:L, :], AF.Copy, [Bxt[s], Brs[s]], [Bxn[s]], scale=rs[s][:L, 1:2])
                for c in range(8):
                    TR(psT[:, c * 128:c * 128 + L], xn[s][:L, c * 128:(c + 1) * 128], identb[:L, :L], [Bxn[s], Bc], [BpsT])
                pv3 = psT[:, :].rearrange("p (c t) -> p c t", c=8)[:, :, 0:L]
                TT('dve', (xnT if xT is None else xT)[:, :, dst_cols], pv3, gmixT[:].unsqueeze(2).to_broadcast([128, 8, L]), ALU.mult,
                   [BpsT, Bc], [BdstT])

            def ssd_prep(L, nt, dv_ap, main):
                TT('dve', avall[:L, 0:nt, :], dv_ap, A_b[:L, :].unsqueeze(1).to_broadcast([L, nt, 8]), ALU.mult, [Bdt, Bc], [Bav])
                rhs = avall[:L, 0:nt, :].rearrange("p i h -> p (i h)")
                MM(ps[5][:L, 32:32 + nt * 8], triu[:L, :L], rhs, True, True, [Bc, Bav], [Bps[5]])
                MM(ps[5][:, 64:64 + nt * 8], ones[:L, :], rhs, True, True, [Bc, Bav], [Bps[5]])
                pac = ps[5][:, 32:32 + nt * 8].rearrange("p (i h) -> p i h", h=8)
                pal = ps[5][:, 64:64 + nt * 8].rearrange("p (i h) -> p i h", h=8)
                TS('dve', sm[:L, 0, 0:nt, :], pac[:L], -1.0, None, ALU.mult, None, [Bps[5]], [Bsm])
                TT('dve', sm[:L, 2, 0:nt, :], pal[:L], sm[:L, 0, 0:nt, :], ALU.add, [Bps[5], Bsm], [Bsm])
                ACT(sm[:L, 2, 0:nt, :], sm[:L, 2, 0:nt, :], AF.Exp, [Bsm], [Bsm])
                TT('dve', sm[:L, 2, 0:nt, :], sm[:L, 2, 0:nt, :], dv_ap, ALU.mult, [Bsm, Bdt], [Bsm])
                ACT(sm[:, 3, 0:nt, :], pal, AF.Exp, [Bps[5]], [Bsm])
                if main:
                    ACT(sm[:L, 1, 0:nt, :], pac[:L], AF.Exp, [Bps[5]], [Bsm])

            def ssd_gen(L, cs, BxT, kind, dt_ap, mix_cols, ti, par, mixt=None, Bmix=None, xT=None):
                main = (kind == 'main')
                xsB, BxsB = xsB2[par], BxsB2[par]
                Lm, BLm, Mb, BMb = Lm2[par], BLm2[par], Mb2[par], BMb2[par]
                xdt, Bxdt, xde, Bxde, xsk, Bxsk = xdt2[par], Bxdt2[par], xde2[par], Bxde2[par], xsk2[par], Bxsk2[par]
                ytl, Bytl = ytl2[par], Bytl2[par]
                for j in range(6):
                    TR(psT[:L, j * 128:(j + 1) * 128], xbcT[:, j, cs], identb[:, :], [Bxbc[j], Bc], [BpsT])
                CP('act', xsB[:L, :], psT[:L, 0:768], [BpsT], [BxsB])
                yield
                av = avall[:, ti, :]
                TT('dve', xde[:L], xsB[:L, 0:512].rearrange("p (h e) -> p h e", h=8),
                   sm[:L, 2, ti, :].unsqueeze(2).to_broadcast([L, 8, 64]), ALU.mult, [BxsB, Bsm], [Bxde])
                if main:
                    TT('dve', xdt[:L], xsB[:L, 0:512].rearrange("p (h e) -> p h e", h=8),
                       dt_ap.unsqueeze(2).to_broadcast([L, 8, 64]), ALU.mult, [BxsB, Bdt], [Bxdt])
                    TT('pool', xsk[:L], xsB[:L, 0:512].rearrange("p (h e) -> p h e", h=8),
                       dskip_b[:L, :].unsqueeze(2).to_broadcast([L, 8, 64]), ALU.mult, [BxsB, Bc], [Bxsk])
                    for g in range(2):
                        MM(ps[2][:L, g * 256:(g + 1) * 256], xbcT[:, 6 + g, cs], stb[:, g * 256:(g + 1) * 256], g == 0, g == 1,
                           [Bxbc[6 + g], Bstb], [Bps[2]])
                yield
                if main:
                    TT('dve', ytl[:L, :].rearrange("p (h e) -> p h e", h=8), ps[2][:L, :].rearrange("p (h e) -> p h e", h=8),
                       sm[:L, 1, ti, :].unsqueeze(2).to_broadcast([L, 8, 64]), ALU.mult, [Bps[2], Bsm], [Bytl])
                for g in range(2):
                    MM(ps[3][:, g * 256:(g + 1) * 256], xsB[:L, 512 + g * 128:512 + (g + 1) * 128],
                       xde[:L, g * 4:(g + 1) * 4, :].rearrange("p h e -> p (h e)"), g == 0, g == 1, [BxsB, Bxde], [Bps[3]])
                TT('dve', stf[:], stf[:], sm[:, 3, ti, :].unsqueeze(2).to_broadcast([128, 8, 64]), ALU.mult, [Bstf, Bsm], [Bstf])
                yield
                TT('dve', stf[:].rearrange("p h e -> p (h e)"), stf[:].rearrange("p h e -> p (h e)"), ps[3][:, :], ALU.add,
                   [Bstf, Bps[3]], [Bstf])
                CP('act', stb[:, :], stf[:].rearrange("p h e -> p (h e)"), [Bstf], [Bstb])
                yield 'S'
                if not main:
                    return
                for hb in range(2):
                    bank = ps[6 + hb]
                    o = bank[:, :].rearrange("p (h t) -> p h t", h=4)[:L, :, :L]
                    for h4 in range(4):
                        MM(o[:, h4, :], av[:L, hb * 4 + h4:hb * 4 + h4 + 1].to_broadcast([L, L]), triu[:L, :L], h4 == 0, False,
                           [Bc, Bav], [Bps[6 + hb]])
                    if L == 128:
                        MM(o, identb[:L, :L], negtri8[:L, hb * 4:(hb + 1) * 4, :L], False, True, [Bc], [Bps[6 + hb]])
                    else:
                        for h4 in range(4):
                            MM(o[:, h4, :], identb[:L, :L], negtri8[:L, hb * 4 + h4, :L], False, h4 == 3, [Bc], [Bps[6 + hb]])
                for g in range(2):
                    MM(ps[5][:L, 256 + g * 128:256 + g * 128 + L], xbcT[:, 4 + g, cs], xbcT[:, 6 + g, cs], True, True,
                       [Bxbc[4 + g], Bxbc[6 + g]], [Bps[5]])
                yield
                for h in range(8):
                    bank = ps[6 + h // 4]
                    o = bank[:, :].rearrange("p (h t) -> p h t", h=4)[:L, h % 4, :L]
                    ACT(Lm[:L, h, :L], o, AF.Exp, [Bps[6 + h // 4], Bsm], [BLm], bias=sm[:L, 0, ti, h:h + 1])
                    if h == 3:
                        yield
                yield
                cbv = ps[5][:, 256:512].rearrange("p (g t) -> p g t", g=2)[:L, :, :L]
                TT('dve', Mb[:L, :, :L].rearrange("p (g h) t -> p g h t", g=2),
                   Lm[:L, :, :L].rearrange("p (g h) t -> p g h t", g=2),
                   cbv.unsqueeze(2).to_broadcast([L, 2, 4, L]), ALU.mult, [BLm, Bps[5]], [BMb])
                yield
                for h in range(8):
                    MM(ps[1][:L, h * 64:(h + 1) * 64], Mb[:L, h, :L], xdt[:L, h, :], h == 0, False, [BMb, Bxdt], [Bps[1]])
                MM(ps[1][:L, :], identb[:L, :L], xsk[:L].rearrange("p h e -> p (h e)"), False, True, [Bc, Bxsk], [Bps[1]])
                xT_ = xnT if xT is None else xT
                for c in range(8):
                    MM(ps[4][:L, :], xT_[:, c, cs], w_in_sb[:, c, 1536:2048], c == 0, c == 7, [BxT, Bwin], [Bps[4]])
                yield
                TT('dve', ytl[:L, :], ytl[:L, :], ps[1][:L, :], ALU.add, [Bytl, Bps[1]], [Bytl])
                ACT(zs[:L, :], ps[4][:L, :], AF.Tanh, [Bps[4]], [Bzs], scale=0.5)
                yield
                STT(zs[:L, :], zs[:L, :], 1.0, ps[4][:L, :], ALU.add, ALU.mult, [Bzs, Bps[4]], [Bzs])
                yield
                STT(ytl[:L, :], zs[:L, :], 0.5, ytl[:L, :], ALU.mult, ALU.mult, [Bzs, Bytl], [Bytl])
                yield
                for g in range(2):
                    ACT(zs[:L, g * 256:(g + 1) * 256], ytl[:L, g * 256:(g + 1) * 256], AF.Square, [Bytl], [Bzs, Bss2],
                        accum_out=ss2[:L, g:g + 1])
                yield
                TS('dve', ss2[:L, 0:2], ss2[:L, 0:2], 1.0 / 256, EPS, ALU.mult, ALU.add, [Bss2], [Bss2])
                yield
                TT('pool', ss2[:L, 2:4], ss2[:L, 0:2], mhalf[:L, :].to_broadcast([L, 2]), ALU.pow, [Bss2, Bc], [Bss2])
                yield
                for g in range(2):
                    STT(ynb[:L, g * 256:(g + 1) * 256], ytl[:L, g * 256:(g + 1) * 256], ss2[:L, 2 + g:3 + g],
                        gssm_b[:L, g * 256:(g + 1) * 256], ALU.mult, ALU.mult, [Bytl, Bss2, Bc], [Bynb])
                yield
                for c in range(4):
                    TR(psT[:, c * 128:c * 128 + L], ynb[:L, c * 128:(c + 1) * 128], identb[:L, :L], [Bynb, Bc], [BpsT])
                mixt_ = mixT if mixt is None else mixt
                CP('act', mixt_[:, 4:8, mix_cols], psT[:, 0:512].rearrange("p (c t) -> p c t", c=4)[:, :, 0:L], [BpsT],
                   [BmixS if Bmix is None else Bmix])
                yield

            def run_chunks(gens, width=2):
                gens = list(gens)
                active = []
                nxt = [0]
                ready = [True]

                def start():
                    if nxt[0] < len(gens) and len(active) < width and ready[0]:
                        active.append(gens[nxt[0]])
                        nxt[0] += 1
                        ready[0] = False
                start()
                while active or nxt[0] < len(gens):
                    if not active:
                        ready[0] = True
                        start()
                    for g in list(active):
                        try:
                            r = next(g)
                        except StopIteration:
                            active.remove(g)
                            start()
                            continue
                        if r == 'S':
                            ready[0] = True
                            start()

            def emit_state(dst_ap):
                for c in range(4):
                    TR(ps[4][:, c * 128:(c + 1) * 128], stf[:, 2 * c:2 * c + 2, :].rearrange("p h e -> p (h e)"), identf[:, :],
                       [Bstf, Bc], [Bps[4]])
                CP('dve', sto[:].rearrange("p c n -> p (c n)"), ps[4][:, :], [Bps[4]], [Bsto])
                S.dma(dst_ap.rearrange("(c p) n -> p c n", p=128), sto[:], reads=[Bsto])

            def a1_norm(t0, T, spar):
                for i in range(T // 128):
                    norm_transpose_tile(xl[t0 + i * 128:t0 + (i + 1) * 128, :], 128, slice(i * 128, (i + 1) * 128), BxnT2[spar][i],
                                        xT=xnT2[spar])

            def a1_supertile(t0, T, kind, spar, hoist_next):
                nt = T // 128
                main = kind != 'prefix'
                xnT = xnT2[spar]
                BxnT = BxnT2[spar]
                BxT_all = BxnT[:nt]
                nx = 8 if main else 6
                chunks = [('x', j, 2048 + j * 128) for j in range(8)]
                if main:
                    chunks += [('q', c, c * 128) for c in range(4)]
                chunks += [('k', c, 512 + c * 128) for c in range(4)]
                chunks += [('v', c, 1024 + c * 128) for c in range(4)]
                for ci, (knd, idx, col0) in enumerate(chunks):
                    bi = 1 + ci % 3
                    for c in range(8):
                        MM(ps[bi][:, 0:T], w_in_sb[:, c, col0:col0 + 128], xnT[:, c, 0:T], c == 0, c == 7, BxT_all + [Bwin], [Bps[bi]])
                    if knd == 'x':
                        eng = 'act' if (idx % 2 == 0) else 'dve'
                        CP(eng, cbuf[:, idx, 3:3 + T], ps[bi][:, 0:T], [Bps[bi]], [Bcb[idx]])
                    else:
                        si = ev_ctr[0] % 3
                        ev_ctr[0] += 1
                        if knd == 'q':
                            ACT(stg[si][:, 0:T], ps[bi][:, 0:T], AF.Copy, [Bps[bi]], [Bstg[si]], scale=0.125)
                            S.dma(qT_d[idx, :, t0 - NPRE:t0 - NPRE + T], stg[si][:, 0:T], reads=[Bstg[si]], writes=[Bqd])
                        else:
                            CP('dve' if knd == 'k' else 'act', stg[si][:, 0:T], ps[bi][:, 0:T], [Bps[bi]], [Bstg[si]])
                            dst = kT_d if knd == 'k' else vT_d
                            S.dma(dst[idx, :, t0:t0 + T], stg[si][:, 0:T], reads=[Bstg[si]], writes=[Bkd if knd == 'k' else Bvd])
                if kind == 'main':
                    for i in range(nt):
                        for wi, (col0, dst) in enumerate(((512, k_loc), (1024, v_loc))):
                            for c in range(8):
                                MM(ps[4][:, :], xnT[:, c, i * 128:(i + 1) * 128], w_in_sb[:, c, col0:col0 + 512], c == 0, c == 7,
                                   [BxnT[i], Bwin], [Bps[4]])
                            CP('dve' if wi == 0 else 'act', kvo[wi][:, :], ps[4][:, :], [Bps[4]], [Bkvo[wi]])
                            r0 = t0 - NPRE + i * 128
                            S.dma(dst[r0:r0 + 128, :], kvo[wi][:, :], reads=[Bkvo[wi]])
                for i in range(nt):
                    for c in range(8):
                        MM(ps[5][:, i * 8:(i + 1) * 8], xnT[:, c, i * 128:(i + 1) * 128], w_in_sb[:, c, 3072:3080], c == 0, c == 7,
                           [BxnT[i], Bwin], [Bps[5]])
                dv = dtall[:, 0:nt, :]
                TT('dve', dv, ps[5][:, 0:nt * 8].rearrange("p (i h) -> p i h", h=8), dtb[:].unsqueeze(1).to_broadcast([128, nt, 8]),
                   ALU.add, [Bps[5], Bc], [Bdt])
                ACT(dv, dv, AF.Exp, [Bdt], [Bdt])
                ACT(dv, dv, AF.Ln, [Bdt, Bc], [Bdt], bias=onec[:, :])
                if kind == 'prefix':
                    TS('dve', dv, dv, pvt[:, 0:1], None, ALU.mult, None, [Bdt, Bc], [Bdt])
                for j in range(nx):
                    a_ = j % 2
                    ACT(acc[a_][:, 0:T], cbuf[:, j, 3:3 + T], AF.Identity, [Bcb[j], Bc], [Bacc[a_]], scale=cwT[:, j, 3:4], bias=cbT[:, j:j + 1])
                    for k in range(3):
                        STT(acc[a_][:, 0:T], cbuf[:, j, k:k + T], cwT[:, j, k:k + 1], acc[a_][:, 0:T], ALU.mult, ALU.add,
                            [Bcb[j], Bc, Bacc[a_]], [Bacc[a_]])
                    ACT(th[a_][:, 0:T], acc[a_][:, 0:T], AF.Tanh, [Bacc[a_]], [Bth[a_]])
                    STT(xbcT[:, j, 0:T], th[a_][:, 0:T], 1.0, acc[a_][:, 0:T], ALU.add, ALU.mult, [Bth[a_], Bacc[a_]], [Bxbc[j]])
                if kind == 'main' and t0 + T == NPRE + NMAIN:
                    for hf in range(2):
                        for j in range(4):
                            TR(ps[4][0:3, j * 128:(j + 1) * 128], cbuf[:, hf * 4 + j, T:T + 3], identf[:, :], [Bcb[hf * 4 + j], Bc], [Bps[4]])
                        CP('dve', kvo[hf][0:3, :], ps[4][0:3, :], [Bps[4]], [Bkvo[hf]])
                        S.dma(conv_loc[:, hf * 512:(hf + 1) * 512], kvo[hf][0:3, :], reads=[Bkvo[hf]])
                for j in range(8):
                    CP('pool', cbuf[:, j, 0:3], cbuf[:, j, T:T + 3], [Bcb[j]], [Bcb[j]])
                hoist_next()
                ssd_prep(128, nt, dtall[:, 0:nt, :], main)
                run_chunks([ssd_gen(128, slice(i * 128, (i + 1) * 128), BxnT[i], 'main' if main else 'prefix', dtall[:, i, :],
                                    slice(t0 - NPRE + i * 128, t0 - NPRE + (i + 1) * 128), i, i % 2, xT=xnT) for i in range(nt)])
                if kind == 'main' and t0 + T == NPRE + NMAIN:
                    emit_state(ssm_loc)

            Bqd, Bkd, Bvd = Buf("qd"), Buf("kd"), Buf("vd")
            st_list = [(s * 512, 512, 'prefix') for s in range(4)] + [(NPRE + s * 512, 512, 'main') for s in range(4)] + \
                      [(NPRE + NMAIN, 128, 'halo')]
            if skip_p:
                st_list = []
                for c_ in range(128):
                    bm_dma(c_)
            if st_list:
                a1_norm(st_list[0][0], st_list[0][1], 0)
            for si_, (t0, T, kind) in enumerate(st_list):
                def hoist(si_=si_):
                    if si_ + 1 < len(st_list):
                        a1_norm(st_list[si_ + 1][0], st_list[si_ + 1][1], (si_ + 1) % 2)
                a1_supertile(t0, T, kind, si_ % 2, hoist)
                for c_ in range(si_ * 16, min(128, si_ * 16 + 16)):
                    bm_dma(c_)


            if not skip_a1:
                scb = sbt(sa, "scb", [128, 8, 4, 11], F32)
                Bscb = [Buf() for _ in range(8)]
                sc3 = sbt(sa, "sc3", [128, 8, 12], F32)
                Bsc3 = Buf()
                hs12 = xt[0]
                Bhs12 = Bxt[0]
                norm_transpose_tile(xs_d[:, :], 32, slice(0, 32), BxnT[0])
                if ks1 == 0.1:
                    S.barrier(); S.finish(); return nc
                schunks = [('q', c, c * 128) for c in range(4)] + [('k', c, 512 + c * 128) for c in range(4)] + \
                          [('x', j, 2048 + j * 128) for j in range(8)]
                for ci, (knd, idx, col0) in enumerate(schunks):
                    bi = 1 + ci % 3
                    for c in range(8):
                        MM(ps[bi][:, 0:32], w_in_sb[:, c, col0:col0 + 128], xnT[:, c, 0:32], c == 0, c == 7, [BxnT[0], Bwin], [Bps[bi]])
                    if knd == 'q':
                        ACT(sQT[0:64, idx, 0, :], ps[bi][0:64, 0:32], AF.Copy, [Bps[bi]], [Bsq], scale=0.125)
                        ACT(sQT[64:128, idx, 1, :], ps[bi][64:128, 0:32], AF.Copy, [Bps[bi]], [Bsq], scale=0.125)
                    elif knd == 'k':
                        CP('dve', sKT[:, idx, :], ps[bi][:, 0:32], [Bps[bi]], [Bsq])
                    else:
                        CP('act' if idx % 2 == 0 else 'dve', scb[:, idx, :, 3:11], ps[bi][:, 0:32].rearrange("p (b t) -> p b t", b=4),
                           [Bps[bi]], [Bscb[idx]])
                if ks1 == 0.2:
                    S.barrier(); S.finish(); return nc
                kvm = os.environ.get('KV', 'all')
                for wi, (col0, dst) in enumerate(((512, ks_d), (1024, vs_d))):
                    for c in range(8):
                        MM(ps[4][0:32, :], xnT[:, c, 0:32], w_in_sb[:, c, col0:col0 + 512], c == 0, c == 7, [BxnT[0], Bwin], [Bps[4]])
                    if kvm in ('all', 'cp', 'cpdma', 'cpsvn'):
                        CP('dve', kvo[wi][0:32, :], ps[4][0:32, :], [Bps[4]], [Bkvo[wi]])
                    if wi == 1 and kvm in ('all', 'cpsvn'):
                        CP('act', sVn[0:32, :, 0:64], ps[4][0:32, :].rearrange("p (h e) -> p h e", h=8), [Bps[4]], [Bsq])
                    if kvm in ('all', 'cpdma'):
                        S.dma(dst[:, :], kvo[wi][0:32, :], reads=[Bkvo[wi]])
                if ks1 == 1:
                    S.barrier(); S.finish(); return nc
                S.dma(hs12[0:12, :], sconv_d[:, :], writes=[Bhs12])
                for hf in range(2):
                    for j in range(4):
                        TR(ps[4][:, j * 12:(j + 1) * 12], hs12[0:12, (hf * 4 + j) * 128:(hf * 4 + j + 1) * 128], identf[0:12, 0:12],
                           [Bhs12, Bc], [Bps[4]])
                    CP('dve', scb[:, hf * 4:(hf + 1) * 4, :, 0:3], ps[4][:, 0:48].rearrange("p (j b t) -> p j b t", j=4, b=4),
                       [Bps[4]], Bscb[hf * 4:(hf + 1) * 4])
                for j in range(8):
                    a_ = j % 2
                    accv = acc[a_][:, 0:32].rearrange("p (b t) -> p b t", b=4)
                    ACT(accv, scb[:, j, :, 3:11], AF.Identity, [Bscb[j], Bc], [Bacc[a_]], scale=cwT[:, j, 3:4], bias=cbT[:, j:j + 1])
                    for k in range(3):
                        STT(accv, scb[:, j, :, k:k + 8], cwT[:, j, k:k + 1], accv, ALU.mult, ALU.add, [Bscb[j], Bc, Bacc[a_]], [Bacc[a_]])
                    ACT(th[a_][:, 0:32], acc[a_][:, 0:32], AF.Tanh, [Bacc[a_]], [Bth[a_]])
                    STT(xbcT[:, j, 0:32], th[a_][:, 0:32], 1.0, acc[a_][:, 0:32], ALU.add, ALU.mult, [Bth[a_], Bacc[a_]], [Bxbc[j]])
                CP('pool', sc3[:].rearrange("p j (b t) -> p j b t", b=4), scb[:, :, :, 8:11], Bscb, [Bsc3])
                for hf in range(2):
                    for j in range(4):
                        TR(ps[4][0:12, j * 128:(j + 1) * 128], sc3[:, hf * 4 + j, :], identf[:, :], [Bsc3, Bc], [Bps[4]])
                    CP('dve', hs12[0:12, hf * 512:(hf + 1) * 512], ps[4][0:12, :], [Bps[4]], [Bhs12])
                S.dma(conv_s_d[:, :], hs12[0:12, :], reads=[Bhs12])
                if ks1 == 2:
                    S.barrier(); S.finish(); return nc
                for b in range(4):
                    for c in range(8):
                        MM(ps[5][0:8, b * 8:(b + 1) * 8], xnT[:, c, b * 8:(b + 1) * 8], w_in_sb[:, c, 3072:3080], c == 0, c == 7,
                           [BxnT[0], Bwin], [Bps[5]])
                dvs = dtall[0:8, 0:4, :]
                TT('dve', dvs, ps[5][0:8, 0:32].rearrange("p (i h) -> p i h", h=8), dtb[0:8, :].unsqueeze(1).to_broadcast([8, 4, 8]),
                   ALU.add, [Bps[5], Bc], [Bdt])
                ACT(dvs, dvs, AF.Exp, [Bdt], [Bdt])
                ACT(dvs, dvs, AF.Ln, [Bdt, Bc], [Bdt], bias=onec[0:8, :])
                if ks1 == 3:
                    S.barrier(); S.finish(); return nc
                ssd_prep(8, 4, dtall[0:8, 0:4, :], True)
                for b in range(4):
                    S.dma(sto[:], sssm_d[b].rearrange("(c p) n -> p c n", p=128), writes=[Bsto])
                    for c in range(4):
                        TR(ps[4][:, c * 128:(c + 1) * 128], sto[:, c, :], identf[:, :], [Bsto, Bc], [Bps[4]])
                    CP('dve', stf[:].rearrange("p h e -> p (h e)"), ps[4][:, :], [Bps[4]], [Bstf])
                    CP('act', stb[:, :], stf[:].rearrange("p h e -> p (h e)"), [Bstf], [Bstb])
                    for _ in ssd_gen(8, slice(b * 8, (b + 1) * 8), BxnT[0], 'main', dtall[0:8, b, :], slice(b * 8, (b + 1) * 8), b, b % 2,
                                     mixt=smixT, Bmix=BsmixS):
                        pass
                    emit_state(ssm_s_d[b])

        S.barrier()
        if stage <= 1:
            S.finish()
            return nc

        with ExitStack() as s2:
            sel = sbt(s2, "sel", [128, 64], F32)
            QT = [sbt(s2, f"QT{i}", [128, 2, NQ], BF16) for i in range(2)]
            KT = [sbt(s2, f"KT{i}", [128, NLOC], BF16) for i in range(2)]
            VT = [sbt(s2, f"VT{i}", [128, NLOC], BF16) for i in range(2)]
            Bqkv = [Buf() for _ in range(2)]
            NVB = 70
            Vb = sbt(s2, "Vb", [128, NVB, 2, 66], BF16)
            BVb = Buf()
            accA = sbt(s2, "accA", [128, 2, NQ], F32)
            BaccA = Buf()
            PT = [sbt(s2, f"PT{i}", [128, 512], BF16) for i in range(4)]
            BPT = [Buf() for _ in range(4)]
            rc = [sbt(s2, f"rc{i}", [64, 512], F32) for i in range(2)]
            Brc = [Buf() for _ in range(2)]
            TT('dve', Bdg[:], Bm[:, :, 0, 0:2], negd[:].unsqueeze(1).to_broadcast([128, 8, 2]), ALU.add, [BBm, Bc], [Bc])
            if stage == 1.2:
                S.barrier(); S.finish(); return nc
            wtmp = [sbt(s2, f"wtmp{i}", [128, 8, 256], BF16) for i in range(2)]
            Bwtmp = [Buf(f"wtmp{i}") for i in range(2)]
            Bwup_d = Buf("wup_d")
            w_up_v = w_up.rearrange("(c p) n -> p c n", p=128)

            def prep_wup(j):
                s = j % 2
                S.dma(wtmp[s][:, :, 0:128], w_up_v[:, :, j * 128:(j + 1) * 128], writes=[Bwtmp[s]], eng='pool')
                S.dma(wtmp[s][:, :, 128:256], w_up_v[:, :, DFF + j * 128:DFF + (j + 1) * 128], writes=[Bwtmp[s]], eng='pool')
                S.dma(wup_d[j, :, :], wtmp[s][:].rearrange("p c n -> p (c n)"), reads=[Bwtmp[s]], writes=[Bwup_d])


            wtb = [sbt(s2, f"wtb{i}", [128, D], BF16) for i in range(2)]
            Bwtb = [Buf() for _ in range(2)]
            BwB_d = Buf("wB_d")
            wB_src = [w_out[c * 128:(c + 1) * 128, :] for c in range(8)] + [w_down[j * 128:(j + 1) * 128, :] for j in range(NG)] + \
                     [w_ple_proj[c * 128:(c + 1) * 128, :] for c in range(2)] + [w_ple_gate[c * 128:(c + 1) * 128, :] for c in range(8)]

            fcwT = sbt(s2, "fcwT", [128, 2 * NG, 3], F32)
            Bfcw = Buf()
            for k in range(3):
                for hf in range(2):
                    S.dma(fcwT[:, hf * NG:(hf + 1) * NG, k], ffn_conv_w[k, hf * DFF:(hf + 1) * DFF].rearrange("(c p) -> p c", p=128),
                          writes=[Bfcw], allow_slow_non_contiguous=True)
            dgs = [sbt(s2, f"dgs{i}", [128, 2, 3, 128], BF16) for i in range(2)]
            Bdgs = [Buf() for _ in range(2)]
            Bdg_d = Buf("dg_d")

            def prep_dg(j):
                sl = j % 2
                for k in range(3):
                    ACT(dgs[sl][:, 0, k, :], identb[:, :], AF.Copy, [Bc, Bfcw], [Bdgs[sl]], scale=fcwT[:, j, k:k + 1])
                    TS('dve', dgs[sl][:, 1, k, :], identb[:, :], fcwT[:, NG + j, k:k + 1], None, ALU.mult, None, [Bc, Bfcw], [Bdgs[sl]])
                S.dma(dg_d[j, :, :], dgs[sl][:].rearrange("p a k n -> p (a k n)"), reads=[Bdgs[sl]], writes=[Bdg_d])

            def prep_wB(i):
                sl = i % 2
                S.dma(wtb[sl][:, :], wB_src[i], writes=[Bwtb[sl]], eng='pool')
                S.dma(wB_d[i, :, :], wtb[sl][:, :], reads=[Bwtb[sl]], writes=[BwB_d])

            sKc = sbt(s2, "sKc", [128, 4, 13 * 128], BF16)
            BsKc = Buf()
            sVc = sbt(s2, "sVc", [128, 13, 8, 66], BF16)
            BsVc = Buf()
            kst = [sbt(s2, f"kst{i}", [128, 512], F32) for i in range(2)]
            Bkst = [Buf() for _ in range(2)]
            vst = [sbt(s2, f"vst{i}", [128, 512], F32) for i in range(2)]
            Bvst = [Buf() for _ in range(2)]
            accS = sbt(s2, "accS", [128, 4, 2, 32], F32)
            BaccS = [Buf() for _ in range(4)]
            cmt = sbt(s2, "cmt", [32, 3, 32], F32)
            Bown = sbt(s2, "Bown", [32, 8, 3, 32], BF16)
            S.dma(cmt[:], c_cm[:, :, :], writes=[Bc])
            TT('dve', Bown[:], Bm[0:32, :, 0, 0:32].unsqueeze(2).to_broadcast([32, 8, 3, 32]),
               cmt[:].unsqueeze(1).to_broadcast([32, 8, 3, 32]), ALU.add, [Bc], [Bc])
            MEMSET('pool', sVc[:].rearrange("p t h e -> p (t h e)"), 1.0, [BsVc])
            MEMSET('dve', sel[:], 0.0, [Bc])
            MEMSET('dve', sel[64:65, :], 1.0, [Bc])
            MEMSET('dve', accA[:], 1.0, [BaccA])
            for i_ in range(2):
                MEMSET('pool', QT[i_][:].rearrange("p h q -> p (h q)"), 0.0, [Bqkv[i_]])
            MEMSET('pool', Vb[:].rearrange("p b h e -> p (b h e)"), 1.0, [BVb])
            for (a0, a1) in ((0, 1), (18, 22), (38, 54)):
                TS('dve', Vb[:, a0:a1, :, 64:65], Vb[:, a0:a1, :, 64:65], pvt[:, 0:1], None, ALU.mult, None, [BVb, Bc], [BVb])

            vblocks = []
            for tau in range(15, 33):
                vblocks.append((tau - 15, 128 * tau, 1))
            for sg in range(3, 8):
                for r in range(4):
                    vblocks.append((18 + (sg - 3) * 4 + r, 512 * sg + r, 4))
            for z_ in range(2):
                for r in range(16):
                    vblocks.append((38 + 16 * z_ + r, 2048 * z_ + r, 16))
            assert len(vblocks) == NVB

            cnt = {'s': 0, 'o': 0, 'pt': 0, 'n': 0, 'ev': 0}

            def cols(c0, step, n):
                return slice(c0, c0 + step * (n - 1) + 1, step) if step > 1 else slice(c0, c0 + n)

            def attn_core(N, nkeys, k_aps, q_fn, v_fn, bm_ap, acc_ap, mode, pv_rep, rK, rQ, rV, wAcc):
                nk = len(k_aps)
                bi = 1 + cnt['s'] % 3
                cnt['s'] += 1
                psv = ps[bi][:, :].rearrange("p (h k q) -> p h k q", h=2, k=2)
                first = True
                for h2 in range(2):
                    for k in range(nk):
                        MM(psv[:nkeys, h2, k, 0:N], k_aps[k], q_fn(h2), first, False, rK + rQ, [Bps[bi]])
                        first = False
                if N == 128 and nk == 2:
                    MM(psv[:nkeys, :, 0:nk, 0:N], identb[:nkeys, :nkeys], bm_ap, False, True, [Bc], [Bps[bi]])
                else:
                    for h2 in range(2):
                        for k in range(nk):
                            MM(psv[:nkeys, h2, k, 0:N], identb[:nkeys, :nkeys], bm_ap[:, h2, k, :], False, (h2 == 1 and k == nk - 1),
                               [Bc], [Bps[bi]])
                pi = cnt['pt'] % 4
                cnt['pt'] += 1
                ptv = PT[pi][:, :].rearrange("p (h k q) -> p h k q", h=2, k=2)
                ACT(ptv[:nkeys, :, 0:nk, 0:N], psv[:nkeys, :, 0:nk, 0:N], AF.Exp, [Bps[bi]], [BPT[pi]])
                def stage2():
                    oi = 4 + cnt['o'] % 2
                    cnt['o'] += 1
                    pso = ps[oi][0:65, 0:256].rearrange("p (h q) -> p h q", h=2)
                    first2 = True
                    for h2 in range(2):
                        tot = nk * pv_rep
                        ii = 0
                        for k in range(nk):
                            for _rep in range(pv_rep):
                                ii += 1
                                MM(pso[:, h2, 0:N], v_fn(k, h2), ptv[:nkeys, h2, k, 0:N], first2, ii == tot, rV + [BPT[pi]], [Bps[oi]])
                                first2 = False
                    if mode == 'copy':
                        CP('act', acc_ap, pso[:, :, 0:N], [Bps[oi]], wAcc)
                    else:
                        TT('dve', acc_ap, acc_ap, pso[:, :, 0:N], ALU.add, [Bps[oi]] + wAcc, wAcc)

                if pending:
                    pending.pop(0)()
                pending.append(stage2)

            pending = []

            def flush_pending():
                while pending:
                    pending.pop(0)()

            def attn_unit(p, s, N, qc, kbs, bm_ap, accc, mode, pv_rep=1):
                attn_core(N, 128, [KT[s][:, cols(kc0, kst, 128)] for (kc0, kst, _) in kbs],
                          lambda h2: QT[s][:, h2, cols(qc[0], qc[1], N)],
                          lambda k, h2: Vb[:, kbs[k][2], h2, 0:65], bm_ap,
                          accA[0:65, :, cols(accc[0], accc[1], N)], mode, pv_rep, [Bqkv[s]], [], [BVb], [BaccA])

            def bm_std(p, br, N):
                return Bm[:, 2 * p:2 * p + 2, br, :].rearrange("p h (k q) -> p h k q", k=2)[:, :, :, 0:N]

            def bm_prev(p, br, N):
                return Bm[:, 2 * p:2 * p + 2, br, 128:128 + N].unsqueeze(2)

            for p in range(4):
                s = p % 2
                for j in range(p * 6, min(NG, p * 6 + 6)):
                    prep_wup(j)
                for j in range(p * 10, p * 10 + 10):
                    prep_wB(j)
                for j in range(p * 6, min(NG, p * 6 + 6)):
                    prep_dg(j)
                S.dma(QT[s][0:64, 0, :], qT_d[p, 0:64, :], reads=[Bqd], writes=[Bqkv[s]])
                S.dma(QT[s][64:128, 1, :], qT_d[p, 64:128, :], reads=[Bqd], writes=[Bqkv[s]])
                S.dma(KT[s][:, :], kT_d[p, :, :], reads=[Bkd], writes=[Bqkv[s]])
                S.dma(VT[s][:, :], vT_d[p, :, :], reads=[Bvd], writes=[Bqkv[s]])
                for g0 in range(0, NVB, 8):
                    grp = vblocks[g0:g0 + 8]
                    for sl, (vbi, c0, st_) in enumerate(grp):
                        TR(psT[:, sl * 128:(sl + 1) * 128], VT[s][:, cols(c0, st_, 128)], identb[:, :], [Bqkv[s], Bc], [BpsT])
                    n = len(grp)
                    eng = 'act' if (cnt['ev'] % 2 == 0) else 'dve'
                    cnt['ev'] += 1
                    CP(eng, Vb[:, g0:g0 + n, :, 0:64], psT[:, 0:n * 128].rearrange("p (b h e) -> p b h e", b=n, h=2), [BpsT], [BVb])
                if stage == 1.4:
                    S.barrier(); S.finish(); return nc
                for n in range(16):
                    attn_unit(p, s, 128, (128 * n, 1), [(NPRE + 128 * n, 1, n + 1), (NPRE + 128 * (n - 1), 1, n)],
                              bm_std(p, 0, 128), (128 * n, 1), 'copy')
                attn_unit(p, s, 2, (2048, 1), [(NPRE + 2048, 1, 17), (NPRE + 1920, 1, 16)], bm_std(p, 0, 2), (2048, 1), 'copy')
                if stage == 1.6:
                    S.barrier(); S.finish(); return nc
                for sg in range(4):
                    for r in range(4):
                        attn_unit(p, s, 128, (512 * sg + r, 4),
                                  [(NPRE + 512 * sg + r, 4, 18 + (sg + 1) * 4 + r), (NPRE + 512 * (sg - 1) + r, 4, 18 + sg * 4 + r)],
                                  bm_std(p, 1, 128), (512 * sg + r, 4), 'add')
                for r in range(16):
                    attn_unit(p, s, 128, (r, 16), [(NPRE + r, 16, 38 + 16 + r), (r, 16, 38 + r)], bm_std(p, 2, 128), (r, 16), 'add')
                for qi in range(2):
                    attn_unit(p, s, 1, (2048 + qi, 1), [(NPRE + 1536 + qi, 4, 18 + 16 + qi)], bm_prev(p, 1, 1), (2048 + qi, 1), 'add')
                    attn_unit(p, s, 1, (2048 + qi, 1), [(NPRE + qi, 16, 38 + 16 + qi)], bm_prev(p, 2, 1), (2048 + qi, 1), 'add')
                attn_unit(p, s, 2, (2048, 1), [(NPRE + 2048, 1, 17)], Bdg[:, 2 * p:2 * p + 2, 0:2].unsqueeze(2), (2048, 1), 'add', pv_rep=2)
                flush_pending()
                for h2 in range(2):
                    for c0 in range(0, NQ, 512):
                        n = min(512, NQ - c0)
                        bi = 6 + cnt['n'] % 2
                        ri = cnt['n'] % 2
                        cnt['n'] += 1
                        MM(ps[bi][0:64, 0:n], sel[0:65, 0:64], accA[0:65, h2, c0:c0 + n], True, True, [Bc, BaccA], [Bps[bi]])
                        S.op('dve', lambda e, o=rc[ri][0:64, 0:n], i_=ps[bi][0:64, 0:n]: e.reciprocal(out=o, in_=i_), [Bps[bi]], [Brc[ri]])
                        TT('dve', mixT[h2 * 64:(h2 + 1) * 64, p, c0:c0 + n], accA[0:64, h2, c0:c0 + n], rc[ri][0:64, 0:n], ALU.mult,
                           [BaccA, Brc[ri]], [BmixA[p]])

            if not skip_a1:
                for p in range(4):
                    for br in range(3):
                        attn_core(32, 32, [sKT[:, p, 0:32]], lambda h2, p=p: sQT[:, p, h2, 0:32],
                                  lambda k, h2, p=p: sVn[0:32, 2 * p + h2, 0:65], Bown[0:32, 2 * p:2 * p + 2, br, :].unsqueeze(2),
                                  accS[0:65, p, :, :], 'copy' if br == 0 else 'add', 1, [Bsq], [], [Bsq], [BaccS[p]])
                flush_pending()
                tix = [0]
                for b in range(4):
                    tiles = [(1920, 1)] + [(1536 + r, 4) for r in range(4)] + [(r, 16) for r in range(8)]
                    for ti, (r0, st_) in enumerate(tiles):
                        sl = tix[0] % 2
                        tix[0] += 1
                        rows = slice(r0, r0 + 127 * st_ + 1, st_) if st_ > 1 else slice(r0, r0 + 128)
                        S.dma(kst[sl][:, :], ck_d[b, rows, :], writes=[Bkst[sl]])
                        S.dma(vst[sl][:, :], cv_d[b, rows, :], writes=[Bvst[sl]])
                        for c in range(4):
                            TR(ps[7][:, c * 128:(c + 1) * 128], kst[sl][:, c * 128:(c + 1) * 128], identf[:, :], [Bkst[sl], Bc], [Bps[7]])
                        CP('act' if ti % 2 == 0 else 'dve', sKc[:, :, ti * 128:(ti + 1) * 128],
                           ps[7][:, :].rearrange("p (c k) -> p c k", c=4), [Bps[7]], [BsKc])
                        CP('dve' if ti % 2 == 0 else 'act', sVc[:, ti, :, 0:64], vst[sl][:, :].rearrange("p (h e) -> p h e", h=8), [Bvst[sl]], [BsVc])
                    for p in range(4):
                        def unit(ti, N, qcol0, qstep, br, p=p, b=b):
                            attn_core(N, 128, [sKc[:, p, ti * 128:(ti + 1) * 128]],
                                      lambda h2: sQT[:, p, h2, cols(b * 8 + qcol0, qstep, N)],
                                      lambda k, h2: sVc[:, ti, 2 * p + h2, 0:65], bm_prev(p, br, N),
                                      accS[0:65, p, :, cols(b * 8 + qcol0, qstep, N)], 'add', 1, [BsKc], [Bsq], [BsVc], [BaccS[p]])
                        unit(0, 8, 0, 1, 0)
                        for r in range(4):
                            unit(1 + r, 2, r, 4, 1)
                        for r in range(8):
                            unit(5 + r, 1, r, 1, 2)
                    flush_pending()
                for p in range(4):
                    for h2 in range(2):
                        bi = 6 + cnt['n'] % 2
                        ri = cnt['n'] % 2
                        cnt['n'] += 1
                        MM(ps[bi][0:64, 0:32], sel[0:65, 0:64], accS[0:65, p, h2, :], True, True, [Bc, BaccS[p]], [Bps[bi]])
                        S.op('dve', lambda e, o=rc[ri][0:64, 0:32], i_=ps[bi][0:64, 0:32]: e.reciprocal(out=o, in_=i_), [Bps[bi]], [Brc[ri]])
                        TT('dve', smixT[h2 * 64:(h2 + 1) * 64, p, :], accS[0:64, p, h2, :], rc[ri][0:64, 0:32], ALU.mult,
                           [BaccS[p], Brc[ri]], [BsmixA[p]])
        sA.close()
        S.barrier()
        if stage <= 2:
            S.finish()
            return nc

        with ExitStack() as s3:
            w_out_sb = sbt(s3, "w_out_sb", [128, 8, D], BF16)
            w_dn_sb = sbt(s3, "w_dn_sb", [128, NG, D], BF16)
            w_pp_sb = sbt(s3, "w_pp_sb", [128, 2, D], BF16)
            w_pg_sb = sbt(s3, "w_pg_sb", [128, 8, D], BF16)
            BwB = Buf()
            S.dma(w_out_sb[:], wB_d[0:8, :, :].rearrange("c p n -> p c n"), reads=[BwB_d], writes=[BwB])
            S.dma(w_dn_sb[:], wB_d[8:30, :, :].rearrange("c p n -> p c n"), reads=[BwB_d], writes=[BwB])
            S.dma(w_pp_sb[:], wB_d[30:32, :, :].rearrange("c p n -> p c n"), reads=[BwB_d], writes=[BwB])
            S.dma(w_pg_sb[:], wB_d[32:40, :, :].rearrange("c p n -> p c n"), reads=[BwB_d], writes=[BwB])
            fcbT = sbt(s3, "fcbT", [128, 2 * NG], F32)
            gple_b = sbt(s3, "gple_b", [128, D], F32)
            gfin_b = sbt(s3, "gfin_b", [128, D], F32)
            for hf in range(2):
                S.dma(fcbT[:, hf * NG:(hf + 1) * NG], ffn_conv_b[hf * DFF:(hf + 1) * DFF].rearrange("(c p) -> p c", p=128),
                      writes=[Bc], allow_slow_non_contiguous=True)
            S.dma(gple_b[:], g_ple.partition_broadcast(128), writes=[Bc])
            S.dma(gfin_b[:], g_final.partition_broadcast(128), writes=[Bc])

            TB = 256
            xtb = [sbt(s3, f"xtb{i}", [128, D], F32) for i in range(2)]
            Bxtb = [Buf() for _ in range(2)]
            ptb = [sbt(s3, f"ptb{i}", [128, 256], F32) for i in range(2)]
            Bptb = [Buf() for _ in range(2)]
            hh = sbt(s3, "hh", [128, 2, D], F32)
            Bhh = [Buf() for _ in range(2)]
            hn2 = [sbt(s3, f"hn{i}", [128, D], BF16) for i in range(2)]
            Bhn2 = [Buf() for _ in range(2)]
            rsb2 = [sbt(s3, f"rsb{i}", [128, 8], F32) for i in range(2)]
            Brsb2 = [Buf() for _ in range(2)]

            def lockstep(gens):
                gens = list(gens)
                while gens:
                    for g in list(gens):
                        try:
                            next(g)
                        except StopIteration:
                            gens.remove(g)

            hnT = sbt(s3, "hnT", [128, 8, TB], BF16)
            BhnT = [Buf() for _ in range(2)]
            wg = [sbt(s3, f"wg{i}", [128, 8, 256], BF16) for i in range(3)]
            Bwg = [Buf() for _ in range(3)]
            ub = [sbt(s3, f"ub{i}", [128, 2, TB + 2], BF16) for i in range(2)]
            Bub = [Buf() for _ in range(2)]
            hist = sbt(s3, "hist", [128, NG, 2, 2], BF16)
            Bhist = [Buf() for _ in range(NG)]
            ffo = sbt(s3, "ffo", [128, 2, NG, 2], F32)
            Bffo = Buf()
            ffs = sbt(s3, "ffs", [8, 512], F32)
            ffo_s = sbt(s3, "ffo_s", [128, 2, NG, 4, 2], F32)
            hist_s = sbt(s3, "hist_s", [128, 2, NG, 4, 2], BF16)
            Bhist_s = Buf()
            hs8 = sbt(s3, "hs8", [8, 512], F32)
            Bhs8 = Buf()
            Bffs = Buf()
            dg = [sbt(s3, f"dg{i}", [128, 2, 3, 128], BF16) for i in range(4)]
            Bdgm = [Buf() for _ in range(4)]
            sa_ = [sbt(s3, f"sa{i}", [128, TB], F32) for i in range(2)]
            Bsa = [Buf() for _ in range(2)]
            gT = sbt(s3, "gT", [128, NG, TB], BF16)
            BgT = Buf()
            ppb2 = [sbt(s3, f"ppb{i}", [128, 256], BF16) for i in range(2)]
            Bppb2 = [Buf() for _ in range(2)]
            ppT2 = [sbt(s3, f"ppT{i}", [128, 2, 128], BF16) for i in range(2)]
            BppT2 = [Buf() for _ in range(2)]
            t12 = xtb
            Bt12 = Bxtb
            h2T2 = [sbt(s3, f"h2T{i}", [128, 8, 128], BF16) for i in range(2)]
            Bh2T2 = [Buf() for _ in range(2)]
            gt2 = [sbt(s3, f"gt{i}", [128, D], F32) for i in range(2)]
            Bgt2 = [Buf() for _ in range(2)]
            MEMSET('dve', hist[:].rearrange("p j a t -> p (j a t)"), 0.0, Bhist)

            tcb = [0]

            def b_supertile(t0, T, samp=False):
                L = 32 if samp else 128
                nt = 1 if samp else T // 128
                mixsrc = smixT if samp else mixT
                Bmixsrc = ([BsmixS] + BsmixA) if samp else ([BmixS] + BmixA)
                def head_gen(i):
                    s = i % 2
                    hn, Bhn, rsb, Brsb = hn2[i], Bhn2[i], rsb2[i], Brsb2[i]
                    r0 = t0 + i * 128
                    xsrc = xs_d[0:32, :] if samp else xl[NPRE + r0:NPRE + r0 + 128, :]
                    S.dma(xtb[s][:L, :], xsrc, writes=[Bxtb[s]])
                    for hf in range(2):
                        bk = 1 + 2 * (i % 2) + hf
                        for c in range(8):
                            MM(ps[bk][:L, :], mixsrc[:, c, r0:r0 + L], w_out_sb[:, c, hf * 512:(hf + 1) * 512], c == 0, c == 7,
                               [BwB] + Bmixsrc, [Bps[bk]])
                        TT('dve', hh[:L, i, hf * 512:(hf + 1) * 512], xtb[s][:L, hf * 512:(hf + 1) * 512], ps[bk][:L, :], ALU.add,
                           [Bxtb[s], Bps[bk]], [Bhh[i]])
                        yield
                    rms_rstd(hh[:L, i, :], L, hn[:L, :], rsb[:L, 0:1], rsb[:L, 1:2], [Bhh[i]], [Bhn, Brsb], 1.0 / D)
                    yield
                    ACT(hn[:L, :], hh[:L, i, :], AF.Copy, [Bhh[i], Brsb], [Bhn], scale=rsb[:L, 1:2])
                    yield
                    for c in range(8):
                        TR(psT[:, c * 128:c * 128 + L], hn[:L, c * 128:(c + 1) * 128], identb[:L, :L], [Bhn, Bc], [BpsT])
                    TT('dve', hnT[:, :, i * 128:i * 128 + L], psT[:, :].rearrange("p (c t) -> p c t", c=8)[:, :, 0:L],
                       gffnT[:].unsqueeze(2).to_broadcast([128, 8, L]), ALU.mult, [BpsT, Bc], [BhnT[i]])
                    yield

                lockstep([head_gen(i) for i in range(nt)])
                last_main = (t0 + T == NMAIN) or samp

                def up_stage(j):
                    ws = j % 3
                    S.dma(wg[ws][:].rearrange("p c n -> p (c n)"), wup_d[j, :, :], reads=[Bwup_d], writes=[Bwg[ws]])
                    us = j % 2
                    if samp:
                        ubv = ub[us][:, :, 0:40].rearrange("p a (b t) -> p a b t", b=4)
                        CP('pool', ubv[:, :, :, 0:2], hist_s[:, :, j, :, :], [Bhist_s], [Bub[us]])
                    else:
                        CP('pool', ub[us][:, :, 0:2], hist[:, j, :, :], [Bhist[j]], [Bub[us]])
                    for ab in range(2):
                        ubk = (3 + ab) if j % 2 == 0 else (1 + ab)
                        for c in range(8):
                            MM(ps[ubk][:, 0:T], wg[ws][:, c, ab * 128:(ab + 1) * 128], hnT[:, c, 0:T], c == 0, c == 7,
                               [Bwg[ws]] + BhnT[:nt], [Bps[ubk]])
                        if samp:
                            pv4 = ps[ubk][:, 0:32].rearrange("p (b t) -> p b t", b=4)
                            CP('act' if ab == 0 else 'dve', ubv[:, ab, :, 2:10], pv4, [Bps[ubk]], [Bub[us]])
                            CP('dve', ffo_s[:, ab, j, :, :], pv4[:, :, 6:8], [Bps[ubk]], [Bffo])
                        else:
                            CP('act' if ab == 0 else 'dve', ub[us][:, ab, 2:2 + T], ps[ubk][:, 0:T], [Bps[ubk]], [Bub[us]])
                            if last_main:
                                CP('dve', ffo[:, ab, j, :], ps[ubk][:, T - 2:T], [Bps[ubk]], [Bffo])
                    if not samp:
                        CP('pool', hist[:, j, :, :], ub[us][:, :, T:T + 2], [Bub[us]], [Bhist[j]])
                    S.dma(dg[j % 4][:].rearrange("p a k n -> p (a k n)"), dg_d[j, :, :], reads=[Bdg_d], writes=[Bdgm[j % 4]])

                def conv_stage(j):
                    us = j % 2
                    for ab in range(2):
                        for k in range(3):
                            if samp:
                                ubv = ub[us][:, :, 0:40].rearrange("p a (b t) -> p a b t", b=4)
                                rhs = ubv[:, ab, :, k:k + 8]
                                o_ = ps[5 + ab][:, 0:32].rearrange("p (b t) -> p b t", b=4)
                            else:
                                rhs = ub[us][:, ab, k:k + T]
                                o_ = ps[5 + ab][:, 0:T]
                            MM(o_, dg[j % 4][:, ab, k, :], rhs, k == 0, k == 2, [Bdgm[j % 4], Bub[us]], [Bps[5 + ab]])
                    ACT(sa_[us][:, 0:T], ps[5][:, 0:T], AF.Silu, [Bps[5], Bc], [Bsa[us]], bias=fcbT[:, j:j + 1])
                    STT(gT[:, j, 0:T], ps[6][:, 0:T], fcbT[:, NG + j:NG + j + 1], sa_[us][:, 0:T], ALU.add, ALU.mult,
                        [Bps[6], Bc, Bsa[us]], [BgT])

                for j in range(NG):
                    up_stage(j)
                    if j > 0:
                        conv_stage(j - 1)
                conv_stage(NG - 1)

                def tail_gen(i):
                    s = i % 2
                    hn, Bhn, rsb, Brsb = hn2[i], Bhn2[i], rsb2[i], Brsb2[i]
                    ppb, Bppb, ppT, BppT = ppb2[i], Bppb2[i], ppT2[i], BppT2[i]
                    t1, Bt1, h2T, Bh2T, gt, Bgt = t12[i], Bt12[i], h2T2[i], Bh2T2[i], gt2[i], Bgt2[i]
                    b1 = [1 + 2 * (i % 2), 2 + 2 * (i % 2)]
                    b2 = [5, 6] if i % 2 == 0 else [7, 6]
                    r0 = t0 + i * 128
                    psrc = psm_d[0:32, :] if samp else pl[r0:r0 + 128, :]
                    S.dma(ptb[s][:L, :], psrc, writes=[Bptb[s]])
                    for hf in range(2):
                        for j in range(NG):
                            MM(ps[b1[hf]][:L, :], gT[:, j, i * 128:i * 128 + L], w_dn_sb[:, j, hf * 512:(hf + 1) * 512], j == 0, j == NG - 1,
                               [BgT, BwB], [Bps[b1[hf]]])
                        TT('dve', hh[:L, i, hf * 512:(hf + 1) * 512], hh[:L, i, hf * 512:(hf + 1) * 512], ps[b1[hf]][:L, :], ALU.add,
                           [Bhh[i], Bps[b1[hf]]], [Bhh[i]])
                        yield
                    CP('act', ppb[:L, :], ptb[s][:L, :], [Bptb[s]], [Bppb])
                    yield
                    for c in range(2):
                        TR(psT[:, c * 128:c * 128 + L], ppb[:L, c * 128:(c + 1) * 128], identb[:L, :L], [Bppb, Bc], [BpsT])
                    CP('dve', ppT[:, :, 0:L], psT[:, 0:256].rearrange("p (c t) -> p c t", c=2)[:, :, 0:L], [BpsT], [BppT])
                    yield
                    for hf in range(2):
                        for c in range(2):
                            MM(ps[b1[hf]][:L, :], ppT[:, c, 0:L], w_pp_sb[:, c, hf * 512:(hf + 1) * 512], c == 0, c == 1, [BppT, BwB], [Bps[b1[hf]]])
                        ACT(t1[:L, hf * 512:(hf + 1) * 512], ps[b1[hf]][:L, :], AF.Square, [Bps[b1[hf]]], [Bt1, Brsb], accum_out=rsb[:L, 2 + hf:3 + hf])
                        yield
                    TT('dve', rsb[:L, 4:5], rsb[:L, 2:3], rsb[:L, 3:4], ALU.add, [Brsb], [Brsb])
                    TS('dve', rsb[:L, 4:5], rsb[:L, 4:5], 1.0 / D, EPS, ALU.mult, ALU.add, [Brsb], [Brsb])
                    yield
                    TT('pool', rsb[:L, 5:6], rsb[:L, 4:5], mhalf[:L, :], ALU.pow, [Brsb, Bc], [Brsb])
                    yield
                    for hf in range(2):
                        STT(t1[:L, hf * 512:(hf + 1) * 512], ps[b1[hf]][:L, :], rsb[:L, 5:6], gple_b[:L, hf * 512:(hf + 1) * 512], ALU.mult, ALU.mult,
                            [Bps[b1[hf]], Brsb, Bc], [Bt1])
                    yield
                    CP('act', hn[:L, :], hh[:L, i, :], [Bhh[i]], [Bhn])
                    yield
                    for c in range(8):
                        TR(psT[:, c * 128:c * 128 + L], hn[:L, c * 128:(c + 1) * 128], identb[:L, :L], [Bhn, Bc], [BpsT])
                    CP('dve', h2T[:, :, 0:L], psT[:, :].rearrange("p (c t) -> p c t", c=8)[:, :, 0:L], [BpsT], [Bh2T])
                    yield
                    for hf in range(2):
                        for c in range(8):
                            MM(ps[b2[hf]][:L, :], h2T[:, c, 0:L], w_pg_sb[:, c, hf * 512:(hf + 1) * 512], c == 0, c == 7, [Bh2T, BwB], [Bps[b2[hf]]])
                        ACT(gt[:L, hf * 512:(hf + 1) * 512], ps[b2[hf]][:L, :], AF.Tanh, [Bps[b2[hf]]], [Bgt], scale=0.5)
                        yield
                    STT(gt[:L, :], gt[:L, :], 1.0, t1[:L, :], ALU.add, ALU.mult, [Bgt, Bt1], [Bgt])
                    yield
                    STT(hh[:L, i, :], gt[:L, :], 0.5, hh[:L, i, :], ALU.mult, ALU.add, [Bgt, Bhh[i]], [Bhh[i]])
                    yield
                    rms_rstd(hh[:L, i, :], L, hn[:L, :], rsb[:L, 6:7], rsb[:L, 7:8], [Bhh[i]], [Bhn, Brsb], 1.0 / D)
                    yield
                    STT(gt[:L, :], hh[:L, i, :], rsb[:L, 7:8], gfin_b[:L, :], ALU.mult, ALU.mult, [Bhh[i], Brsb, Bc], [Bgt])
                    ydst = ys_d[0:32, :] if samp else y_loc[r0:r0 + 128, :]
                    S.dma(ydst, gt[:L, :], reads=[Bgt])
                    yield

                lockstep([tail_gen(i) for i in range(nt)])
                if last_main:
                    nr = 8 if samp else 2
                    for rd in range(11):
                        for q4 in range(4):
                            ch = rd * 4 + q4
                            src = ffo_s[:, ch // NG, ch % NG, :, :].rearrange("p b t -> p (b t)") if samp else ffo[:, ch // NG, ch % NG, :]
                            TR(ps[7][0:nr, q4 * 128:(q4 + 1) * 128], src, identf[:, :], [Bffo, Bc], [Bps[7]])
                        CP('dve', ffs[0:nr, :], ps[7][0:nr, :], [Bps[7]], [Bffs])
                        fdst = ffn_s_d if samp else ffn_loc
                        S.dma(fdst[:, rd * 512:(rd + 1) * 512], ffs[0:nr, :], reads=[Bffs])

            for t0 in range(0, NMAIN, TB):
                b_supertile(t0, TB)
            b_supertile(NMAIN, NHALO)
            if not skip_a1:
                for rd in range(11):
                    S.dma(hs8[:, :], sffn_d[:, rd * 512:(rd + 1) * 512], writes=[Bhs8])
                    for q4 in range(4):
                        TR(ps[7][:, q4 * 8:(q4 + 1) * 8], hs8[0:8, q4 * 128:(q4 + 1) * 128], identf[0:8, 0:8], [Bhs8, Bc], [Bps[7]])
                    for q4 in range(4):
                        ch = rd * 4 + q4
                        CP('dve', hist_s[:, ch // NG, ch % NG, :, :], ps[7][:, q4 * 8:(q4 + 1) * 8].rearrange("p (b t) -> p b t", b=4),
                           [Bps[7]], [Bhist_s])
                b_supertile(0, 32, samp=True)

        S.finish()
    return nc


_NC_CACHE = {}


def _get_nc():
    if 'nc' not in _NC_CACHE:
        import os
        _NC_CACHE['nc'] = build_nc(float(os.environ.get('KSTAGE', '99')))
    return _NC_CACHE['nc']


def _prep_inputs(inputs):
    f = lambda a: np.ascontiguousarray(np.asarray(a, dtype=np.float32))
    x = f(inputs["x_prompt"]); p = f(inputs["p_prompt"])[0]
    consts = host_consts()
    shared = dict(consts)
    for k in ["rel_bias"]:
        shared[k] = f(inputs[k])
    for k in ["g_mix", "w_in", "conv_w", "conv_b", "dt_bias", "a_log", "d_skip", "g_ssm", "w_out", "g_ffn", "w_up",
              "ffn_conv_w", "ffn_conv_b", "w_down", "w_ple_proj", "g_ple", "w_ple_gate"]:
        shared[k] = f(inputs[k])[0]
    shared["g_final"] = f(inputs["g_final"])
    xs = f(inputs["x_sample"]); psm = f(inputs["p_sample"])[0]
    ck = f(inputs["cache_k"])[0]; cv = f(inputs["cache_v"])[0]
    sssm = f(inputs["state_ssm"])[0]; sconv = f(inputs["state_conv"])[0]; sffn = f(inputs["state_ffn_conv"])[0]
    in_maps = []
    for core in range(8):
        b, half = core // 2, core % 2
        if half == 0:
            xl = np.concatenate([np.zeros((NPRE, D), np.float32), x[b, 0:NMAIN + NHALO]], axis=0)
            pl = p[b, 0:NQ]
        else:
            xl = np.concatenate([x[b], np.zeros((NHALO, D), np.float32)], axis=0)
            pl = np.concatenate([p[b, NPRE:], np.zeros((NHALO, 256), np.float32)], axis=0)
        m = dict(shared)
        m["xl"] = np.ascontiguousarray(xl)
        m["pl"] = np.ascontiguousarray(pl)
        m["pv"] = np.full((128, 1), float(half), np.float32)
        sq = slice(4 * core, 4 * core + 4)
        m["xs"] = np.ascontiguousarray(xs[sq].reshape(32, D))
        m["psm"] = np.ascontiguousarray(psm[sq].reshape(32, 256))
        m["ck"] = np.ascontiguousarray(ck[sq].reshape(4, 2048, 512))
        m["cv"] = np.ascontiguousarray(cv[sq].reshape(4, 2048, 512))
        m["sssm"] = np.ascontiguousarray(sssm[sq].reshape(4, 512, 128))
        m["sconv"] = np.ascontiguousarray(sconv[sq].reshape(12, 1024))
        m["sffn"] = np.ascontiguousarray(sffn[sq].reshape(8, 2 * DFF))
        in_maps.append(m)
    return in_maps


def kernel(**inputs):
    in_maps = _prep_inputs(inputs)
    nc = _get_nc()
    res = run_bass_kernel_spmd(nc, in_maps, core_ids=list(range(8)))
    R = res.results
    y_prompt = np.zeros((4, 4096, D), np.float32)
    k_prompt = np.zeros((1, 4, 2048, 8, 64), np.float32)
    v_prompt = np.zeros((1, 4, 2048, 8, 64), np.float32)
    ssm_prompt = np.zeros((1, 4, 8, 64, 128), np.float32)
    conv_prompt = np.zeros((1, 4, 3, 1024), np.float32)
    ffn_prompt = np.zeros((1, 4, 2, 2 * DFF), np.float32)
    for b in range(4):
        A, Bc = R[2 * b], R[2 * b + 1]
        y_prompt[b, 0:NMAIN + 2] = A["y_loc"][0:NMAIN + 2]
        y_prompt[b, NMAIN + 2:] = Bc["y_loc"][2:NMAIN]
        k_prompt[0, b] = Bc["k_loc"].reshape(2048, 8, 64)
        v_prompt[0, b] = Bc["v_loc"].reshape(2048, 8, 64)
        ssm_prompt[0, b] = Bc["ssm_loc"].reshape(8, 64, 128)
        conv_prompt[0, b] = Bc["conv_loc"]
        ffn_prompt[0, b] = Bc["ffn_loc"]
    y_sample = np.zeros((32, 8, D), np.float32)
    k_sample = np.zeros((1, 32, 8, 8, 64), np.float32)
    v_sample = np.zeros((1, 32, 8, 8, 64), np.float32)
    ssm_sample = np.zeros((1, 32, 8, 64, 128), np.float32)
    conv_sample = np.zeros((1, 32, 3, 1024), np.float32)
    ffn_sample = np.zeros((1, 32, 2, 2 * DFF), np.float32)
    for core in range(8):
        sq = slice(4 * core, 4 * core + 4)
        r = R[core]
        y_sample[sq] = r["ys"].reshape(4, 8, D)
        k_sample[0, sq] = r["ks"].reshape(4, 8, 8, 64)
        v_sample[0, sq] = r["vs"].reshape(4, 8, 8, 64)
        ssm_sample[0, sq] = r["ssm_s"].reshape(4, 8, 64, 128)
        conv_sample[0, sq] = r["conv_s"].reshape(4, 3, 1024)
        ffn_sample[0, sq] = r["ffn_s"].reshape(4, 2, 2 * DFF)
    return (y_prompt, y_sample, k_prompt, v_prompt, k_sample, v_sample, ssm_prompt, ssm_sample,
            conv_prompt, conv_sample, ffn_prompt, ffn_sample)
```
